# Optimizing a Trainium2 kernel written in Bass

```python
import jax, jax.numpy as jnp
from jax import lax
import numpy as np

D_MODEL = 1024
BATCH = 32
SEQ = 256
DEPTH = 4
DEC_BATCH = 8
DEC_SEQ = 4096
PAST_LEN = 512

GRID_W = 64
HEAD_DIM = 64
N_BRANCH = 4
BRANCH_W = 256
NA_HEADS = 4
NA_KR_MAX = 8
NA_KC = 16
ML_HEADS = 4
ML_CHUNK = 128
SW_HEADS = 4
SW_KV_HEADS = 2
SW_GROUP = SW_HEADS // SW_KV_HEADS
SW_WINDOW = 128
SW_BLOCK = 128
MLA_HEADS = 4
MLA_Q_RANK = 192
MLA_KV_RANK = 128
MLA_NOPE = 64
MLA_ROPE = 32
MLA_V = 64
MLA_SCALE = (MLA_NOPE + MLA_ROPE) ** -0.5
PEER_HEADS = 8
PEER_KEY_DIM = 64
PEER_N_KEYS = 128
PEER_TOPK = 16
PEER_N_EXPERTS = PEER_N_KEYS * PEER_N_KEYS
PEER_BLOCK = 128
Q_BLOCK = 128
ROPE_BASE = 10000.0
RMS_EPS = 1e-6
NEG_INF = -1e30
FORGET_BIAS = 3.0
IN_SPLITS = (NA_HEADS * HEAD_DIM, NA_HEADS * HEAD_DIM, NA_HEADS * HEAD_DIM,
             ML_HEADS * HEAD_DIM, ML_HEADS * HEAD_DIM, ML_HEADS * HEAD_DIM, ML_HEADS * HEAD_DIM,
             2 * ML_HEADS, 2 * ML_HEADS,
             SW_HEADS * HEAD_DIM, SW_KV_HEADS * HEAD_DIM, SW_KV_HEADS * HEAD_DIM,
             MLA_Q_RANK, MLA_KV_RANK, MLA_ROPE,
             N_BRANCH * D_MODEL)
IN_WIDTH = sum(IN_SPLITS)

kernel_name = 'hybrid_diffusion_trunk_step'


def rmsnorm(x, g):
    xf = x.astype(jnp.float32)
    y = xf * lax.rsqrt(jnp.mean(xf * xf, axis=-1, keepdims=True) + RMS_EPS)
    return (y * g.astype(jnp.float32)).astype(x.dtype)


def rms_unit(x):
    xf = x.astype(jnp.float32)
    return (xf * lax.rsqrt(jnp.mean(xf * xf, axis=-1, keepdims=True) + RMS_EPS)).astype(x.dtype)


def modulate(x, g, shift, scale):
    return rmsnorm(x, g) * (1 + scale) + shift


def adaln(cond, w_mod, b_mod):
    return jnp.split(jax.nn.silu(cond) @ w_mod + b_mod, 6, axis=-1)


def rope_1d(x, pos):
    half = x.shape[-1] // 2
    freqs = ROPE_BASE ** (-jnp.arange(half, dtype=jnp.float32) / half)
    ang = pos.astype(jnp.float32)[:, None] * freqs[None, :]
    ang = ang.reshape((ang.shape[0],) + (1,) * (x.ndim - 3) + (half,))
    cos, sin = jnp.cos(ang), jnp.sin(ang)
    x1 = x[..., :half].astype(jnp.float32)
    x2 = x[..., half:].astype(jnp.float32)
    return jnp.concatenate([x1 * cos - x2 * sin, x1 * sin + x2 * cos], axis=-1).astype(x.dtype)


def rope2d(x, rows, cols):
    d = x.shape[-1]
    return jnp.concatenate([rope_1d(x[..., :d // 2], rows), rope_1d(x[..., d // 2:], cols)], axis=-1)


def split_proj(p):
    idx = [int(v) for v in np.cumsum(IN_SPLITS)[:-1]]
    return jnp.split(p, idx, axis=-1)


def dense_attention(q, k, v, scale, sink=None):
    B, Tq, G, R, d = q.shape
    Tk = k.shape[1]
    nblk = Tq // Q_BLOCK
    qb = jnp.moveaxis(q.reshape(B, nblk, Q_BLOCK, G, R, d), 1, 0)

    def block(qi):
        s = jnp.einsum('bqgrd,bkgd->bgrqk', qi, k).astype(jnp.float32) * scale
        if sink is not None:
            s_sink = jnp.broadcast_to(sink.astype(jnp.float32)[None, :, :, None, None], s.shape[:-1] + (1,))
            s = jnp.concatenate([s, s_sink], axis=-1)
        pr = jax.nn.softmax(s, axis=-1)[..., :Tk].astype(v.dtype)
        return jnp.einsum('bgrqk,bkgd->bqgrd', pr, v)

    out = lax.map(block, qb)
    return jnp.moveaxis(out, 0, 1).reshape(B, Tq, -1)


def window_attention(q, k, v, k_ctx, v_ctx, sink):
    B, T, G, R, d = q.shape
    nb = T // SW_BLOCK
    scale = d ** -0.5
    pad = ((0, 0), (SW_BLOCK, SW_BLOCK), (0, 0), (0, 0))
    kp = jnp.pad(k, pad).reshape(B, nb + 2, SW_BLOCK, G, d)
    vp = jnp.pad(v, pad).reshape(B, nb + 2, SW_BLOCK, G, d)
    kb = jnp.concatenate([kp[:, :-2], kp[:, 1:-1], kp[:, 2:]], axis=2)
    vb = jnp.concatenate([vp[:, :-2], vp[:, 1:-1], vp[:, 2:]], axis=2)
    qb = q.reshape(B, nb, SW_BLOCK, G, R, d)
    s_loc = jnp.einsum('bnqgrd,bnkgd->bngrqk', qb, kb).astype(jnp.float32) * scale
    blk = jnp.arange(nb)
    qpos = blk[:, None] * SW_BLOCK + jnp.arange(SW_BLOCK)[None, :]
    kpos = (blk[:, None] - 1) * SW_BLOCK + jnp.arange(3 * SW_BLOCK)[None, :]
    valid = ((jnp.abs(qpos[:, :, None] - kpos[:, None, :]) <= SW_WINDOW)
             & (kpos[:, None, :] >= 0) & (kpos[:, None, :] < T))
    s_loc = jnp.where(valid[None, :, None, None], s_loc, NEG_INF)
    s_ctx = jnp.einsum('bnqgrd,blgd->bngrql', qb, k_ctx).astype(jnp.float32) * scale
    s_sink = jnp.broadcast_to(sink.astype(jnp.float32)[None, None, :, :, None, None], s_loc.shape[:-1] + (1,))
    pr = jax.nn.softmax(jnp.concatenate([s_loc, s_ctx, s_sink], axis=-1), axis=-1).astype(v.dtype)
    nk = 3 * SW_BLOCK
    lc = k_ctx.shape[1]
    out = (jnp.einsum('bngrqk,bnkgd->bnqgrd', pr[..., :nk], vb)
           + jnp.einsum('bngrql,blgd->bnqgrd', pr[..., nk:nk + lc], v_ctx))
    return out.reshape(B, T, G * R * d)


def neighbourhood_attention(q, k, v, k_ctx, v_ctx, rpb):
    B, T, H, d = q.shape
    rows_n = T // GRID_W
    kr = min(NA_KR_MAX, rows_n)
    scale = d ** -0.5
    r = jnp.arange(rows_n)
    row_idx = jnp.clip(r - kr // 2, 0, rows_n - kr)[:, None] + jnp.arange(kr)[None, :]
    c = jnp.arange(GRID_W)
    col_start = jnp.clip(c - NA_KC // 2, 0, GRID_W - NA_KC)
    col_ok = (c[None, :] >= col_start[:, None]) & (c[None, :] < col_start[:, None] + NA_KC)
    dr = row_idx - r[:, None] + (NA_KR_MAX - 1)
    dc = jnp.clip(c[None, :] - c[:, None] + (NA_KC - 1), 0, 2 * NA_KC - 2)
    bias = rpb.astype(jnp.float32)[:, dr[:, :, None, None], dc[None, None, :, :]]
    bias = jnp.where(col_ok[None, None, None], bias, NEG_INF).transpose(0, 1, 3, 2, 4)
    qg = q.reshape(B, rows_n, GRID_W, H, d)
    kg = k.reshape(B, rows_n, GRID_W, H, d)[:, row_idx]
    vg = v.reshape(B, rows_n, GRID_W, H, d)[:, row_idx]
    s_nb = jnp.einsum('brchd,brkwhd->bhrckw', qg, kg).astype(jnp.float32) * scale + bias[None]
    s_nb = s_nb.reshape(B, H, rows_n, GRID_W, kr * GRID_W)
    s_ctx = jnp.einsum('brchd,blhd->bhrcl', qg, k_ctx).astype(jnp.float32) * scale
    pr = jax.nn.softmax(jnp.concatenate([s_nb, s_ctx], axis=-1), axis=-1).astype(v.dtype)
    p_nb = pr[..., :kr * GRID_W].reshape(B, H, rows_n, GRID_W, kr, GRID_W)
    p_ctx = pr[..., kr * GRID_W:]
    out = (jnp.einsum('bhrckw,brkwhd->brchd', p_nb, vg)
           + jnp.einsum('bhrcl,blhd->brchd', p_ctx, v_ctx))
    return out.reshape(B, T, H * d)


def mlstm_scan(q, k, v, i_pre, f_pre, C0, n0, m0):
    B, T, H, d = q.shape
    nc = T // ML_CHUNK

    def chunks(a):
        a = a.astype(jnp.float32).reshape((B, nc, ML_CHUNK, H) + a.shape[3:])
        return jnp.moveaxis(a, (1, 3), (0, 2))

    qc, kc, vc = chunks(q), chunks(k) * (d ** -0.5), chunks(v)
    ic, fc = chunks(i_pre), chunks(f_pre)
    causal = jnp.tril(jnp.ones((ML_CHUNK, ML_CHUNK), dtype=bool))

    def step(carry, xs):
        C, n, m = carry
        qx, kx, vx, ix, fx = xs
        b = jnp.cumsum(jax.nn.log_sigmoid(fx), axis=-1)
        dmat = jnp.where(causal, b[..., :, None] - b[..., None, :] + ix[..., None, :], -jnp.inf)
        inter = b + m[..., None]
        m_t = jnp.maximum(inter, dmat.max(axis=-1))
        w_intra = jnp.exp(dmat - m_t[..., None])
        w_inter = jnp.exp(inter - m_t)
        s = jnp.einsum('bhtd,bhsd->bhts', qx, kx) * w_intra
        num = (jnp.einsum('bhts,bhsv->bhtv', s, vx)
               + w_inter[..., None] * jnp.einsum('bhtd,bhdv->bhtv', qx, C))
        den = s.sum(axis=-1) + w_inter * jnp.einsum('bhtd,bhd->bht', qx, n)
        h = num / jnp.maximum(jnp.abs(den), jnp.exp(-m_t))[..., None]
        b_last = b[..., -1]
        g = b_last[..., None] - b + ix
        m_new = jnp.maximum(b_last + m, g.max(axis=-1))
        w = jnp.exp(g - m_new[..., None])
        decay = jnp.exp(b_last + m - m_new)
        C_new = decay[..., None, None] * C + jnp.einsum('bhs,bhsd,bhsv->bhdv', w, kx, vx)
        n_new = decay[..., None] * n + jnp.einsum('bhs,bhsd->bhd', w, kx)
        return (C_new, n_new, m_new), h

    init = (C0.astype(jnp.float32), n0.astype(jnp.float32), m0.astype(jnp.float32))
    (C, n, m), h = lax.scan(step, init, (qc, kc, vc, ic, fc))
    h = jnp.moveaxis(h, (0, 2), (1, 3)).reshape(B, T, H, -1)
    return h, C, n, m


def mlstm_bidir(q, k, v, i_pre, f_pre, C0, n0, m0):
    hf, Cf, nf, mf = mlstm_scan(q, k, v, i_pre[:, :, 0], f_pre[:, :, 0], C0[:, 0], n0[:, 0], m0[:, 0])
    fl = lambda a: jnp.flip(a, axis=1)
    hb, Cb, nb, mb = mlstm_scan(fl(q), fl(k), fl(v), fl(i_pre[:, :, 1]), fl(f_pre[:, :, 1]),
                                C0[:, 1], n0[:, 1], m0[:, 1])
    h = hf + fl(hb)
    return h, jnp.stack([Cf, Cb], axis=1), jnp.stack([nf, nb], axis=1), jnp.stack([mf, mb], axis=1)


def mlstm_branch(ml_q, ml_k, ml_v, ml_o, ml_i, ml_f, C0, n0, m0):
    B, T, _ = ml_q.shape
    hd = lambda a: a.reshape(B, T, ML_HEADS, HEAD_DIM)
    h, C, n, m = mlstm_bidir(hd(ml_q), hd(ml_k), hd(ml_v), ml_i.reshape(B, T, 2, ML_HEADS),
                             ml_f.reshape(B, T, 2, ML_HEADS), C0, n0, m0)
    out = rms_unit(h) * jax.nn.sigmoid(hd(ml_o).astype(jnp.float32))
    return out.reshape(B, T, -1).astype(ml_q.dtype), C, n, m


def mla_kv(ckv_n, krope, w_uk, w_uv):
    B, L, _ = ckv_n.shape
    k_nope = (ckv_n @ w_uk).reshape(B, L, MLA_HEADS, MLA_NOPE)
    v = (ckv_n @ w_uv).reshape(B, L, MLA_HEADS, MLA_V)
    k = jnp.concatenate([k_nope, jnp.broadcast_to(krope[:, :, None, :], (B, L, MLA_HEADS, MLA_ROPE)).astype(k_nope.dtype)], axis=-1)
    return k, v


def merge_branches(outs, gate_pre, w_branch, w_out):
    acc = jax.nn.sigmoid(gate_pre[..., :D_MODEL]) * (outs[0] @ w_branch[0])
    for n in range(1, N_BRANCH):
        g = jax.nn.sigmoid(gate_pre[..., n * D_MODEL:(n + 1) * D_MODEL])
        acc = acc + g * (outs[n] @ w_branch[n])
    return acc @ w_out


def peer_ffn(h, wq, keys, u, v):
    B, T, D = h.shape
    nblk = (B * T) // PEER_BLOCK

    def block(xb):
        q = (xb @ wq).reshape(PEER_BLOCK, PEER_HEADS, 2, PEER_KEY_DIM)
        s = jnp.einsum('phxd,hxnd->phxn', q, keys).astype(jnp.float32)
        s1, i1 = lax.top_k(s[:, :, 0], PEER_TOPK)
        s2, i2 = lax.top_k(s[:, :, 1], PEER_TOPK)
        cand = (s1[..., :, None] + s2[..., None, :]).reshape(PEER_BLOCK, PEER_HEADS, PEER_TOPK * PEER_TOPK)
        cidx = (i1[..., :, None] * PEER_N_KEYS + i2[..., None, :]).reshape(PEER_BLOCK, PEER_HEADS, PEER_TOPK * PEER_TOPK)
        top, pos = lax.top_k(cand, PEER_TOPK)
        idx = jnp.take_along_axis(cidx, pos, axis=-1)
        g = jax.nn.softmax(top, axis=-1)
        ue, ve = u[idx], v[idx]
        a = jax.nn.gelu(jnp.einsum('pd,phkd->phk', xb, ue).astype(jnp.float32))
        return jnp.einsum('phk,phkd->pd', (g * a).astype(xb.dtype), ve)

    out = lax.map(block, h.reshape(nblk, PEER_BLOCK, D))
    return out.reshape(B, T, D)


def context_mixers(h, p):
    B, L, _ = h.shape
    (na_q, na_k, na_v, ml_q, ml_k, ml_v, ml_o, ml_i, ml_f,
     sw_q, sw_k, sw_v, cq, ckv, krope, gate_pre) = split_proj(h @ p['w_in'] + p['b_in'])
    heads = lambda a, n: a.reshape(B, L, n, HEAD_DIM)
    hd = HEAD_DIM
    na_k, na_v = heads(na_k, NA_HEADS), heads(na_v, NA_HEADS)
    o_na = dense_attention(na_q.reshape(B, L, NA_HEADS, 1, hd), na_k, na_v, hd ** -0.5)
    C0 = jnp.zeros((B, 2, ML_HEADS, hd, hd), jnp.float32)
    n0 = jnp.zeros((B, 2, ML_HEADS, hd), jnp.float32)
    m0 = jnp.zeros((B, 2, ML_HEADS), jnp.float32)
    o_ml, C, n, m = mlstm_branch(ml_q, ml_k, ml_v, ml_o, ml_i, ml_f, C0, n0, m0)
    sw_k, sw_v = heads(sw_k, SW_KV_HEADS), heads(sw_v, SW_KV_HEADS)
    o_sw = dense_attention(sw_q.reshape(B, L, SW_KV_HEADS, SW_GROUP, hd), sw_k, sw_v, hd ** -0.5,
                           sink=p['sw_sink'].reshape(SW_KV_HEADS, SW_GROUP))
    q_mla = (rmsnorm(cq, p['mla_q_norm']) @ p['w_uq']).reshape(B, L, MLA_HEADS, MLA_NOPE + MLA_ROPE)
    ckv_n = rmsnorm(ckv, p['mla_kv_norm'])
    k_mla, v_mla = mla_kv(ckv_n, krope, p['w_uk'], p['w_uv'])
    o_mla = dense_attention(q_mla[:, :, :, None, :], k_mla, v_mla, MLA_SCALE)
    out = merge_branches([o_na, o_ml, o_sw, o_mla], gate_pre, p['w_branch'], p['w_out'])
    return out, (na_k, na_v, C, n, m, sw_k, sw_v, ckv_n, krope)


def latent_mixers(h, p, cache):
    na_k_c, na_v_c, ml_C, ml_n, ml_m, sw_k_c, sw_v_c, ckv_c, krope_c = cache
    B, T, _ = h.shape
    pos = jnp.arange(T)
    rows, cols = pos // GRID_W, pos % GRID_W
    (na_q, na_k, na_v, ml_q, ml_k, ml_v, ml_o, ml_i, ml_f,
     sw_q, sw_k, sw_v, cq, ckv, krope, gate_pre) = split_proj(h @ p['w_in'] + p['b_in'])
    heads = lambda a, n: a.reshape(B, T, n, HEAD_DIM)
    o_na = neighbourhood_attention(heads(na_q, NA_HEADS), heads(na_k, NA_HEADS), heads(na_v, NA_HEADS),
                                   na_k_c, na_v_c, p['na_rpb'])
    o_ml, _, _, _ = mlstm_branch(ml_q, ml_k, ml_v, ml_o, ml_i, ml_f, ml_C, ml_n, ml_m)
    q_sw = rope2d(sw_q.reshape(B, T, SW_KV_HEADS, SW_GROUP, HEAD_DIM), rows, cols)
    k_sw = rope2d(heads(sw_k, SW_KV_HEADS), rows, cols)
    o_sw = window_attention(q_sw, k_sw, heads(sw_v, SW_KV_HEADS), sw_k_c, sw_v_c,
                            p['sw_sink'].reshape(SW_KV_HEADS, SW_GROUP))
    q_mla = (rmsnorm(cq, p['mla_q_norm']) @ p['w_uq']).reshape(B, T, MLA_HEADS, MLA_NOPE + MLA_ROPE)
    q_mla = jnp.concatenate([q_mla[..., :MLA_NOPE], rope2d(q_mla[..., MLA_NOPE:], rows, cols)], axis=-1)
    k_lat, v_lat = mla_kv(rmsnorm(ckv, p['mla_kv_norm']), rope2d(krope, rows, cols), p['w_uk'], p['w_uv'])
    k_ctx, v_ctx = mla_kv(ckv_c, krope_c, p['w_uk'], p['w_uv'])
    o_mla = dense_attention(q_mla[:, :, :, None, :], jnp.concatenate([k_lat, k_ctx.astype(k_lat.dtype)], axis=1),
                            jnp.concatenate([v_lat, v_ctx.astype(v_lat.dtype)], axis=1), MLA_SCALE)
    return merge_branches([o_na, o_ml, o_sw, o_mla], gate_pre, p['w_branch'], p['w_out'])


def setup_inputs(seed: int = 0) -> dict:
    key = jax.random.key(seed)
    ks = iter(jax.random.split(key, 64))
    nrm = lambda shape, s: s * jax.random.normal(next(ks), shape, jnp.float32)
    hd = HEAD_DIM
    f_off = int(np.cumsum(IN_SPLITS)[7])
    b_in = nrm((DEPTH, IN_WIDTH), 0.02).at[:, f_off:f_off + 2 * ML_HEADS].add(FORGET_BIAS)
    return {
        'x_prompt': nrm((BATCH, SEQ, D_MODEL), 1.0),
        'x_sample': nrm((DEC_BATCH, DEC_SEQ, D_MODEL), 1.0),
        'cache_na_k': nrm((DEC_BATCH, DEPTH, PAST_LEN, NA_HEADS, hd), 1.0),
        'cache_na_v': nrm((DEC_BATCH, DEPTH, PAST_LEN, NA_HEADS, hd), 1.0),
        'state_mlstm_C': nrm((DEC_BATCH, DEPTH, 2, ML_HEADS, hd, hd), 0.1),
        'state_mlstm_n': nrm((DEC_BATCH, DEPTH, 2, ML_HEADS, hd), 0.3),
        'state_mlstm_m': nrm((DEC_BATCH, DEPTH, 2, ML_HEADS), 1.0),
        'cache_swa_k': nrm((DEC_BATCH, DEPTH, PAST_LEN, SW_KV_HEADS, hd), 1.0),
        'cache_swa_v': nrm((DEC_BATCH, DEPTH, PAST_LEN, SW_KV_HEADS, hd), 1.0),
        'cache_mla_ckv': nrm((DEC_BATCH, DEPTH, PAST_LEN, MLA_KV_RANK), 1.0),
        'cache_mla_krope': nrm((DEC_BATCH, DEPTH, PAST_LEN, MLA_ROPE), 1.0),
        'c': nrm((DEC_BATCH, D_MODEL), 1.0),
        'c_ctx': nrm((D_MODEL,), 1.0),
        'w_mod': nrm((DEPTH, D_MODEL, 6 * D_MODEL), 0.5 * D_MODEL ** -0.5),
        'b_mod': nrm((DEPTH, 6 * D_MODEL), 0.02),
        'norm1_g': 1.0 + nrm((DEPTH, D_MODEL), 0.02),
        'norm2_g': 1.0 + nrm((DEPTH, D_MODEL), 0.02),
        'w_in': nrm((DEPTH, D_MODEL, IN_WIDTH), D_MODEL ** -0.5),
        'b_in': b_in,
        'na_rpb': nrm((DEPTH, NA_HEADS, 2 * NA_KR_MAX - 1, 2 * NA_KC - 1), 0.1),
        'sw_sink': nrm((DEPTH, SW_HEADS), 0.5),
        'mla_q_norm': 1.0 + nrm((DEPTH, MLA_Q_RANK), 0.02),
        'w_uq': nrm((DEPTH, MLA_Q_RANK, MLA_HEADS * (MLA_NOPE + MLA_ROPE)), MLA_Q_RANK ** -0.5),
        'mla_kv_norm': 1.0 + nrm((DEPTH, MLA_KV_RANK), 0.02),
        'w_uk': nrm((DEPTH, MLA_KV_RANK, MLA_HEADS * MLA_NOPE), MLA_KV_RANK ** -0.5),
        'w_uv': nrm((DEPTH, MLA_KV_RANK, MLA_HEADS * MLA_V), MLA_KV_RANK ** -0.5),
        'w_branch': nrm((DEPTH, N_BRANCH, BRANCH_W, D_MODEL), BRANCH_W ** -0.5),
        'w_out': nrm((DEPTH, D_MODEL, D_MODEL), D_MODEL ** -0.5),
        'peer_wq': nrm((DEPTH, D_MODEL, PEER_HEADS * 2 * PEER_KEY_DIM), D_MODEL ** -0.5),
        'peer_keys': nrm((DEPTH, PEER_HEADS, 2, PEER_N_KEYS, PEER_KEY_DIM), PEER_KEY_DIM ** -0.5),
        'peer_u': nrm((DEPTH, PEER_N_EXPERTS, D_MODEL), D_MODEL ** -0.5),
        'peer_v': nrm((DEPTH, PEER_N_EXPERTS, D_MODEL), 0.1),
        'final_norm_g': 1.0 + nrm((D_MODEL,), 0.02),
    }


def reference(x_prompt, x_sample, cache_na_k, cache_na_v, state_mlstm_C, state_mlstm_n, state_mlstm_m,
              cache_swa_k, cache_swa_v, cache_mla_ckv, cache_mla_krope, c, c_ctx,
              w_mod, b_mod, norm1_g, norm2_g, w_in, b_in, na_rpb, sw_sink, mla_q_norm, w_uq,
              mla_kv_norm, w_uk, w_uv, w_branch, w_out, peer_wq, peer_keys, peer_u, peer_v, final_norm_g):
    xp, xs = x_prompt, x_sample
    per_layer = tuple([] for _ in range(9))
    for l in range(DEPTH):
        p = {'w_in': w_in[l], 'b_in': b_in[l], 'na_rpb': na_rpb[l], 'sw_sink': sw_sink[l],
             'mla_q_norm': mla_q_norm[l], 'w_uq': w_uq[l], 'mla_kv_norm': mla_kv_norm[l],
             'w_uk': w_uk[l], 'w_uv': w_uv[l], 'w_branch': w_branch[l], 'w_out': w_out[l]}
        sh1, sc1, g1, sh2, sc2, g2 = adaln(c_ctx[None, None, :], w_mod[l], b_mod[l])
        mix, ctx_tensors = context_mixers(modulate(xp, norm1_g[l], sh1, sc1), p)
        xp = xp + g1 * mix
        xp = xp + g2 * peer_ffn(modulate(xp, norm2_g[l], sh2, sc2), peer_wq[l], peer_keys[l], peer_u[l], peer_v[l])
        for i, t in enumerate(ctx_tensors):
            per_layer[i].append(t)
        sh1, sc1, g1, sh2, sc2, g2 = adaln(c[:, None, :], w_mod[l], b_mod[l])
        cache_l = (cache_na_k[:, l], cache_na_v[:, l], state_mlstm_C[:, l], state_mlstm_n[:, l], state_mlstm_m[:, l],
                   cache_swa_k[:, l], cache_swa_v[:, l], cache_mla_ckv[:, l], cache_mla_krope[:, l])
        xs = xs + g1 * latent_mixers(modulate(xs, norm1_g[l], sh1, sc1), p, cache_l)
        xs = xs + g2 * peer_ffn(modulate(xs, norm2_g[l], sh2, sc2), peer_wq[l], peer_keys[l], peer_u[l], peer_v[l])
    y_prompt = rmsnorm(xp, final_norm_g)
    y_sample = rmsnorm(xs, final_norm_g)
    (new_na_k, new_na_v, new_mlstm_C, new_mlstm_n, new_mlstm_m,
     new_swa_k, new_swa_v, new_mla_ckv, new_mla_krope) = [jnp.stack(s, axis=1) for s in per_layer]
    return (y_prompt, y_sample, new_na_k, new_na_v, new_mlstm_C, new_mlstm_n, new_mlstm_m,
            new_swa_k, new_swa_v, new_mla_ckv, new_mla_krope)
```

```python
import numpy as np
import ml_dtypes
from contextlib import ExitStack
import concourse.bass as bass
import concourse.mybir as mybir
from concourse.bass_utils import run_bass_kernel_spmd

F32 = mybir.dt.float32
F32R = mybir.dt.float32r
BF16 = mybir.dt.bfloat16
I32 = mybir.dt.int32
U32 = mybir.dt.uint32
AF = mybir.ActivationFunctionType
ALU = mybir.AluOpType
AX = mybir.AxisListType

D = 1024
DEPTH = 4
NCORES = 8
PSEQ = 256
NPSEQ = 4
TP = NPSEQ * PSEQ
TS = 4096
PAST = 512
INW = 6768
NG = 2672
EPS = 1e-6

O_NAQ, O_NAK, O_NAV = 0, 256, 512
O_MLQ, O_MLK, O_MLV, O_MLO, O_MLI, O_MLF = 768, 1024, 1280, 1536, 1792, 1800
O_SWQ, O_SWK, O_SWV = 1808, 2064, 2192
O_CQ, O_CKV, O_KR = 2320, 2512, 2640
O_GATE = 2672


class Dummy:
    def __getitem__(self, k):
        return self

    def __getattr__(self, n):
        return self

    def __call__(self, *a, **k):
        return self


class Ten:
    def __init__(self, t, multi=False):
        self.t = t
        self.multi = multi
        self.w = {}
        self.r = {}

    def __getitem__(self, k):
        return self.t[k]

    def ap(self):
        return self.t[:]

    def reset(self):
        self.w = {}
        self.r = {}


class Builder:
    ENG = ['pe', 'act', 'dve', 'pool', 'sp']

    def __init__(self, nc, ndma=40):
        self.nc = nc
        self.h = {'pe': nc.tensor, 'act': nc.scalar, 'dve': nc.vector, 'pool': nc.gpsimd, 'sp': nc.sync}
        self.ndma = ndma
        self.dram = []
        self.need = set()
        self.uid = 0
        self.es = ExitStack()
        self.sem = {e: self.es.enter_context(nc.semaphore("s_" + e)) for e in self.ENG}
        self.dsem = [self.es.enter_context(nc.semaphore("d_%d" % i)) for i in range(ndma + 24)]

    def begin(self, dry):
        self.dry = dry
        self.opi = 0
        self.seq = {e: 0 for e in self.ENG}
        self.sig = {e: 0 for e in self.ENG}
        self.seen = {e: {} for e in self.ENG}
        self.last = {}
        self.last_pb = 0
        self.dval = [0] * (self.ndma + 24)
        self.drr = 0
        self.nwait = 0
        for t in self.dram:
            t.reset()

    def run(self, fn):
        self.begin(True)
        fn(self)
        nops = self.opi
        self.begin(False)
        fn(self)
        assert self.opi == nops, (self.opi, nops)

    def dr(self, name, shape, dtype, kind="Internal"):
        t = Ten(self.nc.dram_tensor(name, list(shape), dtype, kind=kind), multi=True)
        self.dram.append(t)
        return t

    def sb(self, es, name, shape, dtype):
        if self.dry:
            return Ten(Dummy())
        self.uid += 1
        return Ten(es.enter_context(self.nc.sbuf_tensor("sb%d_%s" % (self.uid, name), list(shape), dtype)))

    def ps(self, es, name, shape, dtype):
        if self.dry:
            return Ten(Dummy())
        self.uid += 1
        return Ten(es.enter_context(self.nc.psum_tensor("ps%d_%s" % (self.uid, name), list(shape), dtype)))

    def _wait(self, eng, key, ev):
        if ev[0] == 'c':
            if self.seen[eng].get(key, 0) >= ev[2]:
                return
            self.seen[eng][key] = ev[2]
            if self.dry:
                self.need.add(ev[3])
            else:
                assert ev[4] is not None
                self.h[eng].wait_ge(self.sem[ev[1]], ev[4])
                self.nwait += 1
        else:
            if self.seen[eng].get(key, 0) >= ev[2]:
                return
            self.seen[eng][key] = ev[2]
            if not self.dry:
                self.h[eng].wait_ge(self.dsem[ev[1]], ev[2])
                self.nwait += 1

    def op(self, eng, reads, writes, fn, dma=False, slot=None):
        deps = []
        for b in reads:
            for k, ev in b.w.items():
                deps.append((k, ev, 'raw'))
        for b in writes:
            if not b.multi:
                for k, ev in b.w.items():
                    deps.append((k, ev, 'waw'))
            for k, ev in b.r.items():
                deps.append((k, ev, 'war'))
        for k, ev, kind in deps:
            if (not dma) and ev[0] == 'c' and ev[1] == eng and kind != 'raw':
                continue
            self._wait(eng, k, ev)
        opidx = self.opi
        self.opi += 1
        if dma:
            if slot is not None:
                idx = self.ndma + slot
            else:
                idx = self.drr
                self.drr = (self.drr + 1) % self.ndma
                if self.dval[idx] > 0:
                    self._wait(eng, ('d', idx), ('d', idx, self.dval[idx]))
            self.dval[idx] += 16
            ev = ('d', idx, self.dval[idx])
            key = ('d', idx)
            if not self.dry:
                fn().then_inc(self.dsem[idx], 16)
        else:
            self.seq[eng] += 1
            sigval = None
            if not self.dry:
                inst = fn()
                if opidx in self.need:
                    self.sig[eng] += 1
                    sigval = self.sig[eng]
                    inst.then_inc(self.sem[eng], 1)
            ev = ('c', eng, self.seq[eng], opidx, sigval)
            key = eng
            self.last[eng] = ev
        for b in reads:
            b.r[key] = ev
        for b in writes:
            if b.multi:
                b.w[key] = ev
            else:
                b.w = {key: ev}
                b.r = {}
        return ev

    def barrier(self):
        for f in self.ENG:
            for e in self.ENG:
                if e != f and e in self.last:
                    self._wait(f, e, self.last[e])
            for idx in range(self.ndma + 24):
                if self.dval[idx] > 0:
                    self._wait(f, ('d', idx), ('d', idx, self.dval[idx]))

    def finish(self):
        for idx in range(self.ndma + 24):
            if self.dval[idx] > 0:
                self._wait('sp', ('d', idx), ('d', idx, self.dval[idx]))

    def dma(self, out, in_, reads, writes, q='sp', **kw):
        return self.op(q, reads, writes, lambda: self.h[q].dma_start(out=out, in_=in_, **kw), dma=True)

    def mm(self, out, lhsT, rhs, start, stop, reads, writes, pbase=0):
        if getattr(self, 'last_pb', 0) != pbase and 'pe' in self.last:
            self._wait('pe', 'pe_ser', self.last['pe'])
        self.last_pb = pbase
        return self.op('pe', reads, writes,
                       lambda: self.nc.tensor.matmul(out, lhsT=lhsT, rhs=rhs, start=start, stop=stop))

    def tr(self, out, in_, ident, reads, writes):
        return self.op('pe', reads, writes, lambda: self.nc.tensor.transpose(out, in_, ident))

    def act(self, out, in_, func, reads, writes, **kw):
        return self.op('act', reads, writes, lambda: self.nc.scalar.activation(out=out, in_=in_, func=func, **kw))

    def v(self, eng, name, reads, writes, *a, **kw):
        return self.op(eng, reads, writes, lambda: getattr(self.h[eng], name)(*a, **kw))


def bcast_rows(ap, n):
    return ap.to_broadcast([n] + list(ap.shape[1:]))


def make_consts():
    c = {}
    return make_consts_B(_make_consts_A(c))


def _make_consts_A(c):
    c['ident_bf'] = np.eye(128, dtype=np.float32).astype(ml_dtypes.bfloat16)
    c['ident_f'] = np.eye(128, dtype=np.float32)
    p = np.arange(128)[:, None]
    f = np.arange(128)[None, :]
    c['tri_le'] = (p <= f).astype(np.float32)
    c['tri_ge'] = (p >= f).astype(np.float32)
    pos = np.arange(TS)
    rows, cols = pos // 64, pos % 64
    def table(half):
        fr = 10000.0 ** (-np.arange(half, dtype=np.float32) / half)
        ar = rows[:, None].astype(np.float32) * fr[None, :]
        ac = cols[:, None].astype(np.float32) * fr[None, :]
        cs = np.concatenate([np.cos(ar), np.cos(ac)], axis=1)
        sn = np.concatenate([np.sin(ar), np.sin(ac)], axis=1)
        return np.concatenate([cs, sn], axis=1).astype(np.float32)
    c['rope16'] = table(16)
    c['rope8'] = table(8)
    return c


CONST_SPECS = {
    'ident_bf': ([128, 128], BF16), 'ident_f': ([128, 128], F32),
    'tri_le': ([128, 128], F32), 'tri_ge': ([128, 128], F32),
    'rope16': ([TS, 64], F32), 'rope8': ([TS, 32], F32),
}

IN_SPECS = {
    'xp': ([TP, D], F32), 'xs': ([TS, D], F32),
    'cnak': ([DEPTH, PAST, 256], F32), 'cnav': ([DEPTH, PAST, 256], F32),
    'cswk': ([DEPTH, PAST, 128], F32), 'cswv': ([DEPTH, PAST, 128], F32),
    'cckv': ([DEPTH, PAST, 128], F32), 'ckr': ([DEPTH, PAST, 32], F32),
    'stC': ([DEPTH, 8, 64, 64], F32), 'stn': ([DEPTH, 8, 64], F32), 'stm': ([DEPTH, 8], F32),
    'cvec': ([128, 16], F32),
    'w_mod': ([DEPTH, D, 6 * D], F32), 'b_mod': ([DEPTH, 6 * D], F32),
    'norm1_g': ([DEPTH, D], F32), 'norm2_g': ([DEPTH, D], F32),
    'w_in': ([DEPTH, D, INW], F32), 'b_in': ([DEPTH, INW], F32),
    'na_rpb': ([DEPTH, 60, 31], F32), 'sw_sink': ([DEPTH, 4], F32),
    'mla_q_norm': ([DEPTH, 192], F32), 'w_uq': ([DEPTH, 192, 384], F32),
    'mla_kv_norm': ([DEPTH, 128], F32), 'w_uk': ([DEPTH, 128, 256], F32), 'w_uv': ([DEPTH, 128, 256], F32),
    'w_branch': ([DEPTH, 1024, D], F32), 'w_out': ([DEPTH, D, D], F32),
    'peer_wq': ([DEPTH, D, D], F32), 'peer_keys': ([DEPTH, 16, 128, 64], F32),
    'peer_u': ([DEPTH * 16384, D], F32), 'peer_v': ([DEPTH * 16384, D], F32),
    'final_norm_g': ([1, D], F32),
}

OUT_SPECS = {
    'y_p': ([TP, D], F32), 'y_s': ([TS, D], F32),
    'o_nak': ([NPSEQ, DEPTH, PSEQ, 256], F32), 'o_nav': ([NPSEQ, DEPTH, PSEQ, 256], F32),
    'o_C': ([NPSEQ, DEPTH, 8, 64, 64], F32), 'o_n': ([NPSEQ, DEPTH, 8, 64], F32), 'o_m': ([NPSEQ, DEPTH, 8], F32),
    'o_swk': ([NPSEQ, DEPTH, PSEQ, 128], F32), 'o_swv': ([NPSEQ, DEPTH, PSEQ, 128], F32),
    'o_ckv': ([NPSEQ, DEPTH, PSEQ, 128], F32), 'o_kr': ([NPSEQ, DEPTH, PSEQ, 32], F32),
}

def scratch_specs():
    sp = {}
    for s, T in (('p', TP), ('s', TS)):
        sp['xres_' + s] = ([T, D], F32)
        sp['xmid_' + s] = ([T, D], F32)
        sp['featT_' + s] = ([12, 128, T], BF16)
        sp['m64T_' + s] = ([8, 64, T], BF16)
        sp['m32T_' + s] = ([5, 32, T], BF16)
        sp['vaug_' + s] = ([T, 14 * 65], BF16)
        sp['mlx_' + s] = ([T, 528], F32)
        sp['oT_' + s] = ([8, 128, T], BF16)
    sp['modrow'] = ([DEPTH, 2, 6 * D], F32)
    sp['pad2'] = ([60, 128], F32)
    return sp


class Prog:
    def __init__(self, nc, depth=DEPTH, phases=('mod', 'A', 'B', 'C'), dbg=(), skip=()):
        self.skip = skip
        self.nc = nc
        self.depth = depth
        self.phases = phases
        self.K = Builder(nc)
        K = self.K
        self.I = {n: K.dr(n, s, d, kind="ExternalInput") for n, (s, d) in IN_SPECS.items()}
        self.C = {n: K.dr(n, s, d, kind="ExternalInput") for n, (s, d) in CONST_SPECS.items()}
        self.O = {n: K.dr(n, s, d, kind="ExternalOutput") for n, (s, d) in OUT_SPECS.items()}
        self.S = {}
        for n, (s, d) in scratch_specs().items():
            self.S[n] = K.dr(n, s, d, kind=("ExternalOutput" if n in dbg else "Internal"))
        self.T = {'p': TP, 's': TS}

    def build(self):
        self.K.run(self.main)

    def main(self, K):
        with ExitStack() as es:
            self.ident_bf = K.sb(es, "ident_bf", [128, 128], BF16)
            self.ident_f = K.sb(es, "ident_f", [128, 128], F32)
            self.ones_f = K.sb(es, "ones_f", [128, 128], F32)
            K.dma(self.ident_bf[:], self.C['ident_bf'][:, :], [self.C['ident_bf']], [self.ident_bf])
            K.dma(self.ident_f[:], self.C['ident_f'][:, :], [self.C['ident_f']], [self.ident_f])
            K.v('dve', 'memset', [], [self.ones_f], self.ones_f[:], 1.0)
            for l in range(self.depth):
                if 'mod' in self.phases:
                    self.phase_mod(l)
                    K.barrier()
                if 'A' in self.phases:
                    self.phase_A(l)
                    K.barrier()
                if 'B' in self.phases:
                    self.phase_B(l)
                    K.barrier()
                if 'C' in self.phases:
                    self.phase_C(l)
                    K.barrier()
            K.barrier()
            K.finish()

    def phase_mod(self, l):
        K, I = self.K, self.I
        with ExitStack() as es:
            cv = K.sb(es, "cv", [128, 16], F32)
            sm = K.sb(es, "sm", [128, 16], F32)
            bm = K.sb(es, "bm", [1, 6 * D], F32)
            mrow = K.sb(es, "mrow", [2, 6 * D], F32)
            wch = [K.sb(es, "wch%d" % i, [128, 8, 512], F32) for i in range(2)]
            pm = [K.ps(es, "pm%d" % i, [128, 512], F32) for i in range(2)]
            K.dma(cv[:], I['cvec'][:, :], [I['cvec']], [cv])
            K.dma(bm[:], I['b_mod'][l:l + 1, :], [I['b_mod']], [bm])
            K.act(sm[:], cv[:], AF.Silu, [cv], [sm])
            for j in range(12):
                w = wch[j % 2]
                p = pm[j % 2]
                K.dma(w[:], I['w_mod'][l, :, j * 512:(j + 1) * 512].rearrange("(k p) n -> p k n", p=128),
                      [I['w_mod']], [w])
                for k in range(8):
                    K.mm(p[0:2, :], sm[:, k:16:8], w[:, k, :], k == 0, False, [sm, w], [p])
                K.mm(p[0:2, :], self.ones_f[0:1, 0:2], bm[0:1, j * 512:(j + 1) * 512], False, True,
                     [self.ones_f, bm], [p])
                K.act(mrow[:, j * 512:(j + 1) * 512], p[0:2, :], AF.Copy, [p], [mrow])
            K.dma(self.S['modrow'][l, :, :], mrow[:], [mrow], [self.S['modrow']])

    def load_mod(self, es, l, s, which, name):
        K, I = self.K, self.I
        si = 0 if s == 'p' else 1
        out = {}
        for wi in which:
            t = K.sb(es, "%s_%s_%d" % (name, s, wi), [128, D], F32)
            src = self.S['modrow'][l, si:si + 1, wi * D:(wi + 1) * D]
            K.dma(t[:], bcast_rows(src, 128), [self.S['modrow']], [t])
            if wi in (1, 4):
                gsrc = I['norm1_g' if wi == 1 else 'norm2_g']
                g = K.sb(es, "%s_g_%s_%d" % (name, s, wi), [128, D], F32)
                K.dma(g[:], bcast_rows(gsrc[l:l + 1, :], 128), [gsrc], [g])
                K.v('dve', 'scalar_tensor_tensor', [t, g], [t], out=t[:], in0=t[:], scalar=1.0, in1=g[:],
                    op0=ALU.add, op1=ALU.mult)
            out[wi] = t
        return out

    def rstd(self, ss, c, n):
        K = self.K
        K.v('dve', 'tensor_scalar', [ss], [ss], out=ss[:, c:c + 1], in0=ss[:, c:c + 1], scalar1=1.0 / n, scalar2=EPS,
            op0=ALU.mult, op1=ALU.add)
        K.act(ss[:, c:c + 1], ss[:, c:c + 1], AF.Sqrt, [ss], [ss])
        K.v('dve', 'reciprocal', [ss], [ss], out=ss[:, c:c + 1], in_=ss[:, c:c + 1])

    def norm_mod(self, x, A, B, h32, hbf, ss, junk):
        K = self.K
        K.act(junk[:], x[:], AF.Square, [x], [junk, ss], accum_out=ss[:, 0:1])
        self.rstd(ss, 0, D)
        K.v('dve', 'scalar_tensor_tensor', [x, ss, A], [h32], out=h32[:], in0=x[:], scalar=ss[:, 0:1], in1=A[:],
            op0=ALU.mult, op1=ALU.mult)
        if hbf is not None:
            K.v('dve', 'tensor_tensor', [h32, B], [hbf], out=hbf[:], in0=h32[:], in1=B[:], op=ALU.add)
        else:
            K.v('dve', 'tensor_tensor', [h32, B], [h32], out=h32[:], in0=h32[:], in1=B[:], op=ALU.add)

    def load_w_bf(self, dst, src_ap, src_t, stage, rows, cols, kch):
        K = self.K
        K.dma(stage[0:rows, 0:kch, 0:cols], src_ap, [src_t], [stage])
        K.v('pool', 'tensor_copy', [stage], [dst], out=dst, in_=stage[0:rows, 0:kch, 0:cols])

    def phase_A(self, l):
        K, I, S, O, C = self.K, self.I, self.S, self.O, self.C
        with ExitStack() as es:
            winb = K.sb(es, "winb", [128, 8, NG], BF16)
            stage = K.sb(es, "stageA", [128, 8, 512], F32)
            binb = K.sb(es, "binb", [128, NG], F32)
            wuq = K.sb(es, "wuq", [128, 2, 384], BF16)
            wuk = K.sb(es, "wuk", [128, 256], BF16)
            wuv = K.sb(es, "wuv", [128, 256], BF16)
            qng = K.sb(es, "qng", [128, 192], F32)
            kvng = K.sb(es, "kvng", [128, 128], F32)
            x = K.sb(es, "xA", [128, D], F32)
            h32 = K.sb(es, "h32A", [128, D], F32)
            hbf = K.sb(es, "hbfA", [128, D], BF16)
            hT = K.sb(es, "hTA", [128, 8, 128], BF16)
            ss = K.sb(es, "ssA", [128, 4], F32)
            prow = K.sb(es, "prow", [128, NG], F32)
            pb = K.sb(es, "pbA", [128, 12, 128], BF16)
            fst = K.sb(es, "fstA", [128, 12, 128], BF16)
            vaug = K.sb(es, "vaugA", [128, 14, 65], BF16)
            mlx = K.sb(es, "mlxA", [128, 528], F32)
            rt16 = K.sb(es, "rt16", [128, 64], F32)
            rt8 = K.sb(es, "rt8", [128, 32], F32)
            tmp = [K.sb(es, "ropet%d" % i, [128, 192], F32) for i in range(4)]
            qlat = K.sb(es, "qlat", [128, 192], BF16)
            qlT = K.sb(es, "qlT", [128, 2, 128], BF16)
            qm = K.sb(es, "qm", [128, 4, 96], F32)
            qnr = K.sb(es, "qnr", [128, 384], BF16)
            ckn = K.sb(es, "ckn", [128, 128], F32)
            cknb = K.sb(es, "cknb", [128, 128], BF16)
            ckT = K.sb(es, "ckT", [128, 128], BF16)
            krb = K.sb(es, "krb", [128, 32], BF16)
            m64 = K.sb(es, "m64", [64, 8, 128], BF16)
            m32 = K.sb(es, "m32", [32, 5, 128], BF16)
            pT = K.ps(es, "pT", [128, 8, 128], BF16)
            pP = [K.ps(es, "pP%d" % i, [128, 512], F32) for i in range(2)]
            pF = [K.ps(es, "pF%d" % i, [128, 8, 128], BF16) for i in range(2)]
            pM = K.ps(es, "pM", [128, 8, 128], BF16)
            pQ = K.ps(es, "pQ", [128, 512], F32)
            pK = K.ps(es, "pK", [128, 512], F32)

            for j in range(6):
                c0 = j * 512
                w = min(512, NG - c0)
                K.dma(stage[:, :, 0:w], I['w_in'][l, :, c0:c0 + w].rearrange("(k p) n -> p k n", p=128),
                      [I['w_in']], [stage])
                K.v('pool', 'tensor_copy', [stage], [winb], out=winb[:, :, c0:c0 + w], in_=stage[:, :, 0:w])
            K.dma(binb[:], bcast_rows(I['b_in'][l:l + 1, 0:NG], 128), [I['b_in']], [binb])
            K.dma(stage[:, 0, 0:384], I['w_uq'][l, 0:128, :], [I['w_uq']], [stage])
            K.dma(stage[0:64, 1, 0:384], I['w_uq'][l, 128:192, :], [I['w_uq']], [stage])
            K.v('pool', 'tensor_copy', [stage], [wuq], out=wuq[:, 0, :], in_=stage[:, 0, 0:384])
            K.v('pool', 'tensor_copy', [stage], [wuq], out=wuq[0:64, 1, :], in_=stage[0:64, 1, 0:384])
            K.dma(stage[:, 2, 0:256], I['w_uk'][l, :, :], [I['w_uk']], [stage])
            K.dma(stage[:, 3, 0:256], I['w_uv'][l, :, :], [I['w_uv']], [stage])
            K.v('pool', 'tensor_copy', [stage], [wuk], out=wuk[:], in_=stage[:, 2, 0:256])
            K.v('pool', 'tensor_copy', [stage], [wuv], out=wuv[:], in_=stage[:, 3, 0:256])
            K.dma(qng[:], bcast_rows(I['mla_q_norm'][l:l + 1, :], 128), [I['mla_q_norm']], [qng])
            K.dma(kvng[:], bcast_rows(I['mla_kv_norm'][l:l + 1, :], 128), [I['mla_kv_norm']], [kvng])
            K.v('dve', 'memset', [], [vaug], vaug[:], 1.0)
            mods = {s: self.load_mod(es, l, s, [0, 1], "mA") for s in ('p', 's')}

            for s in ('p', 's'):
                T = self.T[s]
                A1, B1 = mods[s][1], mods[s][0]
                xsrc = (I['xp'] if s == 'p' else I['xs']) if l == 0 else S['xres_' + s]
                for t in range(T // 128):
                    t0 = t * 128
                    K.dma(x[:], xsrc[t0:t0 + 128, :], [xsrc], [x])
                    if s == 's':
                        K.dma(rt16[:], C['rope16'][t0:t0 + 128, :], [C['rope16']], [rt16])
                        K.dma(rt8[:], C['rope8'][t0:t0 + 128, :], [C['rope8']], [rt8])
                    self.norm_mod(x, A1, B1, h32, hbf, ss, h32)
                    for k in range(8):
                        K.tr(pT[:, k, :], hbf[:, k * 128:(k + 1) * 128], self.ident_bf[:], [hbf, self.ident_bf], [pT])
                    K.act(hT[:], pT[:], AF.Copy, [pT], [hT])
                    for j in range(6):
                        c0 = j * 512
                        w = min(512, NG - c0)
                        p = pP[j % 2]
                        for k in range(8):
                            K.mm(p[:, 0:w], hT[:, k, :], winb[:, k, c0:c0 + w], k == 0, k == 7, [hT, winb], [p])
                        K.v('dve', 'tensor_tensor', [p, binb], [prow], out=prow[:, c0:c0 + w], in0=p[:, 0:w],
                            in1=binb[:, c0:c0 + w], op=ALU.add)
                    if s == 'p':
                        sq, pos = t // 2, (t % 2) * 128
                        for nm, c0, w in (('o_nak', O_NAK, 256), ('o_nav', O_NAV, 256), ('o_swk', O_SWK, 128),
                                          ('o_swv', O_SWV, 128), ('o_kr', O_KR, 32)):
                            K.dma(O[nm][sq, l, pos:pos + 128, :], prow[:, c0:c0 + w], [prow], [O[nm]])
                    if s == 's':
                        X = prow[:, O_SWQ:O_SWQ + 384].rearrange("p (h a b c) -> p h a b c", h=6, a=2, b=2)
                        xa, xb = X[:, :, :, 0, :], X[:, :, :, 1, :]
                        cs = rt16[:, 0:32].rearrange("p (a c) -> p a c", a=2).unsqueeze(1).to_broadcast([128, 6, 2, 16])
                        sn = rt16[:, 32:64].rearrange("p (a c) -> p a c", a=2).unsqueeze(1).to_broadcast([128, 6, 2, 16])
                        tv = [tt[:, 0:192].rearrange("p (h a c) -> p h a c", h=6, a=2) for tt in tmp]
                        self.rope(prow, xa, xb, cs, sn, tv, tmp, [rt16])
                    for g, c0, nh in ((0, O_NAV, 4), (4, O_MLV, 4), (8, O_SWV, 2)):
                        K.v('pool', 'tensor_copy', [prow], [vaug], out=vaug[:, g:g + nh, 0:64],
                            in_=prow[:, c0:c0 + nh * 64].rearrange("p (h d) -> p h d", h=nh))
                    K.act(mlx[:, 0:256], prow[:, O_MLK:O_MLK + 256], AF.Copy, [prow], [mlx], scale=0.125)
                    K.act(mlx[:, 256:528], prow[:, O_MLO:O_MLO + 272], AF.Copy, [prow], [mlx])
                    K.dma(S['mlx_' + s][t0:t0 + 128, :], mlx[:], [mlx], [S['mlx_' + s]])
                    K.act(pb[:, 0:4, :], prow[:, 0:512].rearrange("p (g c) -> p g c", g=4), AF.Copy, [prow], [pb])
                    K.act(pb[:, 4:6, :], prow[:, O_MLQ:O_MLQ + 256].rearrange("p (g c) -> p g c", g=2), AF.Copy,
                          [prow], [pb])
                    K.act(pb[:, 6:8, :], mlx[:, 0:256].rearrange("p (g c) -> p g c", g=2), AF.Copy, [mlx], [pb])
                    K.act(pb[:, 8:11, :], prow[:, O_SWQ:O_SWQ + 384].rearrange("p (g c) -> p g c", g=3), AF.Copy,
                          [prow], [pb])
                    K.act(pb[:, 11, 0:64], prow[:, O_SWK + 64:O_SWK + 128], AF.Copy, [prow], [pb])
                    K.act(pb[:, 11, 64:128], prow[:, O_SWK:O_SWK + 64], AF.Copy, [prow], [pb])
                    for g in range(12):
                        pf = pF[0] if g < 8 else pF[1]
                        K.tr(pf[:, g % 8, :], pb[:, g, :], self.ident_bf[:], [pb, self.ident_bf], [pf])
                    K.v('dve', 'tensor_copy', [pF[0]], [fst], out=fst[:, 0:8, :], in_=pF[0][:])
                    K.v('dve', 'tensor_copy', [pF[1]], [fst], out=fst[:, 8:12, :], in_=pF[1][:, 0:4, :])
                    K.dma(S['featT_' + s][:, :, t0:t0 + 128].rearrange("g p t -> p g t"), fst[:], [fst],
                          [S['featT_' + s]])
                    K.act(tmp[0][:, 0:192], prow[:, O_CQ:O_CQ + 192], AF.Square, [prow], [tmp[0], ss],
                          accum_out=ss[:, 1:2])
                    self.rstd(ss, 1, 192)
                    K.v('dve', 'scalar_tensor_tensor', [prow, ss, qng], [qlat], out=qlat[:],
                        in0=prow[:, O_CQ:O_CQ + 192], scalar=ss[:, 1:2], in1=qng[:], op0=ALU.mult, op1=ALU.mult)
                    K.tr(pM[:, 0, :], qlat[:, 0:128], self.ident_bf[:], [qlat, self.ident_bf], [pM])
                    K.tr(pM[0:64, 1, :], qlat[:, 128:192], self.ident_bf[:], [qlat, self.ident_bf], [pM])
                    K.act(tmp[0][:, 0:128], prow[:, O_CKV:O_CKV + 128], AF.Square, [prow], [tmp[0], ss],
                          accum_out=ss[:, 2:3])
                    self.rstd(ss, 2, 128)
                    K.v('dve', 'scalar_tensor_tensor', [prow, ss, kvng], [ckn], out=ckn[:],
                        in0=prow[:, O_CKV:O_CKV + 128], scalar=ss[:, 2:3], in1=kvng[:], op0=ALU.mult, op1=ALU.mult)
                    if s == 'p':
                        K.dma(O['o_ckv'][sq, l, pos:pos + 128, :], ckn[:], [ckn], [O['o_ckv']])
                    K.act(cknb[:], ckn[:], AF.Copy, [ckn], [cknb])
                    K.tr(pM[:, 2, :], cknb[:], self.ident_bf[:], [cknb, self.ident_bf], [pM])
                    K.v('dve', 'tensor_copy', [pM], [qlT], out=qlT[:, 0, :], in_=pM[:, 0, :])
                    K.v('dve', 'tensor_copy', [pM], [qlT], out=qlT[0:64, 1, :], in_=pM[0:64, 1, :])
                    K.v('dve', 'tensor_copy', [pM], [ckT], out=ckT[:], in_=pM[:, 2, :])
                    K.mm(pQ[:, 0:384], qlT[:, 0, :], wuq[:, 0, :], True, False, [qlT, wuq], [pQ])
                    K.mm(pQ[:, 0:384], qlT[0:64, 1, :], wuq[0:64, 1, :], False, True, [qlT, wuq], [pQ])
                    K.act(qm[:], pQ[:, 0:384].rearrange("p (h c) -> p h c", h=4), AF.Copy, [pQ], [qm])
                    if s == 's':
                        X = qm[:, :, 64:96].rearrange("p h (a b c) -> p h a b c", a=2, b=2)
                        xa, xb = X[:, :, :, 0, :], X[:, :, :, 1, :]
                        cs = rt8[:, 0:16].rearrange("p (a c) -> p a c", a=2).unsqueeze(1).to_broadcast([128, 4, 2, 8])
                        sn = rt8[:, 16:32].rearrange("p (a c) -> p a c", a=2).unsqueeze(1).to_broadcast([128, 4, 2, 8])
                        tv = [tt[:, 0:64].rearrange("p (h a c) -> p h a c", h=4, a=2) for tt in tmp]
                        self.rope(qm, xa, xb, cs, sn, tv, tmp, [rt8])
                        X = prow[:, O_KR:O_KR + 32].rearrange("p (h a b c) -> p h a b c", h=1, a=2, b=2)
                        xa, xb = X[:, :, :, 0, :], X[:, :, :, 1, :]
                        cs1 = rt8[:, 0:16].rearrange("p (a c) -> p a c", a=2).unsqueeze(1)
                        sn1 = rt8[:, 16:32].rearrange("p (a c) -> p a c", a=2).unsqueeze(1)
                        tv = [tt[:, 0:16].rearrange("p (h a c) -> p h a c", h=1, a=2) for tt in tmp]
                        self.rope(prow, xa, xb, cs1, sn1, tv, tmp, [rt8])
                    K.act(qnr[:, 0:256].rearrange("p (h c) -> p h c", h=4), qm[:, :, 0:64], AF.Copy, [qm], [qnr])
                    K.act(qnr[:, 256:384].rearrange("p (h c) -> p h c", h=4), qm[:, :, 64:96], AF.Copy, [qm], [qnr])
                    K.act(krb[:], prow[:, O_KR:O_KR + 32], AF.Copy, [prow], [krb])
                    for hh in range(4):
                        K.tr(pF[0][0:64, hh, :], qnr[:, hh * 64:(hh + 1) * 64], self.ident_bf[:],
                             [qnr, self.ident_bf], [pF[0]])
                        K.tr(pF[1][0:32, hh, :], qnr[:, 256 + hh * 32:256 + (hh + 1) * 32], self.ident_bf[:],
                             [qnr, self.ident_bf], [pF[1]])
                    K.tr(pF[1][0:32, 4, :], krb[:], self.ident_bf[:], [krb, self.ident_bf], [pF[1]])
                    K.v('dve', 'tensor_copy', [pF[0]], [m64], out=m64[:, 0:4, :], in_=pF[0][0:64, 0:4, :])
                    K.v('dve', 'tensor_copy', [pF[1]], [m32], out=m32[:], in_=pF[1][0:32, 0:5, :])
                    for hh in range(4):
                        K.mm(pK[0:64, hh * 128:(hh + 1) * 128], wuk[:, hh * 64:(hh + 1) * 64], ckT[:], True, True,
                             [wuk, ckT], [pK])
                    K.act(m64[:, 4:8, :], pK[0:64, :].rearrange("p (h t) -> p h t", h=4), AF.Copy, [pK], [m64])
                    K.mm(pQ[:, 0:256], ckT[:], wuv[:], True, True, [ckT, wuv], [pQ])
                    K.v('dve', 'tensor_copy', [pQ], [vaug], out=vaug[:, 10:14, 0:64],
                        in_=pQ[:, 0:256].rearrange("p (h d) -> p h d", h=4))
                    K.dma(S['m64T_' + s][:, :, t0:t0 + 128].rearrange("g p t -> p g t"), m64[:], [m64],
                          [S['m64T_' + s]])
                    K.dma(S['m32T_' + s][:, :, t0:t0 + 128].rearrange("g p t -> p g t"), m32[:], [m32],
                          [S['m32T_' + s]])
                    K.dma(S['vaug_' + s][t0:t0 + 128, :], vaug[:].rearrange("p g c -> p (g c)"), [vaug],
                          [S['vaug_' + s]])

    def rope(self, X, xa, xb, cs, sn, tv, tmp, tabs):
        K = self.K
        K.v('dve', 'tensor_tensor', [X] + tabs, [tmp[0]], out=tv[0], in0=xa, in1=cs, op=ALU.mult)
        K.v('dve', 'tensor_tensor', [X] + tabs, [tmp[1]], out=tv[1], in0=xb, in1=sn, op=ALU.mult)
        K.v('dve', 'tensor_tensor', [X] + tabs, [tmp[2]], out=tv[2], in0=xa, in1=sn, op=ALU.mult)
        K.v('dve', 'tensor_tensor', [X] + tabs, [tmp[3]], out=tv[3], in0=xb, in1=cs, op=ALU.mult)
        K.v('dve', 'tensor_tensor', [tmp[0], tmp[1]], [X], out=xa, in0=tv[0], in1=tv[1], op=ALU.subtract)
        K.v('dve', 'tensor_tensor', [tmp[2], tmp[3]], [X], out=xb, in0=tv[2], in1=tv[3], op=ALU.add)


def prep_core_inputs(inp, core, consts):
    b = core
    f = np.ascontiguousarray
    m = {}
    m['xp'] = f(inp['x_prompt'][4 * b:4 * b + 4].reshape(TP, D))
    m['xs'] = f(inp['x_sample'][b])
    m['cnak'] = f(inp['cache_na_k'][b].reshape(DEPTH, PAST, 256))
    m['cnav'] = f(inp['cache_na_v'][b].reshape(DEPTH, PAST, 256))
    m['cswk'] = f(inp['cache_swa_k'][b].reshape(DEPTH, PAST, 128))
    m['cswv'] = f(inp['cache_swa_v'][b].reshape(DEPTH, PAST, 128))
    m['cckv'] = f(inp['cache_mla_ckv'][b])
    m['ckr'] = f(inp['cache_mla_krope'][b])
    m['stC'] = f(inp['state_mlstm_C'][b].reshape(DEPTH, 8, 64, 64))
    m['stn'] = f(inp['state_mlstm_n'][b].reshape(DEPTH, 8, 64))
    m['stm'] = f(inp['state_mlstm_m'][b].reshape(DEPTH, 8))
    cv = np.concatenate([inp['c_ctx'].reshape(8, 128).T, inp['c'][b].reshape(8, 128).T], axis=1)
    m['cvec'] = f(cv.astype(np.float32))
    for n in ('w_mod', 'b_mod', 'norm1_g', 'norm2_g', 'w_in', 'b_in', 'sw_sink', 'mla_q_norm', 'w_uq',
              'mla_kv_norm', 'w_uk', 'w_uv', 'w_out', 'peer_wq'):
        m[n] = inp[n]
    m['peer_u'] = inp['peer_u'].reshape(DEPTH * 16384, D)
    m['peer_v'] = inp['peer_v'].reshape(DEPTH * 16384, D)
    m['na_rpb'] = inp['na_rpb'].reshape(DEPTH, 60, 31)
    m['w_branch'] = inp['w_branch'].reshape(DEPTH, 1024, D)
    m['peer_keys'] = inp['peer_keys'].reshape(DEPTH, 16, 128, 64)
    m['final_norm_g'] = inp['final_norm_g'].reshape(1, D)
    m.update(consts)
    return m


def make_consts_B(c):
    kcp = np.arange(64)[:, None]
    qc = np.arange(64)[None, :]
    kc = 63 - kcp
    cs = np.clip(qc - 8, 0, 48)
    ok = ((kc >= cs) & (kc < cs + 16)).astype(np.float32)
    c['na_ok8'] = (ok * 8.0).astype(np.float32)
    c['na_neg'] = ((ok - 1.0) * 240000.0).astype(np.float32)
    jlo = np.zeros((64, 128), np.float32)
    jhi = np.zeros((64, 128), np.float32)
    for k in range(64):
        jlo[k, 63 - k] = 1.0
        jhi[k, 127 - k] = 1.0
    c['jlo'] = jlo.astype(ml_dtypes.bfloat16)
    c['jhi'] = jhi.astype(ml_dtypes.bfloat16)
    mge = (c['tri_ge'] - 1.0) * 240000.0
    mle = (c['tri_le'] - 1.0) * 240000.0
    c['mb_ge2'] = np.concatenate([mge, mge], axis=1).astype(ml_dtypes.bfloat16)
    c['mb_le2'] = np.concatenate([mle, mle], axis=1).astype(ml_dtypes.bfloat16)
    return c


CONST_SPECS.update({
    'na_ok8': ([64, 64], F32), 'na_neg': ([64, 64], F32), 'jlo': ([64, 128], BF16), 'jhi': ([64, 128], BF16),
    'mb_ge2': ([128, 256], BF16), 'mb_le2': ([128, 256], BF16),
})


def _load_tok(self, dst, dst_ap_fn, src, col0, col1, tok0, nchunks, step=8):
    K = self.K
    for c0 in range(0, nchunks, step):
        n = min(step, nchunks - c0)
        K.dma(dst_ap_fn(c0, n), src[tok0 + c0 * 128:tok0 + (c0 + n) * 128, col0:col1].rearrange("(c p) f -> p c f", p=128),
              [src], [dst])


def _pipeline(steps, s_fn, rest_fn):
    n = len(steps)
    if n == 0:
        return
    s_fn(steps[0], 0)
    for i in range(n):
        if i + 1 < n:
            s_fn(steps[i + 1], (i + 1) % 2)
        rest_fn(steps[i], i % 2)


def _phase_B(self, l):
    K = self.K
    for nm in ('mix_na', 'mix_sw', 'mix_mla', 'mix_ml'):
        if '_' + nm in self.skip:
            continue
        getattr(self, nm)(l)
        K.barrier()


def _finish_heads(self, acc, nh, rec, ob, extra_den=None, npart=128):
    K = self.K
    rv = rec[0:npart, 0:nh]
    if extra_den is not None:
        K.v('dve', 'tensor_tensor', [acc[0], extra_den], [rec], out=rv, in0=acc[1][:, :, 64], in1=extra_den[0:npart, 0:nh],
            op=ALU.add)
        K.v('dve', 'reciprocal', [rec], [rec], out=rv, in_=rv)
    else:
        K.v('dve', 'reciprocal', [acc[0]], [rec], out=rv, in_=acc[1][:, :, 64])
    K.v('dve', 'tensor_tensor', [acc[0], rec], [ob[0]], out=ob[1], in0=acc[1][:, :, 0:64],
        in1=rv.unsqueeze(2).to_broadcast([npart, nh, 64]), op=ALU.mult)


def _mix_na(self, l):
    K, I, S, C = self.K, self.I, self.S, self.C
    sc = 0.125
    with ExitStack() as es:
        qk = K.sb(es, "naqk", [128, 4, PSEQ], BF16)
        va = K.sb(es, "nava", [128, 2, 260], BF16)
        E = [K.sb(es, "naE%d" % i, [128, 256], BF16) for i in range(2)]
        rec = K.sb(es, "narec", [128, 4], F32)
        ob = K.sb(es, "naob", [128, 256], BF16)
        ot = K.sb(es, "naot", [128, 2, 128], BF16)
        pS = [K.ps(es, "napS%d" % i, [128, 512], F32) for i in range(2)]
        pA = [K.ps(es, "napA%d" % i, [128, 512], F32) for i in range(2)]
        pO = K.ps(es, "napO", [128, 8, 128], BF16)
        for sq in range(NPSEQ):
            b0 = sq * PSEQ
            K.dma(qk[:], S['featT_p'][0:4, :, b0:b0 + PSEQ].rearrange("g p t -> p g t"), [S['featT_p']], [qk])
            K.dma(va[:], S['vaug_p'][b0:b0 + PSEQ, 0:260].rearrange("(c p) f -> p c f", p=128), [S['vaug_p']], [va])
            step = 0
            for h in range(4):
                g, hb = h // 2, (h % 2) * 64
                for c in range(2):
                    p = pS[step % 2]
                    e = E[step % 2]
                    step += 1
                    K.mm(p[:, 0:256], qk[hb:hb + 64, 2 + g, c * 128:(c + 1) * 128], qk[hb:hb + 64, g, :], True, True,
                         [qk], [p], pbase=hb)
                    K.act(e[:], p[:, 0:256], AF.Exp, [p], [e], scale=sc)
                    for j in range(2):
                        K.mm(pA[j][:, h * 65:(h + 1) * 65], e[:, j * 128:(j + 1) * 128], va[:, c, h * 65:(h + 1) * 65],
                             c == 0, c == 1, [e, va], [pA[j]])
            for j in range(2):
                accv = pA[j][:, 0:260].rearrange("p (h c) -> p h c", h=4)
                self.finish_heads((pA[j], accv), 4, rec, (ob, ob[:].rearrange("p (h c) -> p h c", h=4)))
                for g in range(2):
                    K.tr(pO[:, g, :], ob[:, g * 128:(g + 1) * 128], self.ident_bf[:], [ob, self.ident_bf], [pO])
                K.v('dve', 'tensor_copy', [pO], [ot], out=ot[:], in_=pO[:, 0:2, :])
                t0 = b0 + j * 128
                K.dma(S['oT_p'][0:2, :, t0:t0 + 128].rearrange("g p t -> p g t"), ot[:], [ot], [S['oT_p']])
    K.barrier()
    if 'na_s' in self.skip:
        return
    with ExitStack() as es:
        qk = K.sb(es, "nsqk", [128, 4, TS], BF16)
        VA = K.sb(es, "nsVA", [128, 32, 260], BF16)
        VB = K.sb(es, "nsVB", [128, 31, 260], BF16)
        ctok = K.sb(es, "nsctok", [128, 4, 256], F32)
        ctb = K.sb(es, "nsctb", [128, 4, 256], BF16)
        ckT = K.sb(es, "nsckT", [128, 2, 512], BF16)
        cva = K.sb(es, "nscva", [128, 4, 260], BF16)
        rp = K.sb(es, "nsrp", [60, 31], F32)
        rrev = K.sb(es, "nsrrev", [60, 128], F32)
        Tpp = K.sb(es, "nsTpp", [64, 60, 64], F32)
        ok8 = K.sb(es, "nsok8", [64, 64], F32)
        neg = K.sb(es, "nsneg", [64, 64], F32)
        BLK = K.sb(es, "nsBLK", [64, 15, 4, 64], BF16)
        jlo = K.sb(es, "nsjlo", [64, 128], BF16)
        jhi = K.sb(es, "nsjhi", [64, 128], BF16)
        E = [K.sb(es, "nsE%d" % i, [128, 256], BF16) for i in range(2)]
        rec = K.sb(es, "nsrec", [128, 4], F32)
        ob = K.sb(es, "nsob", [64, 256], BF16)
        ot = K.sb(es, "nsot", [128, 2, 128], BF16)
        pS = [K.ps(es, "nspS%d" % i, [128, 512], F32) for i in range(2)]
        pA = [K.ps(es, "nspA%d" % i, [128, 512], F32) for i in range(2)]
        pO = K.ps(es, "nspO", [128, 8, 128], BF16)
        pT = K.ps(es, "nspT", [128, 8, 128], BF16)
        K.dma(qk[:], S['featT_s'][0:4, :, :].rearrange("g p t -> p g t"), [S['featT_s']], [qk])
        self.load_tok(VA, lambda c0, n: VA[:, c0:c0 + n, :], S['vaug_s'], 0, 260, 0, 32)
        self.load_tok(VB, lambda c0, n: VB[:, c0:c0 + n, :], S['vaug_s'], 0, 260, 64, 31)
        K.dma(ctok[:], I['cnak'][l].rearrange("(c p) f -> p c f", p=128), [I['cnak']], [ctok])
        K.act(ctb[:], ctok[:], AF.Copy, [ctok], [ctb])
        for c in range(4):
            for g in range(2):
                K.tr(pT[:, c * 2 + g, :], ctb[:, c, g * 128:(g + 1) * 128], self.ident_bf[:], [ctb, self.ident_bf], [pT])
        K.v('dve', 'tensor_copy', [pT], [ckT], out=ckT[:].rearrange("p g (c t) -> p c g t", c=4),
            in_=pT[:].rearrange("p (c g) t -> p c g t", c=4))
        K.v('dve', 'memset', [], [cva], cva[:], 1.0)
        K.dma(ctok[:], I['cnav'][l].rearrange("(c p) f -> p c f", p=128), [I['cnav']], [ctok])
        K.v('dve', 'tensor_copy', [ctok], [cva], out=cva[:].rearrange("p c (h e) -> p c h e", h=4)[:, :, :, 0:64],
            in_=ctok[:].rearrange("p c (h e) -> p c h e", h=4))
        if 'na_s1' in self.skip:
            return
        K.dma(rp[:], I['na_rpb'][l], [I['na_rpb']], [rp])
        K.v('dve', 'memset', [], [rrev], rrev[:], 0.0)
        K.v('dve', 'tensor_copy', [rp], [rrev], out=rrev[:, 48:79], in_=rp[:, ::-1])
        K.dma(S['pad2'][:, :], rrev[:], [rrev], [S['pad2']])
        src = bass.AP(S['pad2'][:, :].tensor, 0, [[1, 64], [128, 60], [1, 64]])
        K.dma(Tpp[:], src, [S['pad2']], [Tpp])
        K.dma(ok8[:], C['na_ok8'][:, :], [C['na_ok8']], [ok8])
        K.dma(neg[:], C['na_neg'][:, :], [C['na_neg']], [neg])
        K.dma(jlo[:], C['jlo'][:, :], [C['jlo']], [jlo])
        K.dma(jhi[:], C['jhi'][:, :], [C['jhi']], [jhi])
        K.v('dve', 'tensor_tensor', [Tpp, ok8], [Tpp], out=Tpp[:], in0=Tpp[:],
            in1=ok8[:].unsqueeze(1).to_broadcast([64, 60, 64]), op=ALU.mult)
        K.v('dve', 'tensor_tensor', [Tpp, neg], [BLK], out=BLK[:].rearrange("p r h q -> p h r q"),
            in0=Tpp[:].rearrange("p (h r) q -> p h r q", h=4),
            in1=neg[:].unsqueeze(1).unsqueeze(1).to_broadcast([64, 4, 15, 64]), op=ALU.add)
        if 'na_s2' in self.skip:
            return
        steps = []
        for r in range(64):
            rs = min(max(r - 4, 0), 56)
            o = rs - r
            chunks = []
            for jj in range(4):
                k0 = rs * 64 + jj * 128
                if k0 % 128 == 0:
                    vv = VA[:, k0 // 128, :]
                else:
                    vv = VB[:, (k0 - 64) // 128, :]
                chunks.append(('loc', k0, vv, (2 * jj + o + 7, 2 * jj + 1 + o + 7)))
            for c in range(4):
                chunks.append(('ctx', c, cva[:, c, :], None))
            for ci, ch in enumerate(chunks):
                steps.append((r, ci, len(chunks)) + ch)

        def s_fn(st, b):
            r, ci, nch, kind, k0, vv, drs = st
            p = pS[b]
            for h in (0, 2, 1, 3):
                g, hb = h // 2, (h % 2) * 64
                if kind == 'loc':
                    kk = qk[hb:hb + 64, 2 + g, k0:k0 + 128]
                    rd = [qk]
                else:
                    kk = ckT[hb:hb + 64, g, k0 * 128:(k0 + 1) * 128]
                    rd = [qk, ckT]
                K.mm(p[:, h * 64:(h + 1) * 64], kk, qk[hb:hb + 64, g, r * 64:(r + 1) * 64], h == 0,
                     (kind == 'ctx'), rd, [p], pbase=hb)
            if kind == 'loc':
                K.mm(p[:, 0:256], jlo[:], BLK[:, drs[0], :, :].rearrange("p h q -> p (h q)"), False, False,
                     [jlo, BLK], [p])
                K.mm(p[:, 0:256], jhi[:], BLK[:, drs[1], :, :].rearrange("p h q -> p (h q)"), False, True,
                     [jhi, BLK], [p])

        def rest_fn(st, b):
            r, ci, nch, kind, k0, vv, drs = st
            p, e = pS[b], E[b]
            acc = pA[r % 2]
            K.act(e[:], p[:, 0:256], AF.Exp, [p], [e], scale=sc)
            rdv = [e, VA, VB, cva]
            for h in range(4):
                K.mm(acc[0:64, h * 65:(h + 1) * 65], e[:, h * 64:(h + 1) * 64], vv[:, h * 65:(h + 1) * 65],
                     ci == 0 and h == 0, ci == nch - 1, rdv, [acc])
            if ci == nch - 1:
                accv = acc[0:64, 0:260].rearrange("p (h c) -> p h c", h=4)
                self.finish_heads((acc, accv), 4, rec, (ob, ob[:].rearrange("p (h c) -> p h c", h=4)), npart=64)
                for g in range(2):
                    K.tr(pO[:, g, (r % 2) * 64:(r % 2) * 64 + 64], ob[:, g * 128:(g + 1) * 128], self.ident_bf[0:64, 0:64],
                         [ob, self.ident_bf], [pO])
                if r % 2 == 1:
                    K.v('dve', 'tensor_copy', [pO], [ot], out=ot[:], in_=pO[:, 0:2, :])
                    t0 = (r // 2) * 128
                    K.dma(S['oT_s'][0:2, :, t0:t0 + 128].rearrange("g p t -> p g t"), ot[:], [ot], [S['oT_s']])

        _pipeline(steps, s_fn, rest_fn)


Prog.phase_B = _phase_B
Prog.load_tok = _load_tok
Prog.finish_heads = _finish_heads
Prog.mix_na = _mix_na


def _stub(self, l):
    pass


for _n in ('mix_sw', 'mix_mla', 'mix_ml'):
    if not hasattr(Prog, _n):
        setattr(Prog, _n, _stub)


def _mix_sw(self, l):
    K, I, S, C = self.K, self.I, self.S, self.C
    sc = 0.125
    with ExitStack() as es:
        sk = K.sb(es, "swsk", [128, 4], F32)
        K.dma(sk[:], bcast_rows(I['sw_sink'][l:l + 1, :], 128), [I['sw_sink']], [sk])
        K.act(sk[:], sk[:], AF.Exp, [sk], [sk])
        rec = K.sb(es, "swrec", [128, 4], F32)
        ob = K.sb(es, "swob", [128, 256], BF16)
        ot = K.sb(es, "swot", [128, 2, 128], BF16)
        pS = [K.ps(es, "swpS%d" % i, [128, 512], F32) for i in range(2)]
        pA = [K.ps(es, "swpA%d" % i, [128, 512], F32) for i in range(2)]
        pO = K.ps(es, "swpO", [128, 8, 128], BF16)
        pT = K.ps(es, "swpT", [128, 8, 128], BF16)

        def emit_out(acc, s, t0):
            accv = acc[:, 0:260].rearrange("p (h c) -> p h c", h=4)
            self.finish_heads((acc, accv), 4, rec, (ob, ob[:].rearrange("p (h c) -> p h c", h=4)), extra_den=sk)
            for g in range(2):
                K.tr(pO[:, g, :], ob[:, g * 128:(g + 1) * 128], self.ident_bf[:], [ob, self.ident_bf], [pO])
            K.v('dve', 'tensor_copy', [pO], [ot], out=ot[:], in_=pO[:, 0:2, :])
            K.dma(S['oT_' + s][4:6, :, t0:t0 + 128].rearrange("g p t -> p g t"), ot[:], [ot], [S['oT_' + s]])

        with ExitStack() as es2:
            qk = K.sb(es2, "swqk", [128, 4, PSEQ], BF16)
            va = K.sb(es2, "swva", [128, 2, 130], BF16)
            E = [K.sb(es2, "swE%d" % i, [128, 512], BF16) for i in range(2)]
            step = 0
            for sq in range(NPSEQ):
                b0 = sq * PSEQ
                K.dma(qk[:], S['featT_p'][8:12, :, b0:b0 + PSEQ].rearrange("g p t -> p g t"), [S['featT_p']], [qk])
                K.dma(va[:], S['vaug_p'][b0:b0 + PSEQ, 520:650].rearrange("(c p) f -> p c f", p=128), [S['vaug_p']], [va])
                for g in range(2):
                    for c in range(2):
                        p = pS[step % 2]
                        e = E[step % 2]
                        step += 1
                        for r in range(2):
                            kg = 2 if g == r else 3
                            K.mm(p[:, r * 256:(r + 1) * 256], qk[r * 64:r * 64 + 64, kg, c * 128:(c + 1) * 128],
                                 qk[r * 64:r * 64 + 64, g, :], r == 0, True, [qk], [p], pbase=r * 64)
                        K.act(e[:], p[:], AF.Exp, [p], [e], scale=sc)
                        for r in range(2):
                            for j in range(2):
                                hh = g * 2 + r
                                K.mm(pA[j][:, hh * 65:(hh + 1) * 65], e[:, r * 256 + j * 128:r * 256 + (j + 1) * 128],
                                     va[:, c, g * 65:(g + 1) * 65], (g == 0 and c == 0 and r == 0), c == 1, [e, va], [pA[j]])
                for j in range(2):
                    emit_out(pA[j], 'p', b0 + j * 128)
        K.barrier()
        with ExitStack() as es2:
            qk = K.sb(es2, "swsqk", [128, 4, TS], BF16)
            VA = K.sb(es2, "swsVA", [128, 32, 130], BF16)
            ctok = K.sb(es2, "swctok", [128, 4, 128], F32)
            ctb = K.sb(es2, "swctb", [128, 4, 2, 128], BF16)
            ck = K.sb(es2, "swck", [128, 2, 512], BF16)
            cva = K.sb(es2, "swcva", [128, 4, 130], BF16)
            mge = K.sb(es2, "swmge", [128, 256], BF16)
            mle = K.sb(es2, "swmle", [128, 256], BF16)
            E = [K.sb(es2, "swsE%d" % i, [128, 256], BF16) for i in range(2)]
            K.dma(qk[:], S['featT_s'][8:12, :, :].rearrange("g p t -> p g t"), [S['featT_s']], [qk])
            self.load_tok(VA, lambda c0, n: VA[:, c0:c0 + n, :], S['vaug_s'], 520, 650, 0, 32)
            K.dma(mge[:], C['mb_ge2'][:, :], [C['mb_ge2']], [mge])
            K.dma(mle[:], C['mb_le2'][:, :], [C['mb_le2']], [mle])
            K.dma(ctok[:], I['cswk'][l].rearrange("(c p) f -> p c f", p=128), [I['cswk']], [ctok])
            K.act(ctb[:, :, 0, :], ctok[:], AF.Copy, [ctok], [ctb])
            K.act(ctb[:, :, 1, 0:64], ctok[:, :, 64:128], AF.Copy, [ctok], [ctb])
            K.act(ctb[:, :, 1, 64:128], ctok[:, :, 0:64], AF.Copy, [ctok], [ctb])
            for c in range(4):
                for v in range(2):
                    K.tr(pT[:, c * 2 + v, :], ctb[:, c, v, :], self.ident_bf[:], [ctb, self.ident_bf], [pT])
            K.v('dve', 'tensor_copy', [pT], [ck], out=ck[:].rearrange("p v (c t) -> p c v t", c=4),
                in_=pT[:].rearrange("p (c v) t -> p c v t", c=4))
            K.v('dve', 'memset', [], [cva], cva[:], 1.0)
            K.dma(ctok[:], I['cswv'][l].rearrange("(c p) f -> p c f", p=128), [I['cswv']], [ctok])
            K.v('dve', 'tensor_copy', [ctok], [cva], out=cva[:].rearrange("p c (h e) -> p c h e", h=2)[:, :, :, 0:64],
                in_=ctok[:].rearrange("p c (h e) -> p c h e", h=2))
            steps = []
            NB = TS // 128
            for n in range(NB):
                for g in range(2):
                    chunks = []
                    if n > 0:
                        chunks.append(('loc', n - 1, mge))
                    chunks.append(('loc', n, None))
                    if n < NB - 1:
                        chunks.append(('loc', n + 1, mle))
                    for c in range(4):
                        chunks.append(('ctx', c, None))
                    for ci, ch in enumerate(chunks):
                        steps.append((n, g, ci, len(chunks)) + ch)

            def s_fn(st, b):
                n, g, ci, nch, kind, c, mask = st
                p = pS[b]
                for r in range(2):
                    v = 0 if g == r else 1
                    if kind == 'loc':
                        kk = qk[r * 64:r * 64 + 64, 2 + v, c * 128:(c + 1) * 128]
                    else:
                        kk = ck[r * 64:r * 64 + 64, v, c * 128:(c + 1) * 128]
                    K.mm(p[:, r * 128:(r + 1) * 128], kk, qk[r * 64:r * 64 + 64, g, n * 128:(n + 1) * 128],
                         r == 0, mask is None, [qk, ck], [p], pbase=r * 64)
                if mask is not None:
                    K.mm(p[:, 0:256], self.ident_bf[:], mask[:], False, True, [self.ident_bf, mask], [p])

            def rest_fn(st, b):
                n, g, ci, nch, kind, c, mask = st
                p, e = pS[b], E[b]
                acc = pA[n % 2]
                K.act(e[:], p[:, 0:256], AF.Exp, [p], [e], scale=sc)
                vv = VA[:, c, :] if kind == 'loc' else cva[:, c, :]
                for r in range(2):
                    hh = g * 2 + r
                    K.mm(acc[:, hh * 65:(hh + 1) * 65], e[:, r * 128:(r + 1) * 128], vv[:, g * 65:(g + 1) * 65],
                         (g == 0 and ci == 0 and r == 0), ci == nch - 1, [e, VA, cva], [acc])
                if g == 1 and ci == nch - 1:
                    emit_out(acc, 's', n * 128)

            _pipeline(steps, s_fn, rest_fn)


def _mix_mla(self, l):
    K, I, S, C = self.K, self.I, self.S, self.C
    sc = 96.0 ** -0.5
    with ExitStack() as es:
        rec = K.sb(es, "mlrec", [128, 4], F32)
        ob = K.sb(es, "mlob", [128, 256], BF16)
        ot = K.sb(es, "mlot", [128, 2, 128], BF16)
        E = [K.sb(es, "mlaE%d" % i, [128, 512], BF16) for i in range(2)]
        pS = [K.ps(es, "mlpS%d" % i, [128, 512], F32) for i in range(2)]
        pA = [K.ps(es, "mlpA%d" % i, [128, 512], F32) for i in range(4)]
        pO = K.ps(es, "mlpO", [128, 8, 128], BF16)
        knT = K.sb(es, "mlknT", [64, 4, TS + PAST], BF16)
        krT = K.sb(es, "mlkrT", [32, TS + PAST], BF16)
        VA = K.sb(es, "mlVA", [128, 36, 260], BF16)
        qn = K.sb(es, "mlqn", [64, 4, 512], BF16)
        qr = K.sb(es, "mlqr", [32, 4, 512], BF16)

        def run(s, q0, nq, k0, nkc, extra_chunks):
            K.dma(qn[:, :, 0:nq], S['m64T_' + s][0:4, :, q0:q0 + nq].rearrange("g p t -> p g t"), [S['m64T_' + s]], [qn])
            K.dma(qr[:, :, 0:nq], S['m32T_' + s][0:4, :, q0:q0 + nq].rearrange("g p t -> p g t"), [S['m32T_' + s]], [qr])
            chunks = list(range(nkc)) + list(extra_chunks)
            steps = [(h, ci, c) for h in range(4) for ci, c in enumerate(chunks)]

            def s_fn(st, b):
                h, ci, c = st
                p = pS[b]
                K.mm(p[:, 0:nq], knT[:, h, c * 128:(c + 1) * 128], qn[:, h, 0:nq], True, False, [knT, qn], [p])
                K.mm(p[:, 0:nq], krT[:, c * 128:(c + 1) * 128], qr[:, h, 0:nq], False, True, [krT, qr], [p])

            def rest_fn(st, b):
                h, ci, c = st
                p, e = pS[b], E[b]
                K.act(e[:, 0:nq], p[:, 0:nq], AF.Exp, [p], [e], scale=sc)
                for j in range(nq // 128):
                    K.mm(pA[j][:, h * 65:(h + 1) * 65], e[:, j * 128:(j + 1) * 128], VA[:, c, h * 65:(h + 1) * 65],
                         ci == 0, ci == len(chunks) - 1, [e, VA], [pA[j]])

            _pipeline(steps, s_fn, rest_fn)
            for j in range(nq // 128):
                accv = pA[j][:, 0:260].rearrange("p (h c) -> p h c", h=4)
                self.finish_heads((pA[j], accv), 4, rec, (ob, ob[:].rearrange("p (h c) -> p h c", h=4)))
                for g in range(2):
                    K.tr(pO[:, g, :], ob[:, g * 128:(g + 1) * 128], self.ident_bf[:], [ob, self.ident_bf], [pO])
                K.v('dve', 'tensor_copy', [pO], [ot], out=ot[:], in_=pO[:, 0:2, :])
                t0 = q0 + j * 128
                K.dma(S['oT_' + s][6:8, :, t0:t0 + 128].rearrange("g p t -> p g t"), ot[:], [ot], [S['oT_' + s]])

        for sq in range(NPSEQ):
            b0 = sq * PSEQ
            K.dma(knT[:, :, 0:PSEQ], S['m64T_p'][4:8, :, b0:b0 + PSEQ].rearrange("g p t -> p g t"), [S['m64T_p']], [knT])
            K.dma(krT[:, 0:PSEQ], S['m32T_p'][4, :, b0:b0 + PSEQ], [S['m32T_p']], [krT])
            K.dma(VA[:, 0:2, :], S['vaug_p'][b0:b0 + PSEQ, 650:910].rearrange("(c p) f -> p c f", p=128),
                  [S['vaug_p']], [VA])
            run('p', b0, PSEQ, 0, 2, [])
        K.barrier()
        with ExitStack() as es2:
            stage = K.sb(es2, "mlstage", [128, 2, 256], F32)
            wuk = K.sb(es2, "mlwuk", [128, 256], BF16)
            wuv = K.sb(es2, "mlwuv", [128, 256], BF16)
            ctok = K.sb(es2, "mlctok", [128, 4, 160], F32)
            ctb = K.sb(es2, "mlctb", [128, 4, 160], BF16)
            ccT = K.sb(es2, "mlccT", [128, 512], BF16)
            pT = pO
            K.dma(knT[:, :, 0:TS], S['m64T_s'][4:8, :, :].rearrange("g p t -> p g t"), [S['m64T_s']], [knT])
            K.dma(krT[:, 0:TS], S['m32T_s'][4, :, :], [S['m32T_s']], [krT])
            self.load_tok(VA, lambda c0, n: VA[:, c0:c0 + n, :], S['vaug_s'], 650, 910, 0, 32)
            K.dma(stage[:, 0, :], I['w_uk'][l, :, :], [I['w_uk']], [stage])
            K.dma(stage[:, 1, :], I['w_uv'][l, :, :], [I['w_uv']], [stage])
            K.v('dve', 'tensor_copy', [stage], [wuk], out=wuk[:], in_=stage[:, 0, :])
            K.v('dve', 'tensor_copy', [stage], [wuv], out=wuv[:], in_=stage[:, 1, :])
            K.dma(ctok[:, :, 0:128], I['cckv'][l].rearrange("(c p) f -> p c f", p=128), [I['cckv']], [ctok])
            K.dma(ctok[:, :, 128:160], I['ckr'][l].rearrange("(c p) f -> p c f", p=128), [I['ckr']], [ctok])
            K.act(ctb[:], ctok[:], AF.Copy, [ctok], [ctb])
            for c in range(4):
                K.tr(pT[:, c, :], ctb[:, c, 0:128], self.ident_bf[:], [ctb, self.ident_bf], [pT])
                K.tr(pT[0:32, 4 + c, :], ctb[:, c, 128:160], self.ident_bf[:], [ctb, self.ident_bf], [pT])
            K.v('dve', 'tensor_copy', [pT], [ccT], out=ccT[:].rearrange("p (c t) -> p c t", c=4), in_=pT[:, 0:4, :])
            K.v('dve', 'tensor_copy', [pT], [krT], out=krT[:, TS:TS + PAST].rearrange("p (c t) -> p c t", c=4),
                in_=pT[0:32, 4:8, :])
            for h in range(4):
                p = pS[h % 2]
                K.mm(p[0:64, :], wuk[:, h * 64:(h + 1) * 64], ccT[:], True, True, [wuk, ccT], [p])
                K.act(knT[:, h, TS:TS + PAST], p[0:64, :], AF.Copy, [p], [knT])
            K.v('dve', 'memset', [], [VA], VA[:, 32:36, :], 1.0)
            for c in range(4):
                p = pS[c % 2]
                K.mm(p[:, 0:256], ccT[:, c * 128:(c + 1) * 128], wuv[:], True, True, [ccT, wuv], [p])
                K.v('dve', 'tensor_copy', [p], [VA], out=VA[:, 32 + c, :].rearrange("p (h e) -> p h e", h=4)[:, :, 0:64],
                    in_=p[:, 0:256].rearrange("p (h e) -> p h e", h=4))
            for qg in range(TS // 512):
                run('s', qg * 512, 512, 0, 32, [32, 33, 34, 35])


Prog.mix_sw = _mix_sw
Prog.mix_mla = _mix_mla


def _ml_part2(self, d, c, cs, dsl, tri, wcol, WT, PT, pS, ktok, kw, pA, VA, qk, Cbf, pC, Cdec, Cst, ecol, den, htmp, hsum):
    K = self.K
    K.v('pool', 'tensor_tensor', [tri[d], wcol], [WT[d]], out=WT[d][:],
        in0=tri[d][:].unsqueeze(1).to_broadcast([128, 4, 128]),
        in1=wcol[:, c, dsl].unsqueeze(2).to_broadcast([128, 4, 128]), op=ALU.mult)
    K.v('dve', 'tensor_tensor', [pS[d], WT[d]], [PT[d]], out=PT[d][:],
        in0=pS[d][:].rearrange("p (h t) -> p h t", h=4), in1=WT[d][:], op=ALU.mult)
    K.v('pool', 'tensor_tensor', [ktok, wcol], [kw[d]], out=kw[d][:],
        in0=ktok[:, c, :].rearrange("p (h e) -> p h e", h=4),
        in1=wcol[:, c, dsl].unsqueeze(2).to_broadcast([128, 4, 64]), op=ALU.mult)
    for h in range(4):
        K.mm(pA[d][:, h * 65:(h + 1) * 65], PT[d][:, h, :], VA[:, c, h * 65:(h + 1) * 65], h == 0, False,
             [PT[d], VA], [pA[d]])
    for h in (0, 2, 1, 3):
        g, hb = h // 2, (h % 2) * 64
        K.mm(pA[d][:, h * 65:(h + 1) * 65], qk[hb:hb + 64, g, cs], Cbf[d][hb:hb + 64, g, :], False, True,
             [qk, Cbf[d]], [pA[d]], pbase=hb)
    for g in range(2):
        for ab in range(2):
            K.mm(pC[d][:, (g * 2 + ab) * 65:(g * 2 + ab + 1) * 65],
                 kw[d][:, 2 * g:2 * g + 2, :].rearrange("p h e -> p (h e)"),
                 VA[:, c, (2 * g + ab) * 65:(2 * g + ab + 1) * 65], (g == 0 and ab == 0), True,
                 [kw[d], VA], [pC[d]])
    pcv = pC[d][:, 0:260].rearrange("p (g a e) -> p g a e", g=2, a=2)
    for hf in range(2):
        ps_ = slice(hf * 64, hf * 64 + 64)
        K.v('dve', 'tensor_tensor', [Cdec[d], pC[d]], [Cst[d]], out=Cst[d][ps_, :, :],
            in0=Cdec[d][ps_, :, :], in1=pcv[ps_, :, hf, :], op=ALU.add)
    accv = pA[d][:, 0:260].rearrange("p (h e) -> p h e", h=4)
    K.v('dve', 'tensor_copy', [pA[d]], [den[d]], out=den[d][:], in_=accv[:, :, 64])
    K.v('dve', 'scalar_tensor_tensor', [den[d]], [den[d]], out=den[d][:], in0=den[d][:], scalar=-1.0,
        in1=den[d][:], op0=ALU.mult, op1=ALU.max)
    K.v('dve', 'tensor_tensor', [den[d], ecol], [den[d]], out=den[d][:], in0=den[d][:],
        in1=ecol[:, c, dsl], op=ALU.max)
    K.v('dve', 'reciprocal', [den[d]], [den[d]], out=den[d][:], in_=den[d][:])
    K.v('dve', 'tensor_tensor', [pA[d], den[d]], [htmp[d]], out=htmp[d][:], in0=accv[:, :, 0:64],
        in1=den[d][:].unsqueeze(2).to_broadcast([128, 4, 64]), op=ALU.mult)
    K.v('pool', 'tensor_tensor', [hsum, htmp[d]], [hsum], out=hsum[:, c, :], in0=hsum[:, c, :],
        in1=htmp[d][:].rearrange("p h e -> p (h e)"), op=ALU.add)


def _mix_ml(self, l):
    K, I, S, C, O = self.K, self.I, self.S, self.C, self.O
    for s in ('p', 's'):
        seqs = [(sq, sq * PSEQ, PSEQ // 128) for sq in range(NPSEQ)] if s == 'p' else [(0, 0, TS // 128)]
        N = seqs[0][2]
        with ExitStack() as es:
            tle = K.sb(es, "mtle", [128, 128], F32)
            tge = K.sb(es, "mtge", [128, 128], F32)
            K.dma(tle[:], C['tri_le'][:, :], [C['tri_le']], [tle])
            K.dma(tge[:], C['tri_ge'][:, :], [C['tri_ge']], [tge])
            tri = (tle, tge)
            qk = K.sb(es, "mqk", [128, 4, N * 128], BF16)
            VA = K.sb(es, "mVA", [128, N, 260], BF16)
            stg = K.sb(es, "mstg", [128, 4, 256], F32)
            ktok = K.sb(es, "mktok", [128, N, 256], BF16)
            og = K.sb(es, "mog", [128, N, 256], BF16)
            hsum = K.sb(es, "mhsum", [128, N, 256], F32)
            G = K.sb(es, "mG", [128, N, 16], F32)
            sp = K.sb(es, "msp", [128, N, 8], F32)
            nb = K.sb(es, "mnb", [128, N, 8], F32)
            aa = K.sb(es, "maa", [128, N, 8], F32)
            tot = K.sb(es, "mtot", [128, N, 8], F32)
            amx = K.sb(es, "mamx", [128, N, 8], F32)
            Mc = K.sb(es, "mMc", [128, N, 8], F32)
            mpv = K.sb(es, "mmpv", [128, N, 8], F32)
            wcol = K.sb(es, "mwcol", [128, N, 8], F32)
            dcy = K.sb(es, "mdcy", [128, N, 8], F32)
            ecol = K.sb(es, "mecol", [128, N, 8], F32)
            mcur = K.sb(es, "mmcur", [128, 8], F32)
            amr = K.sb(es, "mamr", [4, 2, N], F32)
            Dm = K.sb(es, "mDm", [4, 2, N, 4], F32)
            Cst = [K.sb(es, "mCst%d" % d, [128, 2, 65], F32) for d in range(2)]
            Cdec = [K.sb(es, "mCdec%d" % d, [128, 2, 65], F32) for d in range(2)]
            Cbf = [K.sb(es, "mCbf%d" % d, [128, 2, 65], BF16) for d in range(2)]
            WT = [K.sb(es, "mWT%d" % d, [128, 4, 128], F32) for d in range(2)]
            PT = [K.sb(es, "mPT%d" % d, [128, 4, 128], BF16) for d in range(2)]
            kw = [K.sb(es, "mkw%d" % d, [128, 4, 64], BF16) for d in range(2)]
            den = [K.sb(es, "mden%d" % d, [128, 4], F32) for d in range(2)]
            htmp = [K.sb(es, "mht%d" % d, [128, 4, 64], F32) for d in range(2)]
            ss = K.sb(es, "mss", [128, 4], F32)
            sq2 = K.sb(es, "msq2", [128, 256], F32)
            ob = K.sb(es, "mob", [128, 256], BF16)
            ot = K.sb(es, "mot", [128, 2, 128], BF16)
            pS = [K.ps(es, "mpS%d" % d, [128, 512], F32) for d in range(2)]
            pA = [K.ps(es, "mpA%d" % d, [128, 512], F32) for d in range(2)]
            pC = [K.ps(es, "mpC%d" % d, [128, 512], F32) for d in range(2)]
            pX = K.ps(es, "mpX", [128, 512], F32)
            pO = K.ps(es, "mpO", [128, 8, 128], BF16)
            for (sq, b0, _) in seqs:
                K.dma(qk[:], S['featT_' + s][4:8, :, b0:b0 + N * 128].rearrange("g p t -> p g t"), [S['featT_' + s]], [qk])
                self.load_tok(VA, lambda c0, n: VA[:, c0:c0 + n, :], S['vaug_' + s], 260, 520, b0, N)
                self.load_tok(G, lambda c0, n: G[:, c0:c0 + n, :], S['mlx_' + s], 512, 528, b0, N)
                for c0 in range(0, N, 4):
                    n = min(4, N - c0)
                    self.load_tok(stg, lambda cc, nn: stg[:, 0:n, :], S['mlx_' + s], 0, 256, b0 + c0 * 128, n)
                    K.act(ktok[:, c0:c0 + n, :], stg[:, 0:n, :], AF.Copy, [stg], [ktok])
                    self.load_tok(stg, lambda cc, nn: stg[:, 0:n, :], S['mlx_' + s], 256, 512, b0 + c0 * 128, n)
                    K.act(og[:, c0:c0 + n, :], stg[:, 0:n, :], AF.Sigmoid, [stg], [og])
                K.v('dve', 'memset', [], [hsum], hsum[:], 0.0)
                K.act(sp[:], G[:, :, 8:16], AF.Exp, [G], [sp], scale=-1.0)
                K.act(sp[:], sp[:], AF.Ln, [sp], [sp], bias=1.0)
                for d in range(2):
                    K.mm(pX[:, d * N * 4:(d + 1) * N * 4].rearrange("p (c j) -> p c j", j=4), tri[d][:],
                         sp[:, :, d * 4:(d + 1) * 4], d == 0, d == 1, [tri[d], sp], [pX])
                K.v('dve', 'tensor_copy', [pX], [nb], out=nb[:].rearrange("p c (d j) -> p d c j", d=2),
                    in_=pX[:, 0:N * 8].rearrange("p (d c j) -> p d c j", d=2, c=N))
                K.v('dve', 'tensor_tensor', [G, nb], [aa], out=aa[:], in0=G[:, :, 0:8], in1=nb[:], op=ALU.add)
                K.mm(pX[:, 0:N * 8], self.ones_f[:], sp[:].rearrange("p c j -> p (c j)"), True, True, [self.ones_f, sp], [pX])
                K.v('dve', 'tensor_copy', [pX], [tot], out=tot[:], in_=pX[:, 0:N * 8].rearrange("p (c j) -> p c j", c=N))
                for d in range(2):
                    for c0 in range(0, N, 4):
                        n = min(4, N - c0)
                        for ci in range(n):
                            c = c0 + ci
                            K.mm(pX[0:4, ci * 128:(ci + 1) * 128], G[:, c, d * 4:(d + 1) * 4], self.ident_f[:], ci == 0,
                                 False, [G, self.ident_f], [pX])
                            K.mm(pX[0:4, ci * 128:(ci + 1) * 128], sp[:, c, d * 4:(d + 1) * 4], tri[d][:], False, True,
                                 [sp, tri[d]], [pX])
                        K.v('dve', 'tensor_reduce', [pX], [amr], out=amr[:, d, c0:c0 + n],
                            in_=pX[0:4, 0:n * 128].rearrange("p (c t) -> p c t", c=n), axis=AX.X, op=ALU.max)
                K.v('dve', 'tensor_tensor', [amr, self.ident_f], [Dm], out=Dm[:],
                    in0=amr[:].unsqueeze(3).to_broadcast([4, 2, N, 4]),
                    in1=self.ident_f[0:4, 0:4].unsqueeze(1).unsqueeze(1).to_broadcast([4, 2, N, 4]), op=ALU.mult)
                K.mm(pX[:, 0:N * 8], self.ones_f[0:4, :], Dm[:].rearrange("p d c j -> p (d c j)"), True, True,
                     [self.ones_f, Dm], [pX])
                K.v('dve', 'tensor_copy', [pX], [amx], out=amx[:].rearrange("p c (d j) -> p d c j", d=2),
                    in_=pX[:, 0:N * 8].rearrange("p (d c j) -> p d c j", d=2, c=N))
                if s == 's':
                    K.dma(mcur[:], bcast_rows(I['stm'][l:l + 1, :], 128), [I['stm']], [mcur])
                else:
                    K.v('dve', 'memset', [], [mcur], mcur[:], 0.0)
                for j in range(N):
                    for d in range(2):
                        c = j if d == 0 else N - 1 - j
                        sl = slice(d * 4, d * 4 + 4)
                        K.v('dve', 'tensor_copy', [mcur], [mpv], out=mpv[:, c, sl], in_=mcur[:, sl])
                        K.v('dve', 'tensor_tensor', [mcur, amx], [Mc], out=Mc[:, c, sl], in0=mcur[:, sl], in1=amx[:, c, sl],
                            op=ALU.max)
                        K.v('dve', 'tensor_tensor', [Mc, tot], [mcur], out=mcur[:, sl], in0=Mc[:, c, sl], in1=tot[:, c, sl],
                            op=ALU.subtract)
                if s == 'p':
                    K.dma(O['o_m'][sq, l:l + 1, :], mcur[0:1, :], [mcur], [O['o_m']])
                K.v('dve', 'tensor_tensor', [aa, Mc], [wcol], out=wcol[:], in0=aa[:], in1=Mc[:], op=ALU.subtract)
                K.act(wcol[:], wcol[:], AF.Exp, [wcol], [wcol])
                K.v('dve', 'tensor_tensor', [mpv, Mc], [dcy], out=dcy[:], in0=mpv[:], in1=Mc[:], op=ALU.subtract)
                K.act(dcy[:], dcy[:], AF.Exp, [dcy], [dcy])
                K.v('dve', 'tensor_tensor', [nb, Mc], [ecol], out=ecol[:], in0=nb[:], in1=Mc[:], op=ALU.subtract)
                K.act(ecol[:], ecol[:], AF.Exp, [ecol], [ecol])
                for d in range(2):
                    if s == 's':
                        for h in range(4):
                            g, hb = h // 2, (h % 2) * 64
                            K.dma(Cst[d][hb:hb + 64, g, 0:64], I['stC'][l, d * 4 + h, :, :], [I['stC']], [Cst[d]])
                            K.dma(Cst[d][hb:hb + 64, g, 64:65], I['stn'][l, d * 4 + h:d * 4 + h + 1, :].rearrange("o n -> n o"),
                                  [I['stn']], [Cst[d]])
                    else:
                        K.v('dve', 'memset', [], [Cst[d]], Cst[d][:], 0.0)
                for j in range(N):
                  for part in range(2):
                    for d in range(2):
                        c = j if d == 0 else N - 1 - j
                        cs = slice(c * 128, (c + 1) * 128)
                        dsl = slice(d * 4, d * 4 + 4)
                        if part == 1:
                            self._ml_part2(d, c, cs, dsl, tri, wcol, WT, PT, pS, ktok, kw, pA, VA, qk, Cbf, pC, Cdec, Cst, ecol,
                                           den, htmp, hsum)
                            continue
                        for g in range(2):
                            for hf in range(2):
                                ps_ = slice(hf * 64, hf * 64 + 64)
                                col = d * 4 + 2 * g + hf
                                K.v('pool', 'tensor_scalar', [Cst[d], dcy], [Cdec[d]], out=Cdec[d][ps_, g, :],
                                    in0=Cst[d][ps_, g, :], scalar1=dcy[ps_, c, col:col + 1], scalar2=None, op0=ALU.mult)
                        K.act(Cbf[d][:], Cdec[d][:], AF.Copy, [Cdec[d]], [Cbf[d]])
                        for h in (0, 2, 1, 3):
                            g, hb = h // 2, (h % 2) * 64
                            K.mm(pS[d][:, h * 128:(h + 1) * 128], qk[hb:hb + 64, 2 + g, cs], qk[hb:hb + 64, g, cs], h == 0, True,
                                 [qk], [pS[d]], pbase=hb)
                        continue
                if s == 'p':
                    for d in range(2):
                        for h in range(4):
                            g, hb = h // 2, (h % 2) * 64
                            K.dma(O['o_C'][sq, l, d * 4 + h, :, :], Cst[d][hb:hb + 64, g, 0:64], [Cst[d]], [O['o_C']])
                            K.dma(O['o_n'][sq, l, d * 4 + h:d * 4 + h + 1, :].rearrange("o n -> n o"),
                                  Cst[d][hb:hb + 64, g, 64:65], [Cst[d]], [O['o_n']])
                for c in range(N):
                    K.v('dve', 'tensor_tensor', [hsum], [sq2], out=sq2[:], in0=hsum[:, c, :], in1=hsum[:, c, :], op=ALU.mult)
                    K.v('dve', 'tensor_reduce', [sq2], [ss], out=ss[:], in_=sq2[:].rearrange("p (h e) -> p h e", h=4),
                        axis=AX.X, op=ALU.add)
                    K.v('dve', 'tensor_scalar', [ss], [ss], out=ss[:], in0=ss[:], scalar1=1.0 / 64, scalar2=EPS,
                        op0=ALU.mult, op1=ALU.add)
                    K.act(ss[:], ss[:], AF.Sqrt, [ss], [ss])
                    K.v('dve', 'reciprocal', [ss], [ss], out=ss[:], in_=ss[:])
                    K.v('dve', 'tensor_tensor', [hsum, ss], [sq2], out=sq2[:].rearrange("p (h e) -> p h e", h=4),
                        in0=hsum[:, c, :].rearrange("p (h e) -> p h e", h=4),
                        in1=ss[:].unsqueeze(2).to_broadcast([128, 4, 64]), op=ALU.mult)
                    K.v('dve', 'tensor_tensor', [sq2, og], [ob], out=ob[:], in0=sq2[:], in1=og[:, c, :], op=ALU.mult)
                    for g in range(2):
                        K.tr(pO[:, g, :], ob[:, g * 128:(g + 1) * 128], self.ident_bf[:], [ob, self.ident_bf], [pO])
                    K.v('dve', 'tensor_copy', [pO], [ot], out=ot[:], in_=pO[:, 0:2, :])
                    t0 = b0 + c * 128
                    K.dma(S['oT_' + s][2:4, :, t0:t0 + 128].rearrange("g p t -> p g t"), ot[:], [ot], [S['oT_' + s]])
        K.barrier()


Prog.mix_ml = _mix_ml
Prog._ml_part2 = _ml_part2


def _phase_C(self, l):
    self.phase_C1(l)
    self.K.barrier()
    self.phase_C2(l)


def _phase_C1(self, l):
    K, I, S = self.K, self.I, self.S
    with ExitStack() as es:
        wing = K.sb(es, "c1wing", [128, 8, 4096], BF16)
        wbr = K.sb(es, "c1wbr", [128, 8, 1024], BF16)
        wout = K.sb(es, "c1wout", [128, 8, 1024], BF16)
        stage = K.sb(es, "c1stage", [128, 8, 512], F32)
        bg = K.sb(es, "c1bg", [128, 4096], F32)
        x = K.sb(es, "c1x", [128, D], F32)
        h32 = K.sb(es, "c1h32", [128, D], F32)
        hbf = K.sb(es, "c1hbf", [128, D], BF16)
        hT = K.sb(es, "c1hT", [128, 8, 128], BF16)
        ss = K.sb(es, "c1ss", [128, 4], F32)
        oTt = K.sb(es, "c1oTt", [128, 8, 128], BF16)
        sg = K.sb(es, "c1sg", [128, 1024], F32)
        acc = K.sb(es, "c1acc", [128, 1024], F32)
        accb = K.sb(es, "c1accb", [128, 1024], BF16)
        accT = K.sb(es, "c1accT", [128, 8, 128], BF16)
        pT = K.ps(es, "c1pT", [128, 8, 128], BF16)
        pG = [K.ps(es, "c1pG%d" % i, [128, 512], F32) for i in range(2)]
        pY = [K.ps(es, "c1pY%d" % i, [128, 512], F32) for i in range(2)]
        pM = [K.ps(es, "c1pM%d" % i, [128, 512], F32) for i in range(2)]
        for j in range(8):
            K.dma(stage[:], I['w_in'][l, :, O_GATE + j * 512:O_GATE + (j + 1) * 512].rearrange("(k p) n -> p k n", p=128),
                  [I['w_in']], [stage])
            K.v('pool', 'tensor_copy', [stage], [wing], out=wing[:, :, j * 512:(j + 1) * 512], in_=stage[:])
        for j in range(2):
            K.dma(stage[:], I['w_branch'][l, :, j * 512:(j + 1) * 512].rearrange("(k p) n -> p k n", p=128),
                  [I['w_branch']], [stage])
            K.v('pool', 'tensor_copy', [stage], [wbr], out=wbr[:, :, j * 512:(j + 1) * 512], in_=stage[:])
            K.dma(stage[:], I['w_out'][l, :, j * 512:(j + 1) * 512].rearrange("(k p) n -> p k n", p=128),
                  [I['w_out']], [stage])
            K.v('pool', 'tensor_copy', [stage], [wout], out=wout[:, :, j * 512:(j + 1) * 512], in_=stage[:])
        K.dma(bg[:], bcast_rows(I['b_in'][l:l + 1, O_GATE:INW], 128), [I['b_in']], [bg])
        for s in ('p', 's'):
            with ExitStack() as es2:
                mods = self.load_mod(es2, l, s, [0, 1, 2], "mC1")
                A1, B1, G1 = mods[1], mods[0], mods[2]
                xsrc = (I['xp'] if s == 'p' else I['xs']) if l == 0 else S['xres_' + s]
                for t in range(self.T[s] // 128):
                    t0 = t * 128
                    K.dma(x[:], xsrc[t0:t0 + 128, :], [xsrc], [x])
                    K.dma(oTt[:], S['oT_' + s][:, :, t0:t0 + 128].rearrange("g p t -> p g t"), [S['oT_' + s]], [oTt])
                    self.norm_mod(x, A1, B1, h32, hbf, ss, h32)
                    for k in range(8):
                        K.tr(pT[:, k, :], hbf[:, k * 128:(k + 1) * 128], self.ident_bf[:], [hbf, self.ident_bf], [pT])
                    K.act(hT[:], pT[:], AF.Copy, [pT], [hT])
                    for n in range(4):
                        for j in range(2):
                            c0 = n * 1024 + j * 512
                            for k in range(8):
                                K.mm(pG[j][:], hT[:, k, :], wing[:, k, c0:c0 + 512], k == 0, k == 7, [hT, wing], [pG[j]])
                            K.v('dve', 'tensor_tensor', [pG[j], bg], [sg], out=sg[:, j * 512:(j + 1) * 512], in0=pG[j][:],
                                in1=bg[:, c0:c0 + 512], op=ALU.add)
                            for kc in range(2):
                                K.mm(pY[j][:], oTt[:, 2 * n + kc, :], wbr[:, 2 * n + kc, j * 512:(j + 1) * 512], kc == 0,
                                     kc == 1, [oTt, wbr], [pY[j]])
                        K.act(sg[:], sg[:], AF.Sigmoid, [sg], [sg])
                        for j in range(2):
                            sl = slice(j * 512, (j + 1) * 512)
                            if n == 0:
                                K.v('dve', 'tensor_tensor', [pY[j], sg], [acc], out=acc[:, sl], in0=pY[j][:], in1=sg[:, sl],
                                    op=ALU.mult)
                            else:
                                K.v('dve', 'tensor_tensor', [pY[j], sg], [sg], out=sg[:, sl], in0=pY[j][:], in1=sg[:, sl],
                                    op=ALU.mult)
                                K.v('pool', 'tensor_tensor', [acc, sg], [acc], out=acc[:, sl], in0=acc[:, sl], in1=sg[:, sl],
                                    op=ALU.add)
                    K.act(accb[:], acc[:], AF.Copy, [acc], [accb])
                    for k in range(8):
                        K.tr(pT[:, k, :], accb[:, k * 128:(k + 1) * 128], self.ident_bf[:], [accb, self.ident_bf], [pT])
                    K.act(accT[:], pT[:], AF.Copy, [pT], [accT])
                    for j in range(2):
                        for k in range(8):
                            K.mm(pM[j][:], accT[:, k, :], wout[:, k, j * 512:(j + 1) * 512], k == 0, k == 7, [accT, wout],
                                 [pM[j]])
                        sl = slice(j * 512, (j + 1) * 512)
                        K.v('dve', 'tensor_tensor', [pM[j], G1], [h32], out=h32[:, sl], in0=pM[j][:], in1=G1[:, sl],
                            op=ALU.mult)
                    K.v('dve', 'tensor_tensor', [x, h32], [h32], out=h32[:], in0=x[:], in1=h32[:], op=ALU.add)
                    K.dma(S['xmid_' + s][t0:t0 + 128, :], h32[:], [h32], [S['xmid_' + s]])
            K.barrier()


def _phase_C2(self, l):
    K, I, S, O = self.K, self.I, self.S, self.O
    last = (l == self.depth - 1)
    with ExitStack() as es:
        wq = K.sb(es, "c2wq", [128, 8, 1024], BF16)
        kb = K.sb(es, "c2kb", [128, 16, 64], BF16)
        keysT = K.sb(es, "c2keysT", [128, 8, 128], BF16)
        fng = K.sb(es, "c2fng", [128, D], F32)
        x2 = [K.sb(es, "c2x%d" % i, [128, D], F32) for i in range(2)]
        h32 = K.sb(es, "c2h32", [128, D], F32)
        hbf = K.sb(es, "c2hbf", [128, D], BF16)
        hT = K.sb(es, "c2hT", [128, 8, 128], BF16)
        ss = K.sb(es, "c2ss", [128, 4], F32)
        ss2 = K.sb(es, "c2ss2", [128, 4], F32)
        qT = K.sb(es, "c2qT", [128, 8, 128], BF16)
        sc = K.sb(es, "c2sc", [128, 16, 128], F32)
        scw = K.sb(es, "c2scw", [128, 16, 128], F32)
        topv = K.sb(es, "c2topv", [128, 16, 16], F32)
        topi = K.sb(es, "c2topi", [128, 16, 16], U32)
        topf = K.sb(es, "c2topf", [128, 16, 16], F32)
        cand = K.sb(es, "c2cand", [128, 8, 256], F32)
        cidx = K.sb(es, "c2cidx", [128, 8, 256], F32)
        tv = K.sb(es, "c2tv", [128, 8, 16], F32)
        eq = K.sb(es, "c2eq", [128, 8, 256], F32)
        idxf = K.sb(es, "c2idxf", [128, 128], F32)
        idx2 = [K.sb(es, "c2idx%d" % i, [128, 128], I32) for i in range(2)]
        gw2 = [K.sb(es, "c2gw%d" % i, [128, 8, 16], F32) for i in range(2)]
        ga2 = [K.sb(es, "c2ga%d" % i, [128, 128], F32) for i in range(2)]
        zz = K.sb(es, "c2zz", [128, 8], F32)
        aa = K.sb(es, "c2aa", [128, 128], F32)
        t1 = K.sb(es, "c2t1", [128, 128], F32)
        t2 = K.sb(es, "c2t2", [128, 128], F32)
        acc = K.sb(es, "c2acc", [128, D], F32)
        junk = acc
        dg = [K.sb(es, "c2dg%d" % i, [128, 128], F32) for i in range(4)]
        pT = K.ps(es, "c2pT", [128, 8, 128], BF16)
        pQ = [K.ps(es, "c2pQ%d" % i, [128, 512], F32) for i in range(2)]
        pS = [K.ps(es, "c2pS%d" % i, [128, 512], F32) for i in range(2)]
        pV = [K.ps(es, "c2pV%d" % i, [128, 512], F32) for i in range(2)]
        with ExitStack() as esw:
            stage = K.sb(esw, "c2stage", [128, 8, 512], F32)
            for j in range(2):
                K.dma(stage[:], I['peer_wq'][l, :, j * 512:(j + 1) * 512].rearrange("(k p) n -> p k n", p=128),
                      [I['peer_wq']], [stage])
                K.v('pool', 'tensor_copy', [stage], [wq], out=wq[:, :, j * 512:(j + 1) * 512], in_=stage[:])
            K.dma(stage[:, 0:2, :].rearrange("p a (b d) -> p (a b) d", d=64), I['peer_keys'][l].rearrange("g n d -> n g d"),
                  [I['peer_keys']], [stage])
            K.v('pool', 'tensor_copy', [stage], [kb], out=kb[:], in_=stage[:, 0:2, :].rearrange("p a (b d) -> p (a b) d", d=64))
            K.barrier()
        NR = 20
        ring = [K.sb(es, "c2ring%d" % i, [128, D], F32) for i in range(NR)]
        rc = [0]

        def next_slot():
            i = rc[0] % NR
            rc[0] += 1
            return i

        for hh in range(8):
            K.tr(pT[:, hh, :], kb[:, 2 * hh:2 * hh + 2, :].rearrange("p a d -> p (a d)"), self.ident_bf[:],
                 [kb, self.ident_bf], [pT])
        K.act(keysT[:], pT[:], AF.Copy, [pT], [keysT])
        K.dma(fng[:], bcast_rows(I['final_norm_g'][0:1, :], 128), [I['final_norm_g']], [fng])
        pu, pv = I['peer_u'], I['peer_v']

        def gat(dst, tab, idx, col, slot):
            K.op('pool', [idx, tab], [dst], (lambda: self.nc.gpsimd.indirect_dma_start(
                out=dst[:], out_offset=None, in_=tab[:, :],
                in_offset=bass.IndirectOffsetOnAxis(ap=idx[:, col:col + 1], axis=0))), dma=True, slot=slot)

        for s in ('p', 's'):
            with ExitStack() as es2:
                mods = self.load_mod(es2, l, s, [3, 4, 5], "mC2")
                A2, B2, G2 = mods[4], mods[3], mods[5]

                def stage_R(t):
                    x, idx, gw = x2[t % 2], idx2[t % 2], gw2[t % 2]
                    t0 = t * 128
                    K.dma(x[:], S['xmid_' + s][t0:t0 + 128, :], [S['xmid_' + s]], [x])
                    self.norm_mod(x, A2, B2, h32, None, ss, junk)
                    K.act(hbf[:], h32[:], AF.Copy, [h32], [hbf])
                    for k in range(8):
                        K.tr(pT[:, k, :], hbf[:, k * 128:(k + 1) * 128], self.ident_bf[:], [hbf, self.ident_bf], [pT])
                    K.act(hT[:], pT[:], AF.Copy, [pT], [hT])
                    for j in range(8):
                        pq = pQ[j // 4]
                        for k in range(8):
                            K.mm(pq[:, (j % 4) * 128:(j % 4 + 1) * 128], wq[:, k, j * 128:(j + 1) * 128], hT[:, k, :],
                                 (k == 0 and j % 4 == 0), k == 7, [wq, hT], [pq])
                    for i in range(2):
                        K.act(qT[:, i * 4:(i + 1) * 4, :], pQ[i][:].rearrange("p (j t) -> p j t", j=4), AF.Copy, [pQ[i]], [qT])
                    for half in range(2):
                        for xx in range(2):
                            for h4 in range(4):
                                hh = half * 4 + h4
                                b = h4 // 2
                                cix = (h4 % 2) * 2 + xx
                                K.mm(pS[b][:, cix * 128:(cix + 1) * 128], qT[xx * 64:xx * 64 + 64, hh, :],
                                     keysT[xx * 64:xx * 64 + 64, hh, :], (xx == 0 and h4 % 2 == 0), True, [qT, keysT], [pS[b]],
                                     pbase=xx * 64)
                        for b in range(2):
                            K.act(sc[:, half * 8 + b * 4:half * 8 + b * 4 + 4, :], pS[b][:].rearrange("p (j t) -> p j t", j=4),
                                  AF.Copy, [pS[b]], [sc])
                    for hx in range(16):
                        K.v('dve', 'max', [sc], [topv], out=topv[:, hx, 0:8], in_=sc[:, hx, :])
                        K.v('dve', 'max_index', [sc, topv], [topi], out=topi[:, hx, 0:8], in_max=topv[:, hx, 0:8],
                            in_values=sc[:, hx, :])
                        K.v('dve', 'match_replace', [sc, topv], [scw], out=scw[:, hx, :], in_to_replace=topv[:, hx, 0:8],
                            in_values=sc[:, hx, :], imm_value=-1e30)
                        K.v('dve', 'max', [scw], [topv], out=topv[:, hx, 8:16], in_=scw[:, hx, :])
                        K.v('dve', 'max_index', [scw, topv], [topi], out=topi[:, hx, 8:16], in_max=topv[:, hx, 8:16],
                            in_values=scw[:, hx, :])
                    K.v('dve', 'tensor_copy', [topi], [topf], out=topf[:], in_=topi[:])
                    tvv = topv[:].rearrange("p (h x) k -> p h x k", x=2)
                    tff = topf[:].rearrange("p (h x) k -> p h x k", x=2)
                    K.v('dve', 'tensor_tensor', [topv], [cand], out=cand[:].rearrange("p h (a b) -> p h a b", a=16),
                        in0=tvv[:, :, 0, :].unsqueeze(3).to_broadcast([128, 8, 16, 16]),
                        in1=tvv[:, :, 1, :].unsqueeze(2).to_broadcast([128, 8, 16, 16]), op=ALU.add)
                    K.v('dve', 'tensor_scalar', [topf], [topf], out=tff[:, :, 0, :], in0=tff[:, :, 0, :], scalar1=128.0,
                        scalar2=None, op0=ALU.mult)
                    K.v('dve', 'tensor_tensor', [topf], [cidx], out=cidx[:].rearrange("p h (a b) -> p h a b", a=16),
                        in0=tff[:, :, 0, :].unsqueeze(3).to_broadcast([128, 8, 16, 16]),
                        in1=tff[:, :, 1, :].unsqueeze(2).to_broadcast([128, 8, 16, 16]), op=ALU.add)
                    for hh in range(8):
                        K.v('dve', 'max', [cand], [tv], out=tv[:, hh, 0:8], in_=cand[:, hh, :])
                        candw = scw[:].rearrange("p (h a) n -> p h (a n)", h=8)
                        K.v('dve', 'match_replace', [cand, tv], [scw], out=candw[:, hh, :], in_to_replace=tv[:, hh, 0:8],
                            in_values=cand[:, hh, :], imm_value=-1e30)
                        K.v('dve', 'max', [scw], [tv], out=tv[:, hh, 8:16], in_=candw[:, hh, :])
                        for kh in range(2):
                            ks = slice(kh * 8, kh * 8 + 8)
                            K.v('dve', 'tensor_tensor', [cand, tv], [eq], out=eq[:],
                                in0=cand[:, hh, :].unsqueeze(1).to_broadcast([128, 8, 256]),
                                in1=tv[:, hh, ks].unsqueeze(2).to_broadcast([128, 8, 256]), op=ALU.is_equal)
                            K.v('dve', 'tensor_tensor', [eq, cidx], [eq], out=eq[:], in0=eq[:],
                                in1=cidx[:, hh, :].unsqueeze(1).to_broadcast([128, 8, 256]), op=ALU.mult)
                            K.v('dve', 'tensor_reduce', [eq], [idxf], out=idxf[:, hh * 16 + kh * 8:hh * 16 + kh * 8 + 8],
                                in_=eq[:], axis=AX.X, op=ALU.add)
                    if l > 0:
                        K.v('dve', 'tensor_scalar', [idxf], [idxf], out=idxf[:], in0=idxf[:], scalar1=float(l * 16384),
                            scalar2=None, op0=ALU.add)
                    K.v('dve', 'tensor_scalar', [idxf], [idxf], out=idxf[:], in0=idxf[:], scalar1=float(l * 16384),
                        scalar2=float(l * 16384 + 16383), op0=ALU.max, op1=ALU.min)
                    K.v('dve', 'tensor_copy', [idxf], [idx], out=idx[:], in_=idxf[:])

                def stage_R2(t):
                    gw = gw2[t % 2]
                    K.v('dve', 'tensor_tensor', [tv], [gw], out=gw[:], in0=tv[:],
                        in1=tv[:, :, 0:1].to_broadcast([128, 8, 16]), op=ALU.subtract)
                    K.act(gw[:], gw[:], AF.Exp, [gw], [gw])
                    K.v('dve', 'tensor_reduce', [gw], [zz], out=zz[:], in_=gw[:], axis=AX.X, op=ALU.add)
                    K.v('dve', 'reciprocal', [zz], [zz], out=zz[:], in_=zz[:])
                    K.v('dve', 'tensor_tensor', [gw, zz], [gw], out=gw[:], in0=gw[:],
                        in1=zz[:].unsqueeze(2).to_broadcast([128, 8, 16]), op=ALU.mult)

                def stage_U(t):
                    idx, gw, ga = idx2[t % 2], gw2[t % 2], ga2[t % 2]
                    for col in range(128):
                        si = next_slot()
                        u_ = ring[si]
                        gat(u_, pu, idx, col, si)
                        K.v('dve', 'scalar_tensor_tensor', [u_, h32], [u_, aa], out=u_[:], in0=u_[:], scalar=1.0, in1=h32[:],
                            op0=ALU.mult, op1=ALU.mult, accum_out=aa[:, col:col + 1])
                    K.v('dve', 'tensor_tensor', [aa], [t1], out=t1[:], in0=aa[:], in1=aa[:], op=ALU.mult)
                    K.v('dve', 'tensor_scalar', [t1], [t1], out=t1[:], in0=t1[:], scalar1=0.044715, scalar2=1.0,
                        op0=ALU.mult, op1=ALU.add)
                    K.v('dve', 'tensor_tensor', [t1, aa], [t1], out=t1[:], in0=t1[:], in1=aa[:], op=ALU.mult)
                    K.act(t2[:], t1[:], AF.Tanh, [t1], [t2], scale=0.7978845608028654)
                    K.v('dve', 'tensor_scalar', [t2], [t2], out=t2[:], in0=t2[:], scalar1=1.0, scalar2=0.5, op0=ALU.add,
                        op1=ALU.mult)
                    K.v('dve', 'tensor_tensor', [t2, aa], [t2], out=t2[:], in0=t2[:], in1=aa[:], op=ALU.mult)
                    K.v('dve', 'tensor_tensor', [t2, gw], [ga], out=ga[:], in0=t2[:], in1=gw[:].rearrange("p h k -> p (h k)"),
                        op=ALU.mult)

                def stage_V(t):
                    idx, ga = idx2[t % 2], ga2[t % 2]
                    for col in range(128):
                        si = next_slot()
                        v_ = ring[si]
                        d_ = dg[col % 4]
                        gat(v_, pv, idx, col, si)
                        K.act(d_[:], self.ident_f[:], AF.Copy, [self.ident_f, ga], [d_], scale=ga[:, col:col + 1])
                        for j in range(2):
                            K.mm(pV[j][:], d_[:], v_[:, j * 512:(j + 1) * 512], col == 0, col == 127, [d_, v_], [pV[j]])

                def stage_F(t):
                    x = x2[t % 2]
                    t0 = t * 128
                    for j in range(2):
                        sl = slice(j * 512, (j + 1) * 512)
                        K.v('dve', 'tensor_tensor', [pV[j], G2], [acc], out=acc[:, sl], in0=pV[j][:], in1=G2[:, sl], op=ALU.mult)
                    K.v('dve', 'tensor_tensor', [acc, x], [x], out=x[:], in0=x[:], in1=acc[:], op=ALU.add)
                    if not last:
                        K.dma(S['xres_' + s][t0:t0 + 128, :], x[:], [x], [S['xres_' + s]])
                    else:
                        K.act(acc[:], x[:], AF.Square, [x], [acc, ss2], accum_out=ss2[:, 1:2])
                        self.rstd(ss2, 1, D)
                        K.v('dve', 'scalar_tensor_tensor', [x, ss2, fng], [acc], out=acc[:], in0=x[:], scalar=ss2[:, 1:2],
                            in1=fng[:], op0=ALU.mult, op1=ALU.mult)
                        K.dma(O['y_' + s][t0:t0 + 128, :], acc[:], [acc], [O['y_' + s]])

                nt = self.T[s] // 128
                stage_R(0)
                stage_R2(0)
                stage_U(0)
                for t in range(nt):
                    if t + 1 < nt:
                        stage_R(t + 1)
                    stage_V(t)
                    if t + 1 < nt:
                        stage_R2(t + 1)
                    stage_F(t)
                    if t + 1 < nt:
                        stage_U(t + 1)
            K.barrier()


Prog.phase_C = _phase_C
Prog.phase_C1 = _phase_C1
Prog.phase_C2 = _phase_C2


_CACHE = {}


def _build():
    if 'nc' not in _CACHE:
        nc = bass.Bass("TRN2", target_bir_lowering=False)
        P = Prog(nc, depth=DEPTH)
        P.build()
        _CACHE['nc'] = nc
        _CACHE['consts'] = make_consts()
    return _CACHE['nc'], _CACHE['consts']


def kernel(**inputs):
    inp = {k: np.asarray(v) for k, v in inputs.items()}
    nc, consts = _build()
    in_maps = [prep_core_inputs(inp, c, consts) for c in range(NCORES)]
    res = run_bass_kernel_spmd(nc, in_maps, core_ids=list(range(NCORES)))
    R = res.results
    cat = lambda name: np.concatenate([np.asarray(R[c][name]) for c in range(NCORES)], axis=0)
    B = NCORES * NPSEQ
    y_p = cat('y_p').reshape(B, PSEQ, D)
    y_s = np.stack([np.asarray(R[c]['y_s']) for c in range(NCORES)], axis=0)
    na_k = cat('o_nak').reshape(B, DEPTH, PSEQ, 4, 64)
    na_v = cat('o_nav').reshape(B, DEPTH, PSEQ, 4, 64)
    mC = cat('o_C').reshape(B, DEPTH, 2, 4, 64, 64)
    mn = cat('o_n').reshape(B, DEPTH, 2, 4, 64)
    mm = cat('o_m').reshape(B, DEPTH, 2, 4)
    sw_k = cat('o_swk').reshape(B, DEPTH, PSEQ, 2, 64)
    sw_v = cat('o_swv').reshape(B, DEPTH, PSEQ, 2, 64)
    ckv = cat('o_ckv').reshape(B, DEPTH, PSEQ, 128)
    kr = cat('o_kr').reshape(B, DEPTH, PSEQ, 32)
    outs = (y_p, y_s, na_k, na_v, mC, mn, mm, sw_k, sw_v, ckv, kr)
    return tuple(np.ascontiguousarray(o, dtype=np.float32) for o in outs)
```

```python
import numpy as np
import ml_dtypes
from contextlib import ExitStack
import concourse.bass as bass
import concourse.mybir as mybir
from concourse.bass_utils import run_bass_kernel_spmd

F32 = mybir.dt.float32
F32R = mybir.dt.float32r
BF16 = mybir.dt.bfloat16
I32 = mybir.dt.int32
U32 = mybir.dt.uint32
AF = mybir.ActivationFunctionType
ALU = mybir.AluOpType
AX = mybir.AxisListType

D = 1024
DEPTH = 4
NCORES = 8
PSEQ = 256
NPSEQ = 4
TP = NPSEQ * PSEQ
TS = 4096
PAST = 512
INW = 6768
NG = 2672
EPS = 1e-6

O_NAQ, O_NAK, O_NAV = 0, 256, 512
O_MLQ, O_MLK, O_MLV, O_MLO, O_MLI, O_MLF = 768, 1024, 1280, 1536, 1792, 1800
O_SWQ, O_SWK, O_SWV = 1808, 2064, 2192
O_CQ, O_CKV, O_KR = 2320, 2512, 2640
O_GATE = 2672


class Dummy:
    def __getitem__(self, k):
        return self

    def __getattr__(self, n):
        return self

    def __call__(self, *a, **k):
        return self


class Ten:
    def __init__(self, t, multi=False):
        self.t = t
        self.multi = multi
        self.w = {}
        self.r = {}

    def __getitem__(self, k):
        return self.t[k]

    def ap(self):
        return self.t[:]

    def reset(self):
        self.w = {}
        self.r = {}


class Builder:
    ENG = ['pe', 'act', 'dve', 'pool', 'sp']

    def __init__(self, nc, ndma=40):
        self.nc = nc
        self.h = {'pe': nc.tensor, 'act': nc.scalar, 'dve': nc.vector, 'pool': nc.gpsimd, 'sp': nc.sync}
        self.ndma = ndma
        self.dram = []
        self.need = set()
        self.uid = 0
        self.es = ExitStack()
        self.sem = {e: self.es.enter_context(nc.semaphore("s_" + e)) for e in self.ENG}
        self.dsem = [self.es.enter_context(nc.semaphore("d_%d" % i)) for i in range(ndma + 24)]

    def begin(self, dry):
        self.dry = dry
        self.opi = 0
        self.seq = {e: 0 for e in self.ENG}
        self.sig = {e: 0 for e in self.ENG}
        self.seen = {e: {} for e in self.ENG}
        self.last = {}
        self.last_pb = 0
        self.dval = [0] * (self.ndma + 24)
        self.drr = 0
        self.nwait = 0
        for t in self.dram:
            t.reset()

    def run(self, fn):
        self.begin(True)
        fn(self)
        nops = self.opi
        self.begin(False)
        fn(self)
        assert self.opi == nops, (self.opi, nops)

    def dr(self, name, shape, dtype, kind="Internal"):
        t = Ten(self.nc.dram_tensor(name, list(shape), dtype, kind=kind), multi=True)
        self.dram.append(t)
        return t

    def sb(self, es, name, shape, dtype):
        if self.dry:
            return Ten(Dummy())
        self.uid += 1
        return Ten(es.enter_context(self.nc.sbuf_tensor("sb%d_%s" % (self.uid, name), list(shape), dtype)))

    def ps(self, es, name, shape, dtype):
        if self.dry:
            return Ten(Dummy())
        self.uid += 1
        return Ten(es.enter_context(self.nc.psum_tensor("ps%d_%s" % (self.uid, name), list(shape), dtype)))

    def _wait(self, eng, key, ev):
        if ev[0] == 'c':
            if self.seen[eng].get(key, 0) >= ev[2]:
                return
            self.seen[eng][key] = ev[2]
            if self.dry:
                self.need.add(ev[3])
            else:
                assert ev[4] is not None
                self.h[eng].wait_ge(self.sem[ev[1]], ev[4])
                self.nwait += 1
        else:
            if self.seen[eng].get(key, 0) >= ev[2]:
                return
            self.seen[eng][key] = ev[2]
            if not self.dry:
                self.h[eng].wait_ge(self.dsem[ev[1]], ev[2])
                self.nwait += 1

    def op(self, eng, reads, writes, fn, dma=False, slot=None):
        deps = []
        for b in reads:
            for k, ev in b.w.items():
                deps.append((k, ev, 'raw'))
        for b in writes:
            if not b.multi:
                for k, ev in b.w.items():
                    deps.append((k, ev, 'waw'))
            for k, ev in b.r.items():
                deps.append((k, ev, 'war'))
        for k, ev, kind in deps:
            if (not dma) and ev[0] == 'c' and ev[1] == eng and kind != 'raw':
                continue
            self._wait(eng, k, ev)
        opidx = self.opi
        self.opi += 1
        if dma:
            if slot is not None:
                idx = self.ndma + slot
            else:
                idx = self.drr
                self.drr = (self.drr + 1) % self.ndma
                if self.dval[idx] > 0:
                    self._wait(eng, ('d', idx), ('d', idx, self.dval[idx]))
            self.dval[idx] += 16
            ev = ('d', idx, self.dval[idx])
            key = ('d', idx)
            if not self.dry:
                fn().then_inc(self.dsem[idx], 16)
        else:
            self.seq[eng] += 1
            sigval = None
            if not self.dry:
                inst = fn()
                if opidx in self.need:
                    self.sig[eng] += 1
                    sigval = self.sig[eng]
                    inst.then_inc(self.sem[eng], 1)
            ev = ('c', eng, self.seq[eng], opidx, sigval)
            key = eng
            self.last[eng] = ev
        for b in reads:
            b.r[key] = ev
        for b in writes:
            if b.multi:
                b.w[key] = ev
            else:
                b.w = {key: ev}
                b.r = {}
        return ev

    def barrier(self):
        for f in self.ENG:
            for e in self.ENG:
                if e != f and e in self.last:
                    self._wait(f, e, self.last[e])
            for idx in range(self.ndma + 24):
                if self.dval[idx] > 0:
                    self._wait(f, ('d', idx), ('d', idx, self.dval[idx]))

    def finish(self):
        for idx in range(self.ndma + 24):
            if self.dval[idx] > 0:
                self._wait('sp', ('d', idx), ('d', idx, self.dval[idx]))

    def dma(self, out, in_, reads, writes, q='sp', **kw):
        return self.op(q, reads, writes, lambda: self.h[q].dma_start(out=out, in_=in_, **kw), dma=True)

    def mm(self, out, lhsT, rhs, start, stop, reads, writes, pbase=0):
        if getattr(self, 'last_pb', 0) != pbase and 'pe' in self.last:
            self._wait('pe', 'pe_ser', self.last['pe'])
        self.last_pb = pbase
        return self.op('pe', reads, writes,
                       lambda: self.nc.tensor.matmul(out, lhsT=lhsT, rhs=rhs, start=start, stop=stop))

    def tr(self, out, in_, ident, reads, writes):
        return self.op('pe', reads, writes, lambda: self.nc.tensor.transpose(out, in_, ident))

    def act(self, out, in_, func, reads, writes, **kw):
        return self.op('act', reads, writes, lambda: self.nc.scalar.activation(out=out, in_=in_, func=func, **kw))

    def v(self, eng, name, reads, writes, *a, **kw):
        return self.op(eng, reads, writes, lambda: getattr(self.h[eng], name)(*a, **kw))


def bcast_rows(ap, n):
    return ap.to_broadcast([n] + list(ap.shape[1:]))


def make_consts():
    c = {}
    return make_consts_B(_make_consts_A(c))


def _make_consts_A(c):
    c['ident_bf'] = np.eye(128, dtype=np.float32).astype(ml_dtypes.bfloat16)
    c['ident_f'] = np.eye(128, dtype=np.float32)
    p = np.arange(128)[:, None]
    f = np.arange(128)[None, :]
    c['tri_le'] = (p <= f).astype(np.float32)
    c['tri_ge'] = (p >= f).astype(np.float32)
    pos = np.arange(TS)
    rows, cols = pos // 64, pos % 64
    def table(half):
        fr = 10000.0 ** (-np.arange(half, dtype=np.float32) / half)
        ar = rows[:, None].astype(np.float32) * fr[None, :]
        ac = cols[:, None].astype(np.float32) * fr[None, :]
        cs = np.concatenate([np.cos(ar), np.cos(ac)], axis=1)
        sn = np.concatenate([np.sin(ar), np.sin(ac)], axis=1)
        return np.concatenate([cs, sn], axis=1).astype(np.float32)
    c['rope16'] = table(16)
    c['rope8'] = table(8)
    return c


CONST_SPECS = {
    'ident_bf': ([128, 128], BF16), 'ident_f': ([128, 128], F32),
    'tri_le': ([128, 128], F32), 'tri_ge': ([128, 128], F32),
    'rope16': ([TS, 64], F32), 'rope8': ([TS, 32], F32),
}

IN_SPECS = {
    'xp': ([TP, D], F32), 'xs': ([TS, D], F32),
    'cnak': ([DEPTH, PAST, 256], F32), 'cnav': ([DEPTH, PAST, 256], F32),
    'cswk': ([DEPTH, PAST, 128], F32), 'cswv': ([DEPTH, PAST, 128], F32),
    'cckv': ([DEPTH, PAST, 128], F32), 'ckr': ([DEPTH, PAST, 32], F32),
    'stC': ([DEPTH, 8, 64, 64], F32), 'stn': ([DEPTH, 8, 64], F32), 'stm': ([DEPTH, 8], F32),
    'cvec': ([128, 16], F32),
    'w_mod': ([DEPTH, D, 6 * D], F32), 'b_mod': ([DEPTH, 6 * D], F32),
    'norm1_g': ([DEPTH, D], F32), 'norm2_g': ([DEPTH, D], F32),
    'w_in': ([DEPTH, D, INW], F32), 'b_in': ([DEPTH, INW], F32),
    'na_rpb': ([DEPTH, 60, 31], F32), 'sw_sink': ([DEPTH, 4], F32),
    'mla_q_norm': ([DEPTH, 192], F32), 'w_uq': ([DEPTH, 192, 384], F32),
    'mla_kv_norm': ([DEPTH, 128], F32), 'w_uk': ([DEPTH, 128, 256], F32), 'w_uv': ([DEPTH, 128, 256], F32),
    'w_branch': ([DEPTH, 1024, D], F32), 'w_out': ([DEPTH, D, D], F32),
    'peer_wq': ([DEPTH, D, D], F32), 'peer_keys': ([DEPTH, 16, 128, 64], F32),
    'peer_u': ([DEPTH * 16384, D], F32), 'peer_v': ([DEPTH * 16384, D], F32),
    'final_norm_g': ([1, D], F32),
}

OUT_SPECS = {
    'y_p': ([TP, D], F32), 'y_s': ([TS, D], F32),
    'o_nak': ([NPSEQ, DEPTH, PSEQ, 256], F32), 'o_nav': ([NPSEQ, DEPTH, PSEQ, 256], F32),
    'o_C': ([NPSEQ, DEPTH, 8, 64, 64], F32), 'o_n': ([NPSEQ, DEPTH, 8, 64], F32), 'o_m': ([NPSEQ, DEPTH, 8], F32),
    'o_swk': ([NPSEQ, DEPTH, PSEQ, 128], F32), 'o_swv': ([NPSEQ, DEPTH, PSEQ, 128], F32),
    'o_ckv': ([NPSEQ, DEPTH, PSEQ, 128], F32), 'o_kr': ([NPSEQ, DEPTH, PSEQ, 32], F32),
}

def scratch_specs():
    sp = {}
    for s, T in (('p', TP), ('s', TS)):
        sp['xres_' + s] = ([T, D], F32)
        sp['xmid_' + s] = ([T, D], F32)
        sp['featT_' + s] = ([12, 128, T], BF16)
        sp['m64T_' + s] = ([8, 64, T], BF16)
        sp['m32T_' + s] = ([5, 32, T], BF16)
        sp['vaug_' + s] = ([T, 14 * 65], BF16)
        sp['mlx_' + s] = ([T, 528], F32)
        sp['oT_' + s] = ([8, 128, T], BF16)
    sp['modrow'] = ([DEPTH, 2, 6 * D], F32)
    sp['pad2'] = ([60, 128], F32)
    return sp


class Prog:
    def __init__(self, nc, depth=DEPTH, phases=('mod', 'A', 'B', 'C'), dbg=(), skip=()):
        self.skip = skip
        self.nc = nc
        self.depth = depth
        self.phases = phases
        self.K = Builder(nc)
        K = self.K
        self.I = {n: K.dr(n, s, d, kind="ExternalInput") for n, (s, d) in IN_SPECS.items()}
        self.C = {n: K.dr(n, s, d, kind="ExternalInput") for n, (s, d) in CONST_SPECS.items()}
        self.O = {n: K.dr(n, s, d, kind="ExternalOutput") for n, (s, d) in OUT_SPECS.items()}
        self.S = {}
        for n, (s, d) in scratch_specs().items():
            self.S[n] = K.dr(n, s, d, kind=("ExternalOutput" if n in dbg else "Internal"))
        self.T = {'p': TP, 's': TS}

    def build(self):
        self.K.run(self.main)

    def main(self, K):
        with ExitStack() as es:
            self.ident_bf = K.sb(es, "ident_bf", [128, 128], BF16)
            self.ident_f = K.sb(es, "ident_f", [128, 128], F32)
            self.ones_f = K.sb(es, "ones_f", [128, 128], F32)
            K.dma(self.ident_bf[:], self.C['ident_bf'][:, :], [self.C['ident_bf']], [self.ident_bf])
            K.dma(self.ident_f[:], self.C['ident_f'][:, :], [self.C['ident_f']], [self.ident_f])
            K.v('dve', 'memset', [], [self.ones_f], self.ones_f[:], 1.0)
            for l in range(self.depth):
                if 'mod' in self.phases:
                    self.phase_mod(l)
                    K.barrier()
                if 'A' in self.phases:
                    self.phase_A(l)
                    K.barrier()
                if 'B' in self.phases:
                    self.phase_B(l)
                    K.barrier()
                if 'C' in self.phases:
                    self.phase_C(l)
                    K.barrier()
            K.barrier()
            K.finish()

    def phase_mod(self, l):
        K, I = self.K, self.I
        with ExitStack() as es:
            cv = K.sb(es, "cv", [128, 16], F32)
            sm = K.sb(es, "sm", [128, 16], F32)
            bm = K.sb(es, "bm", [1, 6 * D], F32)
            mrow = K.sb(es, "mrow", [2, 6 * D], F32)
            wch = [K.sb(es, "wch%d" % i, [128, 8, 512], F32) for i in range(2)]
            pm = [K.ps(es, "pm%d" % i, [128, 512], F32) for i in range(2)]
            K.dma(cv[:], I['cvec'][:, :], [I['cvec']], [cv])
            K.dma(bm[:], I['b_mod'][l:l + 1, :], [I['b_mod']], [bm])
            K.act(sm[:], cv[:], AF.Silu, [cv], [sm])
            for j in range(12):
                w = wch[j % 2]
                p = pm[j % 2]
                K.dma(w[:], I['w_mod'][l, :, j * 512:(j + 1) * 512].rearrange("(k p) n -> p k n", p=128),
                      [I['w_mod']], [w])
                for k in range(8):
                    K.mm(p[0:2, :], sm[:, k:16:8], w[:, k, :], k == 0, False, [sm, w], [p])
                K.mm(p[0:2, :], self.ones_f[0:1, 0:2], bm[0:1, j * 512:(j + 1) * 512], False, True,
                     [self.ones_f, bm], [p])
                K.act(mrow[:, j * 512:(j + 1) * 512], p[0:2, :], AF.Copy, [p], [mrow])
            K.dma(self.S['modrow'][l, :, :], mrow[:], [mrow], [self.S['modrow']])

    def load_mod(self, es, l, s, which, name):
        K, I = self.K, self.I
        si = 0 if s == 'p' else 1
        out = {}
        for wi in which:
            t = K.sb(es, "%s_%s_%d" % (name, s, wi), [128, D], F32)
            src = self.S['modrow'][l, si:si + 1, wi * D:(wi + 1) * D]
            K.dma(t[:], bcast_rows(src, 128), [self.S['modrow']], [t])
            if wi in (1, 4):
                gsrc = I['norm1_g' if wi == 1 else 'norm2_g']
                g = K.sb(es, "%s_g_%s_%d" % (name, s, wi), [128, D], F32)
                K.dma(g[:], bcast_rows(gsrc[l:l + 1, :], 128), [gsrc], [g])
                K.v('dve', 'scalar_tensor_tensor', [t, g], [t], out=t[:], in0=t[:], scalar=1.0, in1=g[:],
                    op0=ALU.add, op1=ALU.mult)
            out[wi] = t
        return out

    def rstd(self, ss, c, n):
        K = self.K
        K.v('dve', 'tensor_scalar', [ss], [ss], out=ss[:, c:c + 1], in0=ss[:, c:c + 1], scalar1=1.0 / n, scalar2=EPS,
            op0=ALU.mult, op1=ALU.add)
        K.act(ss[:, c:c + 1], ss[:, c:c + 1], AF.Sqrt, [ss], [ss])
        K.v('dve', 'reciprocal', [ss], [ss], out=ss[:, c:c + 1], in_=ss[:, c:c + 1])

    def norm_mod(self, x, A, B, h32, hbf, ss, junk):
        K = self.K
        K.act(junk[:], x[:], AF.Square, [x], [junk, ss], accum_out=ss[:, 0:1])
        self.rstd(ss, 0, D)
        K.v('dve', 'scalar_tensor_tensor', [x, ss, A], [h32], out=h32[:], in0=x[:], scalar=ss[:, 0:1], in1=A[:],
            op0=ALU.mult, op1=ALU.mult)
        if hbf is not None:
            K.v('dve', 'tensor_tensor', [h32, B], [hbf], out=hbf[:], in0=h32[:], in1=B[:], op=ALU.add)
        else:
            K.v('dve', 'tensor_tensor', [h32, B], [h32], out=h32[:], in0=h32[:], in1=B[:], op=ALU.add)

    def load_w_bf(self, dst, src_ap, src_t, stage, rows, cols, kch):
        K = self.K
        K.dma(stage[0:rows, 0:kch, 0:cols], src_ap, [src_t], [stage])
        K.v('pool', 'tensor_copy', [stage], [dst], out=dst, in_=stage[0:rows, 0:kch, 0:cols])

    def phase_A(self, l):
        K, I, S, O, C = self.K, self.I, self.S, self.O, self.C
        with ExitStack() as es:
            winb = K.sb(es, "winb", [128, 8, NG], BF16)
            stage = K.sb(es, "stageA", [128, 8, 512], F32)
            binb = K.sb(es, "binb", [128, NG], F32)
            wuq = K.sb(es, "wuq", [128, 2, 384], BF16)
            wuk = K.sb(es, "wuk", [128, 256], BF16)
            wuv = K.sb(es, "wuv", [128, 256], BF16)
            qng = K.sb(es, "qng", [128, 192], F32)
            kvng = K.sb(es, "kvng", [128, 128], F32)
            x = K.sb(es, "xA", [128, D], F32)
            h32 = K.sb(es, "h32A", [128, D], F32)
            hbf = K.sb(es, "hbfA", [128, D], BF16)
            hT = K.sb(es, "hTA", [128, 8, 128], BF16)
            ss = K.sb(es, "ssA", [128, 4], F32)
            prow = K.sb(es, "prow", [128, NG], F32)
            pb = K.sb(es, "pbA", [128, 12, 128], BF16)
            fst = K.sb(es, "fstA", [128, 12, 128], BF16)
            vaug = K.sb(es, "vaugA", [128, 14, 65], BF16)
            mlx = K.sb(es, "mlxA", [128, 528], F32)
            rt16 = K.sb(es, "rt16", [128, 64], F32)
            rt8 = K.sb(es, "rt8", [128, 32], F32)
            tmp = [K.sb(es, "ropet%d" % i, [128, 192], F32) for i in range(4)]
            qlat = K.sb(es, "qlat", [128, 192], BF16)
            qlT = K.sb(es, "qlT", [128, 2, 128], BF16)
            qm = K.sb(es, "qm", [128, 4, 96], F32)
            qnr = K.sb(es, "qnr", [128, 384], BF16)
            ckn = K.sb(es, "ckn", [128, 128], F32)
            cknb = K.sb(es, "cknb", [128, 128], BF16)
            ckT = K.sb(es, "ckT", [128, 128], BF16)
            krb = K.sb(es, "krb", [128, 32], BF16)
            m64 = K.sb(es, "m64", [64, 8, 128], BF16)
            m32 = K.sb(es, "m32", [32, 5, 128], BF16)
            pT = K.ps(es, "pT", [128, 8, 128], BF16)
            pP = [K.ps(es, "pP%d" % i, [128, 512], F32) for i in range(2)]
            pF = [K.ps(es, "pF%d" % i, [128, 8, 128], BF16) for i in range(2)]
            pM = K.ps(es, "pM", [128, 8, 128], BF16)
            pQ = K.ps(es, "pQ", [128, 512], F32)
            pK = K.ps(es, "pK", [128, 512], F32)

            for j in range(6):
                c0 = j * 512
                w = min(512, NG - c0)
                K.dma(stage[:, :, 0:w], I['w_in'][l, :, c0:c0 + w].rearrange("(k p) n -> p k n", p=128),
                      [I['w_in']], [stage])
                K.v('pool', 'tensor_copy', [stage], [winb], out=winb[:, :, c0:c0 + w], in_=stage[:, :, 0:w])
            K.dma(binb[:], bcast_rows(I['b_in'][l:l + 1, 0:NG], 128), [I['b_in']], [binb])
            K.dma(stage[:, 0, 0:384], I['w_uq'][l, 0:128, :], [I['w_uq']], [stage])
            K.dma(stage[0:64, 1, 0:384], I['w_uq'][l, 128:192, :], [I['w_uq']], [stage])
            K.v('pool', 'tensor_copy', [stage], [wuq], out=wuq[:, 0, :], in_=stage[:, 0, 0:384])
            K.v('pool', 'tensor_copy', [stage], [wuq], out=wuq[0:64, 1, :], in_=stage[0:64, 1, 0:384])
            K.dma(stage[:, 2, 0:256], I['w_uk'][l, :, :], [I['w_uk']], [stage])
            K.dma(stage[:, 3, 0:256], I['w_uv'][l, :, :], [I['w_uv']], [stage])
            K.v('pool', 'tensor_copy', [stage], [wuk], out=wuk[:], in_=stage[:, 2, 0:256])
            K.v('pool', 'tensor_copy', [stage], [wuv], out=wuv[:], in_=stage[:, 3, 0:256])
            K.dma(qng[:], bcast_rows(I['mla_q_norm'][l:l + 1, :], 128), [I['mla_q_norm']], [qng])
            K.dma(kvng[:], bcast_rows(I['mla_kv_norm'][l:l + 1, :], 128), [I['mla_kv_norm']], [kvng])
            K.v('dve', 'memset', [], [vaug], vaug[:], 1.0)
            mods = {s: self.load_mod(es, l, s, [0, 1], "mA") for s in ('p', 's')}

            for s in ('p', 's'):
                T = self.T[s]
                A1, B1 = mods[s][1], mods[s][0]
                xsrc = (I['xp'] if s == 'p' else I['xs']) if l == 0 else S['xres_' + s]
                for t in range(T // 128):
                    t0 = t * 128
                    K.dma(x[:], xsrc[t0:t0 + 128, :], [xsrc], [x])
                    if s == 's':
                        K.dma(rt16[:], C['rope16'][t0:t0 + 128, :], [C['rope16']], [rt16])
                        K.dma(rt8[:], C['rope8'][t0:t0 + 128, :], [C['rope8']], [rt8])
                    self.norm_mod(x, A1, B1, h32, hbf, ss, h32)
                    for k in range(8):
                        K.tr(pT[:, k, :], hbf[:, k * 128:(k + 1) * 128], self.ident_bf[:], [hbf, self.ident_bf], [pT])
                    K.act(hT[:], pT[:], AF.Copy, [pT], [hT])
                    for j in range(6):
                        c0 = j * 512
                        w = min(512, NG - c0)
                        p = pP[j % 2]
                        for k in range(8):
                            K.mm(p[:, 0:w], hT[:, k, :], winb[:, k, c0:c0 + w], k == 0, k == 7, [hT, winb], [p])
                        K.v('dve', 'tensor_tensor', [p, binb], [prow], out=prow[:, c0:c0 + w], in0=p[:, 0:w],
                            in1=binb[:, c0:c0 + w], op=ALU.add)
                    if s == 'p':
                        sq, pos = t // 2, (t % 2) * 128
                        for nm, c0, w in (('o_nak', O_NAK, 256), ('o_nav', O_NAV, 256), ('o_swk', O_SWK, 128),
                                          ('o_swv', O_SWV, 128), ('o_kr', O_KR, 32)):
                            K.dma(O[nm][sq, l, pos:pos + 128, :], prow[:, c0:c0 + w], [prow], [O[nm]])
                    if s == 's':
                        X = prow[:, O_SWQ:O_SWQ + 384].rearrange("p (h a b c) -> p h a b c", h=6, a=2, b=2)
                        xa, xb = X[:, :, :, 0, :], X[:, :, :, 1, :]
                        cs = rt16[:, 0:32].rearrange("p (a c) -> p a c", a=2).unsqueeze(1).to_broadcast([128, 6, 2, 16])
                        sn = rt16[:, 32:64].rearrange("p (a c) -> p a c", a=2).unsqueeze(1).to_broadcast([128, 6, 2, 16])
                        tv = [tt[:, 0:192].rearrange("p (h a c) -> p h a c", h=6, a=2) for tt in tmp]
                        self.rope(prow, xa, xb, cs, sn, tv, tmp, [rt16])
                    for g, c0, nh in ((0, O_NAV, 4), (4, O_MLV, 4), (8, O_SWV, 2)):
                        K.v('pool', 'tensor_copy', [prow], [vaug], out=vaug[:, g:g + nh, 0:64],
                            in_=prow[:, c0:c0 + nh * 64].rearrange("p (h d) -> p h d", h=nh))
                    K.act(mlx[:, 0:256], prow[:, O_MLK:O_MLK + 256], AF.Copy, [prow], [mlx], scale=0.125)
                    K.act(mlx[:, 256:528], prow[:, O_MLO:O_MLO + 272], AF.Copy, [prow], [mlx])
                    K.dma(S['mlx_' + s][t0:t0 + 128, :], mlx[:], [mlx], [S['mlx_' + s]])
                    K.act(pb[:, 0:4, :], prow[:, 0:512].rearrange("p (g c) -> p g c", g=4), AF.Copy, [prow], [pb])
                    K.act(pb[:, 4:6, :], prow[:, O_MLQ:O_MLQ + 256].rearrange("p (g c) -> p g c", g=2), AF.Copy,
                          [prow], [pb])
                    K.act(pb[:, 6:8, :], mlx[:, 0:256].rearrange("p (g c) -> p g c", g=2), AF.Copy, [mlx], [pb])
                    K.act(pb[:, 8:11, :], prow[:, O_SWQ:O_SWQ + 384].rearrange("p (g c) -> p g c", g=3), AF.Copy,
                          [prow], [pb])
                    K.act(pb[:, 11, 0:64], prow[:, O_SWK + 64:O_SWK + 128], AF.Copy, [prow], [pb])
                    K.act(pb[:, 11, 64:128], prow[:, O_SWK:O_SWK + 64], AF.Copy, [prow], [pb])
                    for g in range(12):
                        pf = pF[0] if g < 8 else pF[1]
                        K.tr(pf[:, g % 8, :], pb[:, g, :], self.ident_bf[:], [pb, self.ident_bf], [pf])
                    K.v('dve', 'tensor_copy', [pF[0]], [fst], out=fst[:, 0:8, :], in_=pF[0][:])
                    K.v('dve', 'tensor_copy', [pF[1]], [fst], out=fst[:, 8:12, :], in_=pF[1][:, 0:4, :])
                    K.dma(S['featT_' + s][:, :, t0:t0 + 128].rearrange("g p t -> p g t"), fst[:], [fst],
                          [S['featT_' + s]])
                    K.act(tmp[0][:, 0:192], prow[:, O_CQ:O_CQ + 192], AF.Square, [prow], [tmp[0], ss],
                          accum_out=ss[:, 1:2])
                    self.rstd(ss, 1, 192)
                    K.v('dve', 'scalar_tensor_tensor', [prow, ss, qng], [qlat], out=qlat[:],
                        in0=prow[:, O_CQ:O_CQ + 192], scalar=ss[:, 1:2], in1=qng[:], op0=ALU.mult, op1=ALU.mult)
                    K.tr(pM[:, 0, :], qlat[:, 0:128], self.ident_bf[:], [qlat, self.ident_bf], [pM])
                    K.tr(pM[0:64, 1, :], qlat[:, 128:192], self.ident_bf[:], [qlat, self.ident_bf], [pM])
                    K.act(tmp[0][:, 0:128], prow[:, O_CKV:O_CKV + 128], AF.Square, [prow], [tmp[0], ss],
                          accum_out=ss[:, 2:3])
                    self.rstd(ss, 2, 128)
                    K.v('dve', 'scalar_tensor_tensor', [prow, ss, kvng], [ckn], out=ckn[:],
                        in0=prow[:, O_CKV:O_CKV + 128], scalar=ss[:, 2:3], in1=kvng[:], op0=ALU.mult, op1=ALU.mult)
                    if s == 'p':
                        K.dma(O['o_ckv'][sq, l, pos:pos + 128, :], ckn[:], [ckn], [O['o_ckv']])
                    K.act(cknb[:], ckn[:], AF.Copy, [ckn], [cknb])
                    K.tr(pM[:, 2, :], cknb[:], self.ident_bf[:], [cknb, self.ident_bf], [pM])
                    K.v('dve', 'tensor_copy', [pM], [qlT], out=qlT[:, 0, :], in_=pM[:, 0, :])
                    K.v('dve', 'tensor_copy', [pM], [qlT], out=qlT[0:64, 1, :], in_=pM[0:64, 1, :])
                    K.v('dve', 'tensor_copy', [pM], [ckT], out=ckT[:], in_=pM[:, 2, :])
                    K.mm(pQ[:, 0:384], qlT[:, 0, :], wuq[:, 0, :], True, False, [qlT, wuq], [pQ])
                    K.mm(pQ[:, 0:384], qlT[0:64, 1, :], wuq[0:64, 1, :], False, True, [qlT, wuq], [pQ])
                    K.act(qm[:], pQ[:, 0:384].rearrange("p (h c) -> p h c", h=4), AF.Copy, [pQ], [qm])
                    if s == 's':
                        X = qm[:, :, 64:96].rearrange("p h (a b c) -> p h a b c", a=2, b=2)
                        xa, xb = X[:, :, :, 0, :], X[:, :, :, 1, :]
                        cs = rt8[:, 0:16].rearrange("p (a c) -> p a c", a=2).unsqueeze(1).to_broadcast([128, 4, 2, 8])
                        sn = rt8[:, 16:32].rearrange("p (a c) -> p a c", a=2).unsqueeze(1).to_broadcast([128, 4, 2, 8])
                        tv = [tt[:, 0:64].rearrange("p (h a c) -> p h a c", h=4, a=2) for tt in tmp]
                        self.rope(qm, xa, xb, cs, sn, tv, tmp, [rt8])
                        X = prow[:, O_KR:O_KR + 32].rearrange("p (h a b c) -> p h a b c", h=1, a=2, b=2)
                        xa, xb = X[:, :, :, 0, :], X[:, :, :, 1, :]
                        cs1 = rt8[:, 0:16].rearrange("p (a c) -> p a c", a=2).unsqueeze(1)
                        sn1 = rt8[:, 16:32].rearrange("p (a c) -> p a c", a=2).unsqueeze(1)
                        tv = [tt[:, 0:16].rearrange("p (h a c) -> p h a c", h=1, a=2) for tt in tmp]
                        self.rope(prow, xa, xb, cs1, sn1, tv, tmp, [rt8])
                    K.act(qnr[:, 0:256].rearrange("p (h c) -> p h c", h=4), qm[:, :, 0:64], AF.Copy, [qm], [qnr])
                    K.act(qnr[:, 256:384].rearrange("p (h c) -> p h c", h=4), qm[:, :, 64:96], AF.Copy, [qm], [qnr])
                    K.act(krb[:], prow[:, O_KR:O_KR + 32], AF.Copy, [prow], [krb])
                    for hh in range(4):
                        K.tr(pF[0][0:64, hh, :], qnr[:, hh * 64:(hh + 1) * 64], self.ident_bf[:],
                             [qnr, self.ident_bf], [pF[0]])
                        K.tr(pF[1][0:32, hh, :], qnr[:, 256 + hh * 32:256 + (hh + 1) * 32], self.ident_bf[:],
                             [qnr, self.ident_bf], [pF[1]])
                    K.tr(pF[1][0:32, 4, :], krb[:], self.ident_bf[:], [krb, self.ident_bf], [pF[1]])
                    K.v('dve', 'tensor_copy', [pF[0]], [m64], out=m64[:, 0:4, :], in_=pF[0][0:64, 0:4, :])
                    K.v('dve', 'tensor_copy', [pF[1]], [m32], out=m32[:], in_=pF[1][0:32, 0:5, :])
                    for hh in range(4):
                        K.mm(pK[0:64, hh * 128:(hh + 1) * 128], wuk[:, hh * 64:(hh + 1) * 64], ckT[:], True, True,
                             [wuk, ckT], [pK])
                    K.act(m64[:, 4:8, :], pK[0:64, :].rearrange("p (h t) -> p h t", h=4), AF.Copy, [pK], [m64])
                    K.mm(pQ[:, 0:256], ckT[:], wuv[:], True, True, [ckT, wuv], [pQ])
                    K.v('dve', 'tensor_copy', [pQ], [vaug], out=vaug[:, 10:14, 0:64],
                        in_=pQ[:, 0:256].rearrange("p (h d) -> p h d", h=4))
                    K.dma(S['m64T_' + s][:, :, t0:t0 + 128].rearrange("g p t -> p g t"), m64[:], [m64],
                          [S['m64T_' + s]])
                    K.dma(S['m32T_' + s][:, :, t0:t0 + 128].rearrange("g p t -> p g t"), m32[:], [m32],
                          [S['m32T_' + s]])
                    K.dma(S['vaug_' + s][t0:t0 + 128, :], vaug[:].rearrange("p g c -> p (g c)"), [vaug],
                          [S['vaug_' + s]])

    def rope(self, X, xa, xb, cs, sn, tv, tmp, tabs):
        K = self.K
        K.v('dve', 'tensor_tensor', [X] + tabs, [tmp[0]], out=tv[0], in0=xa, in1=cs, op=ALU.mult)
        K.v('dve', 'tensor_tensor', [X] + tabs, [tmp[1]], out=tv[1], in0=xb, in1=sn, op=ALU.mult)
        K.v('dve', 'tensor_tensor', [X] + tabs, [tmp[2]], out=tv[2], in0=xa, in1=sn, op=ALU.mult)
        K.v('dve', 'tensor_tensor', [X] + tabs, [tmp[3]], out=tv[3], in0=xb, in1=cs, op=ALU.mult)
        K.v('dve', 'tensor_tensor', [tmp[0], tmp[1]], [X], out=xa, in0=tv[0], in1=tv[1], op=ALU.subtract)
        K.v('dve', 'tensor_tensor', [tmp[2], tmp[3]], [X], out=xb, in0=tv[2], in1=tv[3], op=ALU.add)


def prep_core_inputs(inp, core, consts):
    b = core
    f = np.ascontiguousarray
    m = {}
    m['xp'] = f(inp['x_prompt'][4 * b:4 * b + 4].reshape(TP, D))
    m['xs'] = f(inp['x_sample'][b])
    m['cnak'] = f(inp['cache_na_k'][b].reshape(DEPTH, PAST, 256))
    m['cnav'] = f(inp['cache_na_v'][b].reshape(DEPTH, PAST, 256))
    m['cswk'] = f(inp['cache_swa_k'][b].reshape(DEPTH, PAST, 128))
    m['cswv'] = f(inp['cache_swa_v'][b].reshape(DEPTH, PAST, 128))
    m['cckv'] = f(inp['cache_mla_ckv'][b])
    m['ckr'] = f(inp['cache_mla_krope'][b])
    m['stC'] = f(inp['state_mlstm_C'][b].reshape(DEPTH, 8, 64, 64))
    m['stn'] = f(inp['state_mlstm_n'][b].reshape(DEPTH, 8, 64))
    m['stm'] = f(inp['state_mlstm_m'][b].reshape(DEPTH, 8))
    cv = np.concatenate([inp['c_ctx'].reshape(8, 128).T, inp['c'][b].reshape(8, 128).T], axis=1)
    m['cvec'] = f(cv.astype(np.float32))
    for n in ('w_mod', 'b_mod', 'norm1_g', 'norm2_g', 'w_in', 'b_in', 'sw_sink', 'mla_q_norm', 'w_uq',
              'mla_kv_norm', 'w_uk', 'w_uv', 'w_out', 'peer_wq'):
        m[n] = inp[n]
    m['peer_u'] = inp['peer_u'].reshape(DEPTH * 16384, D)
    m['peer_v'] = inp['peer_v'].reshape(DEPTH * 16384, D)
    m['na_rpb'] = inp['na_rpb'].reshape(DEPTH, 60, 31)
    m['w_branch'] = inp['w_branch'].reshape(DEPTH, 1024, D)
    m['peer_keys'] = inp['peer_keys'].reshape(DEPTH, 16, 128, 64)
    m['final_norm_g'] = inp['final_norm_g'].reshape(1, D)
    m.update(consts)
    return m


def make_consts_B(c):
    kcp = np.arange(64)[:, None]
    qc = np.arange(64)[None, :]
    kc = 63 - kcp
    cs = np.clip(qc - 8, 0, 48)
    ok = ((kc >= cs) & (kc < cs + 16)).astype(np.float32)
    c['na_ok8'] = (ok * 8.0).astype(np.float32)
    c['na_neg'] = ((ok - 1.0) * 240000.0).astype(np.float32)
    jlo = np.zeros((64, 128), np.float32)
    jhi = np.zeros((64, 128), np.float32)
    for k in range(64):
        jlo[k, 63 - k] = 1.0
        jhi[k, 127 - k] = 1.0
    c['jlo'] = jlo.astype(ml_dtypes.bfloat16)
    c['jhi'] = jhi.astype(ml_dtypes.bfloat16)
    mge = (c['tri_ge'] - 1.0) * 240000.0
    mle = (c['tri_le'] - 1.0) * 240000.0
    c['mb_ge2'] = np.concatenate([mge, mge], axis=1).astype(ml_dtypes.bfloat16)
    c['mb_le2'] = np.concatenate([mle, mle], axis=1).astype(ml_dtypes.bfloat16)
    return c


CONST_SPECS.update({
    'na_ok8': ([64, 64], F32), 'na_neg': ([64, 64], F32), 'jlo': ([64, 128], BF16), 'jhi': ([64, 128], BF16),
    'mb_ge2': ([128, 256], BF16), 'mb_le2': ([128, 256], BF16),
})


def _load_tok(self, dst, dst_ap_fn, src, col0, col1, tok0, nchunks, step=8):
    K = self.K
    for c0 in range(0, nchunks, step):
        n = min(step, nchunks - c0)
        K.dma(dst_ap_fn(c0, n), src[tok0 + c0 * 128:tok0 + (c0 + n) * 128, col0:col1].rearrange("(c p) f -> p c f", p=128),
              [src], [dst])


def _pipeline(steps, s_fn, rest_fn):
    n = len(steps)
    if n == 0:
        return
    s_fn(steps[0], 0)
    for i in range(n):
        if i + 1 < n:
            s_fn(steps[i + 1], (i + 1) % 2)
        rest_fn(steps[i], i % 2)


def _phase_B(self, l):
    K = self.K
    for nm in ('mix_na', 'mix_sw', 'mix_mla', 'mix_ml'):
        if '_' + nm in self.skip:
            continue
        getattr(self, nm)(l)
        K.barrier()


def _finish_heads(self, acc, nh, rec, ob, extra_den=None, npart=128):
    K = self.K
    rv = rec[0:npart, 0:nh]
    if extra_den is not None:
        K.v('dve', 'tensor_tensor', [acc[0], extra_den], [rec], out=rv, in0=acc[1][:, :, 64], in1=extra_den[0:npart, 0:nh],
            op=ALU.add)
        K.v('dve', 'reciprocal', [rec], [rec], out=rv, in_=rv)
    else:
        K.v('dve', 'reciprocal', [acc[0]], [rec], out=rv, in_=acc[1][:, :, 64])
    K.v('dve', 'tensor_tensor', [acc[0], rec], [ob[0]], out=ob[1], in0=acc[1][:, :, 0:64],
        in1=rv.unsqueeze(2).to_broadcast([npart, nh, 64]), op=ALU.mult)


def _mix_na(self, l):
    K, I, S, C = self.K, self.I, self.S, self.C
    sc = 0.125
    with ExitStack() as es:
        qk = K.sb(es, "naqk", [128, 4, PSEQ], BF16)
        va = K.sb(es, "nava", [128, 2, 260], BF16)
        E = [K.sb(es, "naE%d" % i, [128, 256], BF16) for i in range(2)]
        rec = K.sb(es, "narec", [128, 4], F32)
        ob = K.sb(es, "naob", [128, 256], BF16)
        ot = K.sb(es, "naot", [128, 2, 128], BF16)
        pS = [K.ps(es, "napS%d" % i, [128, 512], F32) for i in range(2)]
        pA = [K.ps(es, "napA%d" % i, [128, 512], F32) for i in range(2)]
        pO = K.ps(es, "napO", [128, 8, 128], BF16)
        for sq in range(NPSEQ):
            b0 = sq * PSEQ
            K.dma(qk[:], S['featT_p'][0:4, :, b0:b0 + PSEQ].rearrange("g p t -> p g t"), [S['featT_p']], [qk])
            K.dma(va[:], S['vaug_p'][b0:b0 + PSEQ, 0:260].rearrange("(c p) f -> p c f", p=128), [S['vaug_p']], [va])
            step = 0
            for h in range(4):
                g, hb = h // 2, (h % 2) * 64
                for c in range(2):
                    p = pS[step % 2]
                    e = E[step % 2]
                    step += 1
                    K.mm(p[:, 0:256], qk[hb:hb + 64, 2 + g, c * 128:(c + 1) * 128], qk[hb:hb + 64, g, :], True, True,
                         [qk], [p], pbase=hb)
                    K.act(e[:], p[:, 0:256], AF.Exp, [p], [e], scale=sc)
                    for j in range(2):
                        K.mm(pA[j][:, h * 65:(h + 1) * 65], e[:, j * 128:(j + 1) * 128], va[:, c, h * 65:(h + 1) * 65],
                             c == 0, c == 1, [e, va], [pA[j]])
            for j in range(2):
                accv = pA[j][:, 0:260].rearrange("p (h c) -> p h c", h=4)
                self.finish_heads((pA[j], accv), 4, rec, (ob, ob[:].rearrange("p (h c) -> p h c", h=4)))
                for g in range(2):
                    K.tr(pO[:, g, :], ob[:, g * 128:(g + 1) * 128], self.ident_bf[:], [ob, self.ident_bf], [pO])
                K.v('dve', 'tensor_copy', [pO], [ot], out=ot[:], in_=pO[:, 0:2, :])
                t0 = b0 + j * 128
                K.dma(S['oT_p'][0:2, :, t0:t0 + 128].rearrange("g p t -> p g t"), ot[:], [ot], [S['oT_p']])
    K.barrier()
    if 'na_s' in self.skip:
        return
    with ExitStack() as es:
        qk = K.sb(es, "nsqk", [128, 4, TS], BF16)
        VA = K.sb(es, "nsVA", [128, 32, 260], BF16)
        VB = K.sb(es, "nsVB", [128, 31, 260], BF16)
        ctok = K.sb(es, "nsctok", [128, 4, 256], F32)
        ctb = K.sb(es, "nsctb", [128, 4, 256], BF16)
        ckT = K.sb(es, "nsckT", [128, 2, 512], BF16)
        cva = K.sb(es, "nscva", [128, 4, 260], BF16)
        rp = K.sb(es, "nsrp", [60, 31], F32)
        rrev = K.sb(es, "nsrrev", [60, 128], F32)
        Tpp = K.sb(es, "nsTpp", [64, 60, 64], F32)
        ok8 = K.sb(es, "nsok8", [64, 64], F32)
        neg = K.sb(es, "nsneg", [64, 64], F32)
        BLK = K.sb(es, "nsBLK", [64, 15, 4, 64], BF16)
        jlo = K.sb(es, "nsjlo", [64, 128], BF16)
        jhi = K.sb(es, "nsjhi", [64, 128], BF16)
        E = [K.sb(es, "nsE%d" % i, [128, 256], BF16) for i in range(2)]
        rec = K.sb(es, "nsrec", [128, 4], F32)
        ob = K.sb(es, "nsob", [64, 256], BF16)
        ot = K.sb(es, "nsot", [128, 2, 128], BF16)
        pS = [K.ps(es, "nspS%d" % i, [128, 512], F32) for i in range(2)]
        pA = [K.ps(es, "nspA%d" % i, [128, 512], F32) for i in range(2)]
        pO = K.ps(es, "nspO", [128, 8, 128], BF16)
        pT = K.ps(es, "nspT", [128, 8, 128], BF16)
        K.dma(qk[:], S['featT_s'][0:4, :, :].rearrange("g p t -> p g t"), [S['featT_s']], [qk])
        self.load_tok(VA, lambda c0, n: VA[:, c0:c0 + n, :], S['vaug_s'], 0, 260, 0, 32)
        self.load_tok(VB, lambda c0, n: VB[:, c0:c0 + n, :], S['vaug_s'], 0, 260, 64, 31)
        K.dma(ctok[:], I['cnak'][l].rearrange("(c p) f -> p c f", p=128), [I['cnak']], [ctok])
        K.act(ctb[:], ctok[:], AF.Copy, [ctok], [ctb])
        for c in range(4):
            for g in range(2):
                K.tr(pT[:, c * 2 + g, :], ctb[:, c, g * 128:(g + 1) * 128], self.ident_bf[:], [ctb, self.ident_bf], [pT])
        K.v('dve', 'tensor_copy', [pT], [ckT], out=ckT[:].rearrange("p g (c t) -> p c g t", c=4),
            in_=pT[:].rearrange("p (c g) t -> p c g t", c=4))
        K.v('dve', 'memset', [], [cva], cva[:], 1.0)
        K.dma(ctok[:], I['cnav'][l].rearrange("(c p) f -> p c f", p=128), [I['cnav']], [ctok])
        K.v('dve', 'tensor_copy', [ctok], [cva], out=cva[:].rearrange("p c (h e) -> p c h e", h=4)[:, :, :, 0:64],
            in_=ctok[:].rearrange("p c (h e) -> p c h e", h=4))
        if 'na_s1' in self.skip:
            return
        K.dma(rp[:], I['na_rpb'][l], [I['na_rpb']], [rp])
        K.v('dve', 'memset', [], [rrev], rrev[:], 0.0)
        K.v('dve', 'tensor_copy', [rp], [rrev], out=rrev[:, 48:79], in_=rp[:, ::-1])
        K.dma(S['pad2'][:, :], rrev[:], [rrev], [S['pad2']])
        src = bass.AP(S['pad2'][:, :].tensor, 0, [[1, 64], [128, 60], [1, 64]])
        K.dma(Tpp[:], src, [S['pad2']], [Tpp])
        K.dma(ok8[:], C['na_ok8'][:, :], [C['na_ok8']], [ok8])
        K.dma(neg[:], C['na_neg'][:, :], [C['na_neg']], [neg])
        K.dma(jlo[:], C['jlo'][:, :], [C['jlo']], [jlo])
        K.dma(jhi[:], C['jhi'][:, :], [C['jhi']], [jhi])
        K.v('dve', 'tensor_tensor', [Tpp, ok8], [Tpp], out=Tpp[:], in0=Tpp[:],
            in1=ok8[:].unsqueeze(1).to_broadcast([64, 60, 64]), op=ALU.mult)
        K.v('dve', 'tensor_tensor', [Tpp, neg], [BLK], out=BLK[:].rearrange("p r h q -> p h r q"),
            in0=Tpp[:].rearrange("p (h r) q -> p h r q", h=4),
            in1=neg[:].unsqueeze(1).unsqueeze(1).to_broadcast([64, 4, 15, 64]), op=ALU.add)
        if 'na_s2' in self.skip:
            return
        steps = []
        for r in range(64):
            rs = min(max(r - 4, 0), 56)
            o = rs - r
            chunks = []
            for jj in range(4):
                k0 = rs * 64 + jj * 128
                if k0 % 128 == 0:
                    vv = VA[:, k0 // 128, :]
                else:
                    vv = VB[:, (k0 - 64) // 128, :]
                chunks.append(('loc', k0, vv, (2 * jj + o + 7, 2 * jj + 1 + o + 7)))
            for c in range(4):
                chunks.append(('ctx', c, cva[:, c, :], None))
            for ci, ch in enumerate(chunks):
                steps.append((r, ci, len(chunks)) + ch)

        def s_fn(st, b):
            r, ci, nch, kind, k0, vv, drs = st
            p = pS[b]
            for h in (0, 2, 1, 3):
                g, hb = h // 2, (h % 2) * 64
                if kind == 'loc':
                    kk = qk[hb:hb + 64, 2 + g, k0:k0 + 128]
                    rd = [qk]
                else:
                    kk = ckT[hb:hb + 64, g, k0 * 128:(k0 + 1) * 128]
                    rd = [qk, ckT]
                K.mm(p[:, h * 64:(h + 1) * 64], kk, qk[hb:hb + 64, g, r * 64:(r + 1) * 64], h == 0,
                     (kind == 'ctx'), rd, [p], pbase=hb)
            if kind == 'loc':
                K.mm(p[:, 0:256], jlo[:], BLK[:, drs[0], :, :].rearrange("p h q -> p (h q)"), False, False,
                     [jlo, BLK], [p])
                K.mm(p[:, 0:256], jhi[:], BLK[:, drs[1], :, :].rearrange("p h q -> p (h q)"), False, True,
                     [jhi, BLK], [p])

        def rest_fn(st, b):
            r, ci, nch, kind, k0, vv, drs = st
            p, e = pS[b], E[b]
            acc = pA[r % 2]
            K.act(e[:], p[:, 0:256], AF.Exp, [p], [e], scale=sc)
            rdv = [e, VA, VB, cva]
            for h in range(4):
                K.mm(acc[0:64, h * 65:(h + 1) * 65], e[:, h * 64:(h + 1) * 64], vv[:, h * 65:(h + 1) * 65],
                     ci == 0 and h == 0, ci == nch - 1, rdv, [acc])
            if ci == nch - 1:
                accv = acc[0:64, 0:260].rearrange("p (h c) -> p h c", h=4)
                self.finish_heads((acc, accv), 4, rec, (ob, ob[:].rearrange("p (h c) -> p h c", h=4)), npart=64)
                for g in range(2):
                    K.tr(pO[:, g, (r % 2) * 64:(r % 2) * 64 + 64], ob[:, g * 128:(g + 1) * 128], self.ident_bf[0:64, 0:64],
                         [ob, self.ident_bf], [pO])
                if r % 2 == 1:
                    K.v('dve', 'tensor_copy', [pO], [ot], out=ot[:], in_=pO[:, 0:2, :])
                    t0 = (r // 2) * 128
                    K.dma(S['oT_s'][0:2, :, t0:t0 + 128].rearrange("g p t -> p g t"), ot[:], [ot], [S['oT_s']])

        _pipeline(steps, s_fn, rest_fn)


Prog.phase_B = _phase_B
Prog.load_tok = _load_tok
Prog.finish_heads = _finish_heads
Prog.mix_na = _mix_na


def _stub(self, l):
    pass


for _n in ('mix_sw', 'mix_mla', 'mix_ml'):
    if not hasattr(Prog, _n):
        setattr(Prog, _n, _stub)


def _mix_sw(self, l):
    K, I, S, C = self.K, self.I, self.S, self.C
    sc = 0.125
    with ExitStack() as es:
        sk = K.sb(es, "swsk", [128, 4], F32)
        K.dma(sk[:], bcast_rows(I['sw_sink'][l:l + 1, :], 128), [I['sw_sink']], [sk])
        K.act(sk[:], sk[:], AF.Exp, [sk], [sk])
        rec = K.sb(es, "swrec", [128, 4], F32)
        ob = K.sb(es, "swob", [128, 256], BF16)
        ot = K.sb(es, "swot", [128, 2, 128], BF16)
        pS = [K.ps(es, "swpS%d" % i, [128, 512], F32) for i in range(2)]
        pA = [K.ps(es, "swpA%d" % i, [128, 512], F32) for i in range(2)]
        pO = K.ps(es, "swpO", [128, 8, 128], BF16)
        pT = K.ps(es, "swpT", [128, 8, 128], BF16)

        def emit_out(acc, s, t0):
            accv = acc[:, 0:260].rearrange("p (h c) -> p h c", h=4)
            self.finish_heads((acc, accv), 4, rec, (ob, ob[:].rearrange("p (h c) -> p h c", h=4)), extra_den=sk)
            for g in range(2):
                K.tr(pO[:, g, :], ob[:, g * 128:(g + 1) * 128], self.ident_bf[:], [ob, self.ident_bf], [pO])
            K.v('dve', 'tensor_copy', [pO], [ot], out=ot[:], in_=pO[:, 0:2, :])
            K.dma(S['oT_' + s][4:6, :, t0:t0 + 128].rearrange("g p t -> p g t"), ot[:], [ot], [S['oT_' + s]])

        with ExitStack() as es2:
            qk = K.sb(es2, "swqk", [128, 4, PSEQ], BF16)
            va = K.sb(es2, "swva", [128, 2, 130], BF16)
            E = [K.sb(es2, "swE%d" % i, [128, 512], BF16) for i in range(2)]
            step = 0
            for sq in range(NPSEQ):
                b0 = sq * PSEQ
                K.dma(qk[:], S['featT_p'][8:12, :, b0:b0 + PSEQ].rearrange("g p t -> p g t"), [S['featT_p']], [qk])
                K.dma(va[:], S['vaug_p'][b0:b0 + PSEQ, 520:650].rearrange("(c p) f -> p c f", p=128), [S['vaug_p']], [va])
                for g in range(2):
                    for c in range(2):
                        p = pS[step % 2]
                        e = E[step % 2]
                        step += 1
                        for r in range(2):
                            kg = 2 if g == r else 3
                            K.mm(p[:, r * 256:(r + 1) * 256], qk[r * 64:r * 64 + 64, kg, c * 128:(c + 1) * 128],
                                 qk[r * 64:r * 64 + 64, g, :], r == 0, True, [qk], [p], pbase=r * 64)
                        K.act(e[:], p[:], AF.Exp, [p], [e], scale=sc)
                        for r in range(2):
                            for j in range(2):
                                hh = g * 2 + r
                                K.mm(pA[j][:, hh * 65:(hh + 1) * 65], e[:, r * 256 + j * 128:r * 256 + (j + 1) * 128],
                                     va[:, c, g * 65:(g + 1) * 65], (g == 0 and c == 0 and r == 0), c == 1, [e, va], [pA[j]])
                for j in range(2):
                    emit_out(pA[j], 'p', b0 + j * 128)
        K.barrier()
        with ExitStack() as es2:
            qk = K.sb(es2, "swsqk", [128, 4, TS], BF16)
            VA = K.sb(es2, "swsVA", [128, 32, 130], BF16)
            ctok = K.sb(es2, "swctok", [128, 4, 128], F32)
            ctb = K.sb(es2, "swctb", [128, 4, 2, 128], BF16)
            ck = K.sb(es2, "swck", [128, 2, 512], BF16)
            cva = K.sb(es2, "swcva", [128, 4, 130], BF16)
            mge = K.sb(es2, "swmge", [128, 256], BF16)
            mle = K.sb(es2, "swmle", [128, 256], BF16)
            E = [K.sb(es2, "swsE%d" % i, [128, 256], BF16) for i in range(2)]
            K.dma(qk[:], S['featT_s'][8:12, :, :].rearrange("g p t -> p g t"), [S['featT_s']], [qk])
            self.load_tok(VA, lambda c0, n: VA[:, c0:c0 + n, :], S['vaug_s'], 520, 650, 0, 32)
            K.dma(mge[:], C['mb_ge2'][:, :], [C['mb_ge2']], [mge])
            K.dma(mle[:], C['mb_le2'][:, :], [C['mb_le2']], [mle])
            K.dma(ctok[:], I['cswk'][l].rearrange("(c p) f -> p c f", p=128), [I['cswk']], [ctok])
            K.act(ctb[:, :, 0, :], ctok[:], AF.Copy, [ctok], [ctb])
            K.act(ctb[:, :, 1, 0:64], ctok[:, :, 64:128], AF.Copy, [ctok], [ctb])
            K.act(ctb[:, :, 1, 64:128], ctok[:, :, 0:64], AF.Copy, [ctok], [ctb])
            for c in range(4):
                for v in range(2):
                    K.tr(pT[:, c * 2 + v, :], ctb[:, c, v, :], self.ident_bf[:], [ctb, self.ident_bf], [pT])
            K.v('dve', 'tensor_copy', [pT], [ck], out=ck[:].rearrange("p v (c t) -> p c v t", c=4),
                in_=pT[:].rearrange("p (c v) t -> p c v t", c=4))
            K.v('dve', 'memset', [], [cva], cva[:], 1.0)
            K.dma(ctok[:], I['cswv'][l].rearrange("(c p) f -> p c f", p=128), [I['cswv']], [ctok])
            K.v('dve', 'tensor_copy', [ctok], [cva], out=cva[:].rearrange("p c (h e) -> p c h e", h=2)[:, :, :, 0:64],
                in_=ctok[:].rearrange("p c (h e) -> p c h e", h=2))
            steps = []
            NB = TS // 128
            for n in range(NB):
                for g in range(2):
                    chunks = []
                    if n > 0:
                        chunks.append(('loc', n - 1, mge))
                    chunks.append(('loc', n, None))
                    if n < NB - 1:
                        chunks.append(('loc', n + 1, mle))
                    for c in range(4):
                        chunks.append(('ctx', c, None))
                    for ci, ch in enumerate(chunks):
                        steps.append((n, g, ci, len(chunks)) + ch)

            def s_fn(st, b):
                n, g, ci, nch, kind, c, mask = st
                p = pS[b]
                for r in range(2):
                    v = 0 if g == r else 1
                    if kind == 'loc':
                        kk = qk[r * 64:r * 64 + 64, 2 + v, c * 128:(c + 1) * 128]
                    else:
                        kk = ck[r * 64:r * 64 + 64, v, c * 128:(c + 1) * 128]
                    K.mm(p[:, r * 128:(r + 1) * 128], kk, qk[r * 64:r * 64 + 64, g, n * 128:(n + 1) * 128],
                         r == 0, mask is None, [qk, ck], [p], pbase=r * 64)
                if mask is not None:
                    K.mm(p[:, 0:256], self.ident_bf[:], mask[:], False, True, [self.ident_bf, mask], [p])

            def rest_fn(st, b):
                n, g, ci, nch, kind, c, mask = st
                p, e = pS[b], E[b]
                acc = pA[n % 2]
                K.act(e[:], p[:, 0:256], AF.Exp, [p], [e], scale=sc)
                vv = VA[:, c, :] if kind == 'loc' else cva[:, c, :]
                for r in range(2):
                    hh = g * 2 + r
                    K.mm(acc[:, hh * 65:(hh + 1) * 65], e[:, r * 128:(r + 1) * 128], vv[:, g * 65:(g + 1) * 65],
                         (g == 0 and ci == 0 and r == 0), ci == nch - 1, [e, VA, cva], [acc])
                if g == 1 and ci == nch - 1:
                    emit_out(acc, 's', n * 128)

            _pipeline(steps, s_fn, rest_fn)


def _mix_mla(self, l):
    K, I, S, C = self.K, self.I, self.S, self.C
    sc = 96.0 ** -0.5
    with ExitStack() as es:
        rec = K.sb(es, "mlrec", [128, 4], F32)
        ob = K.sb(es, "mlob", [128, 256], BF16)
        ot = K.sb(es, "mlot", [128, 2, 128], BF16)
        E = [K.sb(es, "mlaE%d" % i, [128, 512], BF16) for i in range(2)]
        pS = [K.ps(es, "mlpS%d" % i, [128, 512], F32) for i in range(2)]
        pA = [K.ps(es, "mlpA%d" % i, [128, 512], F32) for i in range(4)]
        pO = K.ps(es, "mlpO", [128, 8, 128], BF16)
        knT = K.sb(es, "mlknT", [64, 4, TS + PAST], BF16)
        krT = K.sb(es, "mlkrT", [32, TS + PAST], BF16)
        VA = K.sb(es, "mlVA", [128, 36, 260], BF16)
        qn = K.sb(es, "mlqn", [64, 4, 512], BF16)
        qr = K.sb(es, "mlqr", [32, 4, 512], BF16)

        def run(s, q0, nq, k0, nkc, extra_chunks):
            K.dma(qn[:, :, 0:nq], S['m64T_' + s][0:4, :, q0:q0 + nq].rearrange("g p t -> p g t"), [S['m64T_' + s]], [qn])
            K.dma(qr[:, :, 0:nq], S['m32T_' + s][0:4, :, q0:q0 + nq].rearrange("g p t -> p g t"), [S['m32T_' + s]], [qr])
            chunks = list(range(nkc)) + list(extra_chunks)
            steps = [(h, ci, c) for h in range(4) for ci, c in enumerate(chunks)]

            def s_fn(st, b):
                h, ci, c = st
                p = pS[b]
                K.mm(p[:, 0:nq], knT[:, h, c * 128:(c + 1) * 128], qn[:, h, 0:nq], True, False, [knT, qn], [p])
                K.mm(p[:, 0:nq], krT[:, c * 128:(c + 1) * 128], qr[:, h, 0:nq], False, True, [krT, qr], [p])

            def rest_fn(st, b):
                h, ci, c = st
                p, e = pS[b], E[b]
                K.act(e[:, 0:nq], p[:, 0:nq], AF.Exp, [p], [e], scale=sc)
                for j in range(nq // 128):
                    K.mm(pA[j][:, h * 65:(h + 1) * 65], e[:, j * 128:(j + 1) * 128], VA[:, c, h * 65:(h + 1) * 65],
                         ci == 0, ci == len(chunks) - 1, [e, VA], [pA[j]])

            _pipeline(steps, s_fn, rest_fn)
            for j in range(nq // 128):
                accv = pA[j][:, 0:260].rearrange("p (h c) -> p h c", h=4)
                self.finish_heads((pA[j], accv), 4, rec, (ob, ob[:].rearrange("p (h c) -> p h c", h=4)))
                for g in range(2):
                    K.tr(pO[:, g, :], ob[:, g * 128:(g + 1) * 128], self.ident_bf[:], [ob, self.ident_bf], [pO])
                K.v('dve', 'tensor_copy', [pO], [ot], out=ot[:], in_=pO[:, 0:2, :])
                t0 = q0 + j * 128
                K.dma(S['oT_' + s][6:8, :, t0:t0 + 128].rearrange("g p t -> p g t"), ot[:], [ot], [S['oT_' + s]])

        for sq in range(NPSEQ):
            b0 = sq * PSEQ
            K.dma(knT[:, :, 0:PSEQ], S['m64T_p'][4:8, :, b0:b0 + PSEQ].rearrange("g p t -> p g t"), [S['m64T_p']], [knT])
            K.dma(krT[:, 0:PSEQ], S['m32T_p'][4, :, b0:b0 + PSEQ], [S['m32T_p']], [krT])
            K.dma(VA[:, 0:2, :], S['vaug_p'][b0:b0 + PSEQ, 650:910].rearrange("(c p) f -> p c f", p=128),
                  [S['vaug_p']], [VA])
            run('p', b0, PSEQ, 0, 2, [])
        K.barrier()
        with ExitStack() as es2:
            stage = K.sb(es2, "mlstage", [128, 2, 256], F32)
            wuk = K.sb(es2, "mlwuk", [128, 256], BF16)
            wuv = K.sb(es2, "mlwuv", [128, 256], BF16)
            ctok = K.sb(es2, "mlctok", [128, 4, 160], F32)
            ctb = K.sb(es2, "mlctb", [128, 4, 160], BF16)
            ccT = K.sb(es2, "mlccT", [128, 512], BF16)
            pT = pO
            K.dma(knT[:, :, 0:TS], S['m64T_s'][4:8, :, :].rearrange("g p t -> p g t"), [S['m64T_s']], [knT])
            K.dma(krT[:, 0:TS], S['m32T_s'][4, :, :], [S['m32T_s']], [krT])
            self.load_tok(VA, lambda c0, n: VA[:, c0:c0 + n, :], S['vaug_s'], 650, 910, 0, 32)
            K.dma(stage[:, 0, :], I['w_uk'][l, :, :], [I['w_uk']], [stage])
            K.dma(stage[:, 1, :], I['w_uv'][l, :, :], [I['w_uv']], [stage])
            K.v('dve', 'tensor_copy', [stage], [wuk], out=wuk[:], in_=stage[:, 0, :])
            K.v('dve', 'tensor_copy', [stage], [wuv], out=wuv[:], in_=stage[:, 1, :])
            K.dma(ctok[:, :, 0:128], I['cckv'][l].rearrange("(c p) f -> p c f", p=128), [I['cckv']], [ctok])
            K.dma(ctok[:, :, 128:160], I['ckr'][l].rearrange("(c p) f -> p c f", p=128), [I['ckr']], [ctok])
            K.act(ctb[:], ctok[:], AF.Copy, [ctok], [ctb])
            for c in range(4):
                K.tr(pT[:, c, :], ctb[:, c, 0:128], self.ident_bf[:], [ctb, self.ident_bf], [pT])
                K.tr(pT[0:32, 4 + c, :], ctb[:, c, 128:160], self.ident_bf[:], [ctb, self.ident_bf], [pT])
            K.v('dve', 'tensor_copy', [pT], [ccT], out=ccT[:].rearrange("p (c t) -> p c t", c=4), in_=pT[:, 0:4, :])
            K.v('dve', 'tensor_copy', [pT], [krT], out=krT[:, TS:TS + PAST].rearrange("p (c t) -> p c t", c=4),
                in_=pT[0:32, 4:8, :])
            for h in range(4):
                p = pS[h % 2]
                K.mm(p[0:64, :], wuk[:, h * 64:(h + 1) * 64], ccT[:], True, True, [wuk, ccT], [p])
                K.act(knT[:, h, TS:TS + PAST], p[0:64, :], AF.Copy, [p], [knT])
            K.v('dve', 'memset', [], [VA], VA[:, 32:36, :], 1.0)
            for c in range(4):
                p = pS[c % 2]
                K.mm(p[:, 0:256], ccT[:, c * 128:(c + 1) * 128], wuv[:], True, True, [ccT, wuv], [p])
                K.v('dve', 'tensor_copy', [p], [VA], out=VA[:, 32 + c, :].rearrange("p (h e) -> p h e", h=4)[:, :, 0:64],
                    in_=p[:, 0:256].rearrange("p (h e) -> p h e", h=4))
            for qg in range(TS // 512):
                run('s', qg * 512, 512, 0, 32, [32, 33, 34, 35])


Prog.mix_sw = _mix_sw
Prog.mix_mla = _mix_mla


def _ml_part2(self, d, c, cs, dsl, tri, wcol, WT, PT, pS, ktok, kw, pA, VA, qk, Cbf, pC, Cdec, Cst, ecol, den, htmp, hsum):
    K = self.K
    K.v('dve', 'tensor_tensor', [tri[d], wcol], [WT[d]], out=WT[d][:],
        in0=tri[d][:].unsqueeze(1).to_broadcast([128, 4, 128]),
        in1=wcol[:, c, dsl].unsqueeze(2).to_broadcast([128, 4, 128]), op=ALU.mult)
    K.v('dve', 'tensor_tensor', [pS[d], WT[d]], [PT[d]], out=PT[d][:],
        in0=pS[d][:].rearrange("p (h t) -> p h t", h=4), in1=WT[d][:], op=ALU.mult)
    K.v('dve', 'tensor_tensor', [ktok, wcol], [kw[d]], out=kw[d][:],
        in0=ktok[:, c, :].rearrange("p (h e) -> p h e", h=4),
        in1=wcol[:, c, dsl].unsqueeze(2).to_broadcast([128, 4, 64]), op=ALU.mult)
    for h in range(4):
        K.mm(pA[d][:, h * 65:(h + 1) * 65], PT[d][:, h, :], VA[:, c, h * 65:(h + 1) * 65], h == 0, False,
             [PT[d], VA], [pA[d]])
    for h in (0, 2, 1, 3):
        g, hb = h // 2, (h % 2) * 64
        K.mm(pA[d][:, h * 65:(h + 1) * 65], qk[hb:hb + 64, g, cs], Cbf[d][hb:hb + 64, g, :], False, True,
             [qk, Cbf[d]], [pA[d]], pbase=hb)
    for g in range(2):
        for ab in range(2):
            K.mm(pC[d][:, (g * 2 + ab) * 65:(g * 2 + ab + 1) * 65],
                 kw[d][:, 2 * g:2 * g + 2, :].rearrange("p h e -> p (h e)"),
                 VA[:, c, (2 * g + ab) * 65:(2 * g + ab + 1) * 65], (g == 0 and ab == 0), True,
                 [kw[d], VA], [pC[d]])
    pcv = pC[d][:, 0:260].rearrange("p (g a e) -> p g a e", g=2, a=2)
    for hf in range(2):
        ps_ = slice(hf * 64, hf * 64 + 64)
        K.v('dve', 'tensor_tensor', [Cdec[d], pC[d]], [Cst[d]], out=Cst[d][ps_, :, :],
            in0=Cdec[d][ps_, :, :], in1=pcv[ps_, :, hf, :], op=ALU.add)
    accv = pA[d][:, 0:260].rearrange("p (h e) -> p h e", h=4)
    K.v('dve', 'tensor_copy', [pA[d]], [den[d]], out=den[d][:], in_=accv[:, :, 64])
    K.v('dve', 'scalar_tensor_tensor', [den[d]], [den[d]], out=den[d][:], in0=den[d][:], scalar=-1.0,
        in1=den[d][:], op0=ALU.mult, op1=ALU.max)
    K.v('dve', 'tensor_tensor', [den[d], ecol], [den[d]], out=den[d][:], in0=den[d][:],
        in1=ecol[:, c, dsl], op=ALU.max)
    K.v('dve', 'reciprocal', [den[d]], [den[d]], out=den[d][:], in_=den[d][:])
    K.v('dve', 'tensor_tensor', [pA[d], den[d]], [htmp[d]], out=htmp[d][:], in0=accv[:, :, 0:64],
        in1=den[d][:].unsqueeze(2).to_broadcast([128, 4, 64]), op=ALU.mult)
    K.v('pool', 'tensor_tensor', [hsum, htmp[d]], [hsum], out=hsum[:, c, :], in0=hsum[:, c, :],
        in1=htmp[d][:].rearrange("p h e -> p (h e)"), op=ALU.add)


def _mix_ml(self, l):
    K, I, S, C, O = self.K, self.I, self.S, self.C, self.O
    for s in ('p', 's'):
        seqs = [(sq, sq * PSEQ, PSEQ // 128) for sq in range(NPSEQ)] if s == 'p' else [(0, 0, TS // 128)]
        N = seqs[0][2]
        with ExitStack() as es:
            tle = K.sb(es, "mtle", [128, 128], F32)
            tge = K.sb(es, "mtge", [128, 128], F32)
            K.dma(tle[:], C['tri_le'][:, :], [C['tri_le']], [tle])
            K.dma(tge[:], C['tri_ge'][:, :], [C['tri_ge']], [tge])
            tri = (tle, tge)
            qk = K.sb(es, "mqk", [128, 4, N * 128], BF16)
            VA = K.sb(es, "mVA", [128, N, 260], BF16)
            stg = K.sb(es, "mstg", [128, 4, 256], F32)
            ktok = K.sb(es, "mktok", [128, N, 256], BF16)
            og = K.sb(es, "mog", [128, N, 256], BF16)
            hsum = K.sb(es, "mhsum", [128, N, 256], F32)
            G = K.sb(es, "mG", [128, N, 16], F32)
            sp = K.sb(es, "msp", [128, N, 8], F32)
            nb = K.sb(es, "mnb", [128, N, 8], F32)
            aa = K.sb(es, "maa", [128, N, 8], F32)
            tot = K.sb(es, "mtot", [128, N, 8], F32)
            amx = K.sb(es, "mamx", [128, N, 8], F32)
            Mc = K.sb(es, "mMc", [128, N, 8], F32)
            mpv = K.sb(es, "mmpv", [128, N, 8], F32)
            wcol = K.sb(es, "mwcol", [128, N, 8], F32)
            dcy = K.sb(es, "mdcy", [128, N, 8], F32)
            ecol = K.sb(es, "mecol", [128, N, 8], F32)
            mcur = K.sb(es, "mmcur", [128, 8], F32)
            amr = K.sb(es, "mamr", [4, 2, N], F32)
            Dm = K.sb(es, "mDm", [4, 2, N, 4], F32)
            Cst = [K.sb(es, "mCst%d" % d, [128, 2, 65], F32) for d in range(2)]
            Cdec = [K.sb(es, "mCdec%d" % d, [128, 2, 65], F32) for d in range(2)]
            Cbf = [K.sb(es, "mCbf%d" % d, [128, 2, 65], BF16) for d in range(2)]
            WT = [K.sb(es, "mWT%d" % d, [128, 4, 128], F32) for d in range(2)]
            PT = [K.sb(es, "mPT%d" % d, [128, 4, 128], BF16) for d in range(2)]
            kw = [K.sb(es, "mkw%d" % d, [128, 4, 64], BF16) for d in range(2)]
            den = [K.sb(es, "mden%d" % d, [128, 4], F32) for d in range(2)]
            htmp = [K.sb(es, "mht%d" % d, [128, 4, 64], F32) for d in range(2)]
            ss = K.sb(es, "mss", [128, 4], F32)
            sq2 = K.sb(es, "msq2", [128, 256], F32)
            ob = K.sb(es, "mob", [128, 256], BF16)
            ot = K.sb(es, "mot", [128, 2, 128], BF16)
            pS = [K.ps(es, "mpS%d" % d, [128, 512], F32) for d in range(2)]
            pA = [K.ps(es, "mpA%d" % d, [128, 512], F32) for d in range(2)]
            pC = [K.ps(es, "mpC%d" % d, [128, 512], F32) for d in range(2)]
            pX = K.ps(es, "mpX", [128, 512], F32)
            pO = K.ps(es, "mpO", [128, 8, 128], BF16)
            for (sq, b0, _) in seqs:
                K.dma(qk[:], S['featT_' + s][4:8, :, b0:b0 + N * 128].rearrange("g p t -> p g t"), [S['featT_' + s]], [qk])
                self.load_tok(VA, lambda c0, n: VA[:, c0:c0 + n, :], S['vaug_' + s], 260, 520, b0, N)
                self.load_tok(G, lambda c0, n: G[:, c0:c0 + n, :], S['mlx_' + s], 512, 528, b0, N)
                for c0 in range(0, N, 4):
                    n = min(4, N - c0)
                    self.load_tok(stg, lambda cc, nn: stg[:, 0:n, :], S['mlx_' + s], 0, 256, b0 + c0 * 128, n)
                    K.act(ktok[:, c0:c0 + n, :], stg[:, 0:n, :], AF.Copy, [stg], [ktok])
                    self.load_tok(stg, lambda cc, nn: stg[:, 0:n, :], S['mlx_' + s], 256, 512, b0 + c0 * 128, n)
                    K.act(og[:, c0:c0 + n, :], stg[:, 0:n, :], AF.Sigmoid, [stg], [og])
                K.v('dve', 'memset', [], [hsum], hsum[:], 0.0)
                K.act(sp[:], G[:, :, 8:16], AF.Exp, [G], [sp], scale=-1.0)
                K.act(sp[:], sp[:], AF.Ln, [sp], [sp], bias=1.0)
                for d in range(2):
                    K.mm(pX[:, d * N * 4:(d + 1) * N * 4].rearrange("p (c j) -> p c j", j=4), tri[d][:],
                         sp[:, :, d * 4:(d + 1) * 4], d == 0, d == 1, [tri[d], sp], [pX])
                K.v('dve', 'tensor_copy', [pX], [nb], out=nb[:].rearrange("p c (d j) -> p d c j", d=2),
                    in_=pX[:, 0:N * 8].rearrange("p (d c j) -> p d c j", d=2, c=N))
                K.v('dve', 'tensor_tensor', [G, nb], [aa], out=aa[:], in0=G[:, :, 0:8], in1=nb[:], op=ALU.add)
                K.mm(pX[:, 0:N * 8], self.ones_f[:], sp[:].rearrange("p c j -> p (c j)"), True, True, [self.ones_f, sp], [pX])
                K.v('dve', 'tensor_copy', [pX], [tot], out=tot[:], in_=pX[:, 0:N * 8].rearrange("p (c j) -> p c j", c=N))
                for d in range(2):
                    for c0 in range(0, N, 4):
                        n = min(4, N - c0)
                        for ci in range(n):
                            c = c0 + ci
                            K.mm(pX[0:4, ci * 128:(ci + 1) * 128], G[:, c, d * 4:(d + 1) * 4], self.ident_f[:], ci == 0,
                                 False, [G, self.ident_f], [pX])
                            K.mm(pX[0:4, ci * 128:(ci + 1) * 128], sp[:, c, d * 4:(d + 1) * 4], tri[d][:], False, True,
                                 [sp, tri[d]], [pX])
                        K.v('dve', 'tensor_reduce', [pX], [amr], out=amr[:, d, c0:c0 + n],
                            in_=pX[0:4, 0:n * 128].rearrange("p (c t) -> p c t", c=n), axis=AX.X, op=ALU.max)
                K.v('dve', 'tensor_tensor', [amr, self.ident_f], [Dm], out=Dm[:],
                    in0=amr[:].unsqueeze(3).to_broadcast([4, 2, N, 4]),
                    in1=self.ident_f[0:4, 0:4].unsqueeze(1).unsqueeze(1).to_broadcast([4, 2, N, 4]), op=ALU.mult)
                K.mm(pX[:, 0:N * 8], self.ones_f[0:4, :], Dm[:].rearrange("p d c j -> p (d c j)"), True, True,
                     [self.ones_f, Dm], [pX])
                K.v('dve', 'tensor_copy', [pX], [amx], out=amx[:].rearrange("p c (d j) -> p d c j", d=2),
                    in_=pX[:, 0:N * 8].rearrange("p (d c j) -> p d c j", d=2, c=N))
                if s == 's':
                    K.dma(mcur[:], bcast_rows(I['stm'][l:l + 1, :], 128), [I['stm']], [mcur])
                else:
                    K.v('dve', 'memset', [], [mcur], mcur[:], 0.0)
                for j in range(N):
                    for d in range(2):
                        c = j if d == 0 else N - 1 - j
                        sl = slice(d * 4, d * 4 + 4)
                        K.v('dve', 'tensor_copy', [mcur], [mpv], out=mpv[:, c, sl], in_=mcur[:, sl])
                        K.v('dve', 'tensor_tensor', [mcur, amx], [Mc], out=Mc[:, c, sl], in0=mcur[:, sl], in1=amx[:, c, sl],
                            op=ALU.max)
                        K.v('dve', 'tensor_tensor', [Mc, tot], [mcur], out=mcur[:, sl], in0=Mc[:, c, sl], in1=tot[:, c, sl],
                            op=ALU.subtract)
                if s == 'p':
                    K.dma(O['o_m'][sq, l:l + 1, :], mcur[0:1, :], [mcur], [O['o_m']])
                K.v('dve', 'tensor_tensor', [aa, Mc], [wcol], out=wcol[:], in0=aa[:], in1=Mc[:], op=ALU.subtract)
                K.act(wcol[:], wcol[:], AF.Exp, [wcol], [wcol])
                K.v('dve', 'tensor_tensor', [mpv, Mc], [dcy], out=dcy[:], in0=mpv[:], in1=Mc[:], op=ALU.subtract)
                K.act(dcy[:], dcy[:], AF.Exp, [dcy], [dcy])
                K.v('dve', 'tensor_tensor', [nb, Mc], [ecol], out=ecol[:], in0=nb[:], in1=Mc[:], op=ALU.subtract)
                K.act(ecol[:], ecol[:], AF.Exp, [ecol], [ecol])
                for d in range(2):
                    if s == 's':
                        for h in range(4):
                            g, hb = h // 2, (h % 2) * 64
                            K.dma(Cst[d][hb:hb + 64, g, 0:64], I['stC'][l, d * 4 + h, :, :], [I['stC']], [Cst[d]])
                            K.dma(Cst[d][hb:hb + 64, g, 64:65], I['stn'][l, d * 4 + h:d * 4 + h + 1, :].rearrange("o n -> n o"),
                                  [I['stn']], [Cst[d]])
                    else:
                        K.v('dve', 'memset', [], [Cst[d]], Cst[d][:], 0.0)
                for j in range(N):
                  for part in range(2):
                    for d in range(2):
                        c = j if d == 0 else N - 1 - j
                        cs = slice(c * 128, (c + 1) * 128)
                        dsl = slice(d * 4, d * 4 + 4)
                        if part == 1:
                            self._ml_part2(d, c, cs, dsl, tri, wcol, WT, PT, pS, ktok, kw, pA, VA, qk, Cbf, pC, Cdec, Cst, ecol,
                                           den, htmp, hsum)
                            continue
                        for g in range(2):
                            for hf in range(2):
                                ps_ = slice(hf * 64, hf * 64 + 64)
                                col = d * 4 + 2 * g + hf
                                K.act(Cdec[d][ps_, g, :], Cst[d][ps_, g, :], AF.Copy, [Cst[d], dcy], [Cdec[d]],
                                      scale=dcy[ps_, c, col:col + 1])
                        K.act(Cbf[d][:], Cdec[d][:], AF.Copy, [Cdec[d]], [Cbf[d]])
                        for h in (0, 2, 1, 3):
                            g, hb = h // 2, (h % 2) * 64
                            K.mm(pS[d][:, h * 128:(h + 1) * 128], qk[hb:hb + 64, 2 + g, cs], qk[hb:hb + 64, g, cs], h == 0, True,
                                 [qk], [pS[d]], pbase=hb)
                        continue
                if s == 'p':
                    for d in range(2):
                        for h in range(4):
                            g, hb = h // 2, (h % 2) * 64
                            K.dma(O['o_C'][sq, l, d * 4 + h, :, :], Cst[d][hb:hb + 64, g, 0:64], [Cst[d]], [O['o_C']])
                            K.dma(O['o_n'][sq, l, d * 4 + h:d * 4 + h + 1, :].rearrange("o n -> n o"),
                                  Cst[d][hb:hb + 64, g, 64:65], [Cst[d]], [O['o_n']])
                for c in range(N):
                    K.v('dve', 'tensor_tensor', [hsum], [sq2], out=sq2[:], in0=hsum[:, c, :], in1=hsum[:, c, :], op=ALU.mult)
                    K.v('dve', 'tensor_reduce', [sq2], [ss], out=ss[:], in_=sq2[:].rearrange("p (h e) -> p h e", h=4),
                        axis=AX.X, op=ALU.add)
                    K.v('dve', 'tensor_scalar', [ss], [ss], out=ss[:], in0=ss[:], scalar1=1.0 / 64, scalar2=EPS,
                        op0=ALU.mult, op1=ALU.add)
                    K.act(ss[:], ss[:], AF.Sqrt, [ss], [ss])
                    K.v('dve', 'reciprocal', [ss], [ss], out=ss[:], in_=ss[:])
                    K.v('dve', 'tensor_tensor', [hsum, ss], [sq2], out=sq2[:].rearrange("p (h e) -> p h e", h=4),
                        in0=hsum[:, c, :].rearrange("p (h e) -> p h e", h=4),
                        in1=ss[:].unsqueeze(2).to_broadcast([128, 4, 64]), op=ALU.mult)
                    K.v('dve', 'tensor_tensor', [sq2, og], [ob], out=ob[:], in0=sq2[:], in1=og[:, c, :], op=ALU.mult)
                    for g in range(2):
                        K.tr(pO[:, g, :], ob[:, g * 128:(g + 1) * 128], self.ident_bf[:], [ob, self.ident_bf], [pO])
                    K.v('dve', 'tensor_copy', [pO], [ot], out=ot[:], in_=pO[:, 0:2, :])
                    t0 = b0 + c * 128
                    K.dma(S['oT_' + s][2:4, :, t0:t0 + 128].rearrange("g p t -> p g t"), ot[:], [ot], [S['oT_' + s]])
        K.barrier()


Prog.mix_ml = _mix_ml
Prog._ml_part2 = _ml_part2


def _phase_C(self, l):
    self.phase_C1(l)
    self.K.barrier()
    self.phase_C2(l)


def _phase_C1(self, l):
    K, I, S = self.K, self.I, self.S
    with ExitStack() as es:
        wing = K.sb(es, "c1wing", [128, 8, 4096], BF16)
        wbr = K.sb(es, "c1wbr", [128, 8, 1024], BF16)
        wout = K.sb(es, "c1wout", [128, 8, 1024], BF16)
        stage = K.sb(es, "c1stage", [128, 8, 512], F32)
        bg = K.sb(es, "c1bg", [128, 4096], F32)
        x = K.sb(es, "c1x", [128, D], F32)
        h32 = K.sb(es, "c1h32", [128, D], F32)
        hbf = K.sb(es, "c1hbf", [128, D], BF16)
        hT = K.sb(es, "c1hT", [128, 8, 128], BF16)
        ss = K.sb(es, "c1ss", [128, 4], F32)
        oTt = K.sb(es, "c1oTt", [128, 8, 128], BF16)
        sg = K.sb(es, "c1sg", [128, 1024], F32)
        acc = K.sb(es, "c1acc", [128, 1024], F32)
        accb = K.sb(es, "c1accb", [128, 1024], BF16)
        accT = K.sb(es, "c1accT", [128, 8, 128], BF16)
        pT = K.ps(es, "c1pT", [128, 8, 128], BF16)
        pG = [K.ps(es, "c1pG%d" % i, [128, 512], F32) for i in range(4)]
        pY = [K.ps(es, "c1pY%d" % i, [128, 512], F32) for i in range(2)]
        pM = pG[0:2]
        sgs = [sg, K.sb(es, "c1sg2", [128, 1024], F32)]
        for j in range(8):
            K.dma(stage[:], I['w_in'][l, :, O_GATE + j * 512:O_GATE + (j + 1) * 512].rearrange("(k p) n -> p k n", p=128),
                  [I['w_in']], [stage])
            K.v('pool', 'tensor_copy', [stage], [wing], out=wing[:, :, j * 512:(j + 1) * 512], in_=stage[:])
        for j in range(2):
            K.dma(stage[:], I['w_branch'][l, :, j * 512:(j + 1) * 512].rearrange("(k p) n -> p k n", p=128),
                  [I['w_branch']], [stage])
            K.v('pool', 'tensor_copy', [stage], [wbr], out=wbr[:, :, j * 512:(j + 1) * 512], in_=stage[:])
            K.dma(stage[:], I['w_out'][l, :, j * 512:(j + 1) * 512].rearrange("(k p) n -> p k n", p=128),
                  [I['w_out']], [stage])
            K.v('pool', 'tensor_copy', [stage], [wout], out=wout[:, :, j * 512:(j + 1) * 512], in_=stage[:])
        K.dma(bg[:], bcast_rows(I['b_in'][l:l + 1, O_GATE:INW], 128), [I['b_in']], [bg])
        for s in ('p', 's'):
            with ExitStack() as es2:
                mods = self.load_mod(es2, l, s, [0, 1, 2], "mC1")
                A1, B1, G1 = mods[1], mods[0], mods[2]
                xsrc = (I['xp'] if s == 'p' else I['xs']) if l == 0 else S['xres_' + s]
                for t in range(self.T[s] // 128):
                    t0 = t * 128
                    K.dma(x[:], xsrc[t0:t0 + 128, :], [xsrc], [x])
                    K.dma(oTt[:], S['oT_' + s][:, :, t0:t0 + 128].rearrange("g p t -> p g t"), [S['oT_' + s]], [oTt])
                    self.norm_mod(x, A1, B1, h32, hbf, ss, h32)
                    for k in range(8):
                        K.tr(pT[:, k, :], hbf[:, k * 128:(k + 1) * 128], self.ident_bf[:], [hbf, self.ident_bf], [pT])
                    K.act(hT[:], pT[:], AF.Copy, [pT], [hT])
                    def gate_mm(n):
                        for j in range(2):
                            c0 = n * 1024 + j * 512
                            pg = pG[(n % 2) * 2 + j]
                            for k in range(8):
                                K.mm(pg[:], hT[:, k, :], wing[:, k, c0:c0 + 512], k == 0, k == 7, [hT, wing], [pg])

                    def branch_mm(n):
                        for j in range(2):
                            for kc in range(2):
                                K.mm(pY[j][:], oTt[:, 2 * n + kc, :], wbr[:, 2 * n + kc, j * 512:(j + 1) * 512], kc == 0,
                                     kc == 1, [oTt, wbr], [pY[j]])

                    def chain(n):
                        sg_ = sgs[n % 2]
                        for j in range(2):
                            c0 = n * 1024 + j * 512
                            pg = pG[(n % 2) * 2 + j]
                            K.v('dve', 'tensor_tensor', [pg, bg], [sg_], out=sg_[:, j * 512:(j + 1) * 512], in0=pg[:],
                                in1=bg[:, c0:c0 + 512], op=ALU.add)
                        K.act(sg_[:], sg_[:], AF.Sigmoid, [sg_], [sg_])
                        for j in range(2):
                            sl = slice(j * 512, (j + 1) * 512)
                            if n == 0:
                                K.v('dve', 'tensor_tensor', [pY[j], sg_], [acc], out=acc[:, sl], in0=pY[j][:], in1=sg_[:, sl],
                                    op=ALU.mult)
                            else:
                                K.v('dve', 'tensor_tensor', [pY[j], sg_], [sg_], out=sg_[:, sl], in0=pY[j][:], in1=sg_[:, sl],
                                    op=ALU.mult)
                                K.v('pool', 'tensor_tensor', [acc, sg_], [acc], out=acc[:, sl], in0=acc[:, sl], in1=sg_[:, sl],
                                    op=ALU.add)

                    gate_mm(0)
                    branch_mm(0)
                    for n in range(4):
                        if n + 1 < 4:
                            gate_mm(n + 1)
                        chain(n)
                        if n + 1 < 4:
                            branch_mm(n + 1)
                    K.act(accb[:], acc[:], AF.Copy, [acc], [accb])
                    for k in range(8):
                        K.tr(pT[:, k, :], accb[:, k * 128:(k + 1) * 128], self.ident_bf[:], [accb, self.ident_bf], [pT])
                    K.act(accT[:], pT[:], AF.Copy, [pT], [accT])
                    for j in range(2):
                        for k in range(8):
                            K.mm(pM[j][:], accT[:, k, :], wout[:, k, j * 512:(j + 1) * 512], k == 0, k == 7, [accT, wout],
                                 [pM[j]])
                        sl = slice(j * 512, (j + 1) * 512)
                        K.v('dve', 'tensor_tensor', [pM[j], G1], [h32], out=h32[:, sl], in0=pM[j][:], in1=G1[:, sl],
                            op=ALU.mult)
                    K.v('dve', 'tensor_tensor', [x, h32], [h32], out=h32[:], in0=x[:], in1=h32[:], op=ALU.add)
                    K.dma(S['xmid_' + s][t0:t0 + 128, :], h32[:], [h32], [S['xmid_' + s]])
            K.barrier()


def _phase_C2(self, l):
    K, I, S, O = self.K, self.I, self.S, self.O
    last = (l == self.depth - 1)
    with ExitStack() as es:
        wq = K.sb(es, "c2wq", [128, 8, 1024], BF16)
        kb = K.sb(es, "c2kb", [128, 16, 64], BF16)
        keysT = K.sb(es, "c2keysT", [128, 8, 128], BF16)
        fng = K.sb(es, "c2fng", [128, D], F32)
        x2 = [K.sb(es, "c2x%d" % i, [128, D], F32) for i in range(2)]
        h32 = K.sb(es, "c2h32", [128, D], F32)
        hbf = K.sb(es, "c2hbf", [128, D], BF16)
        hT = K.sb(es, "c2hT", [128, 8, 128], BF16)
        ss = K.sb(es, "c2ss", [128, 4], F32)
        ss2 = K.sb(es, "c2ss2", [128, 4], F32)
        qT = K.sb(es, "c2qT", [128, 8, 128], BF16)
        sc = K.sb(es, "c2sc", [128, 16, 128], F32)
        scw = K.sb(es, "c2scw", [128, 16, 128], F32)
        topv = K.sb(es, "c2topv", [128, 16, 16], F32)
        topi = K.sb(es, "c2topi", [128, 16, 16], U32)
        topf = K.sb(es, "c2topf", [128, 16, 16], F32)
        cand = K.sb(es, "c2cand", [128, 8, 256], F32)
        cidx = K.sb(es, "c2cidx", [128, 8, 256], F32)
        tv = K.sb(es, "c2tv", [128, 8, 16], F32)
        eq = K.sb(es, "c2eq", [128, 8, 256], F32)
        idxf = K.sb(es, "c2idxf", [128, 128], F32)
        idx2 = [K.sb(es, "c2idx%d" % i, [128, 128], I32) for i in range(2)]
        gw2 = [K.sb(es, "c2gw%d" % i, [128, 8, 16], F32) for i in range(2)]
        ga2 = [K.sb(es, "c2ga%d" % i, [128, 128], F32) for i in range(2)]
        zz = K.sb(es, "c2zz", [128, 8], F32)
        aa = K.sb(es, "c2aa", [128, 128], F32)
        t1 = K.sb(es, "c2t1", [128, 128], F32)
        t2 = K.sb(es, "c2t2", [128, 128], F32)
        acc = K.sb(es, "c2acc", [128, D], F32)
        junk = acc
        dg = [K.sb(es, "c2dg%d" % i, [128, 128], F32) for i in range(4)]
        pT = K.ps(es, "c2pT", [128, 8, 128], BF16)
        pQ = [K.ps(es, "c2pQ%d" % i, [128, 512], F32) for i in range(2)]
        pS = [K.ps(es, "c2pS%d" % i, [128, 512], F32) for i in range(2)]
        pV = [K.ps(es, "c2pV%d" % i, [128, 512], F32) for i in range(2)]
        with ExitStack() as esw:
            stage = K.sb(esw, "c2stage", [128, 8, 512], F32)
            for j in range(2):
                K.dma(stage[:], I['peer_wq'][l, :, j * 512:(j + 1) * 512].rearrange("(k p) n -> p k n", p=128),
                      [I['peer_wq']], [stage])
                K.v('pool', 'tensor_copy', [stage], [wq], out=wq[:, :, j * 512:(j + 1) * 512], in_=stage[:])
            K.dma(stage[:, 0:2, :].rearrange("p a (b d) -> p (a b) d", d=64), I['peer_keys'][l].rearrange("g n d -> n g d"),
                  [I['peer_keys']], [stage])
            K.v('pool', 'tensor_copy', [stage], [kb], out=kb[:], in_=stage[:, 0:2, :].rearrange("p a (b d) -> p (a b) d", d=64))
            K.barrier()
        NR = 20
        ring = [K.sb(es, "c2ring%d" % i, [128, D], F32) for i in range(NR)]
        rc = [0]

        def next_slot():
            i = rc[0] % NR
            rc[0] += 1
            return i

        for hh in range(8):
            K.tr(pT[:, hh, :], kb[:, 2 * hh:2 * hh + 2, :].rearrange("p a d -> p (a d)"), self.ident_bf[:],
                 [kb, self.ident_bf], [pT])
        K.act(keysT[:], pT[:], AF.Copy, [pT], [keysT])
        K.dma(fng[:], bcast_rows(I['final_norm_g'][0:1, :], 128), [I['final_norm_g']], [fng])
        pu, pv = I['peer_u'], I['peer_v']

        def gat(dst, tab, idx, col, slot):
            K.op('pool', [idx, tab], [dst], (lambda: self.nc.gpsimd.indirect_dma_start(
                out=dst[:], out_offset=None, in_=tab[:, :],
                in_offset=bass.IndirectOffsetOnAxis(ap=idx[:, col:col + 1], axis=0))), dma=True, slot=slot)

        for s in ('p', 's'):
            with ExitStack() as es2:
                mods = self.load_mod(es2, l, s, [3, 4, 5], "mC2")
                A2, B2, G2 = mods[4], mods[3], mods[5]

                def stage_R(t):
                    x, idx, gw = x2[t % 2], idx2[t % 2], gw2[t % 2]
                    t0 = t * 128
                    K.dma(x[:], S['xmid_' + s][t0:t0 + 128, :], [S['xmid_' + s]], [x])
                    self.norm_mod(x, A2, B2, h32, None, ss, junk)
                    K.act(hbf[:], h32[:], AF.Copy, [h32], [hbf])
                    for k in range(8):
                        K.tr(pT[:, k, :], hbf[:, k * 128:(k + 1) * 128], self.ident_bf[:], [hbf, self.ident_bf], [pT])
                    K.act(hT[:], pT[:], AF.Copy, [pT], [hT])
                    for j in range(8):
                        pq = pQ[j // 4]
                        for k in range(8):
                            K.mm(pq[:, (j % 4) * 128:(j % 4 + 1) * 128], wq[:, k, j * 128:(j + 1) * 128], hT[:, k, :],
                                 (k == 0 and j % 4 == 0), k == 7, [wq, hT], [pq])
                    for i in range(2):
                        K.act(qT[:, i * 4:(i + 1) * 4, :], pQ[i][:].rearrange("p (j t) -> p j t", j=4), AF.Copy, [pQ[i]], [qT])
                    for half in range(2):
                        for xx in range(2):
                            for h4 in range(4):
                                hh = half * 4 + h4
                                b = h4 // 2
                                cix = (h4 % 2) * 2 + xx
                                K.mm(pS[b][:, cix * 128:(cix + 1) * 128], qT[xx * 64:xx * 64 + 64, hh, :],
                                     keysT[xx * 64:xx * 64 + 64, hh, :], (xx == 0 and h4 % 2 == 0), True, [qT, keysT], [pS[b]],
                                     pbase=xx * 64)
                        for b in range(2):
                            K.act(sc[:, half * 8 + b * 4:half * 8 + b * 4 + 4, :], pS[b][:].rearrange("p (j t) -> p j t", j=4),
                                  AF.Copy, [pS[b]], [sc])
                    for hx in range(16):
                        K.v('dve', 'max', [sc], [topv], out=topv[:, hx, 0:8], in_=sc[:, hx, :])
                        K.v('dve', 'max_index', [sc, topv], [topi], out=topi[:, hx, 0:8], in_max=topv[:, hx, 0:8],
                            in_values=sc[:, hx, :])
                        K.v('dve', 'match_replace', [sc, topv], [scw], out=scw[:, hx, :], in_to_replace=topv[:, hx, 0:8],
                            in_values=sc[:, hx, :], imm_value=-1e30)
                        K.v('dve', 'max', [scw], [topv], out=topv[:, hx, 8:16], in_=scw[:, hx, :])
                        K.v('dve', 'max_index', [scw, topv], [topi], out=topi[:, hx, 8:16], in_max=topv[:, hx, 8:16],
                            in_values=scw[:, hx, :])
                    K.v('dve', 'tensor_copy', [topi], [topf], out=topf[:], in_=topi[:])
                    tvv = topv[:].rearrange("p (h x) k -> p h x k", x=2)
                    tff = topf[:].rearrange("p (h x) k -> p h x k", x=2)
                    K.v('dve', 'tensor_tensor', [topv], [cand], out=cand[:].rearrange("p h (a b) -> p h a b", a=16),
                        in0=tvv[:, :, 0, :].unsqueeze(3).to_broadcast([128, 8, 16, 16]),
                        in1=tvv[:, :, 1, :].unsqueeze(2).to_broadcast([128, 8, 16, 16]), op=ALU.add)
                    K.v('dve', 'tensor_scalar', [topf], [topf], out=tff[:, :, 0, :], in0=tff[:, :, 0, :], scalar1=128.0,
                        scalar2=None, op0=ALU.mult)
                    K.v('dve', 'tensor_tensor', [topf], [cidx], out=cidx[:].rearrange("p h (a b) -> p h a b", a=16),
                        in0=tff[:, :, 0, :].unsqueeze(3).to_broadcast([128, 8, 16, 16]),
                        in1=tff[:, :, 1, :].unsqueeze(2).to_broadcast([128, 8, 16, 16]), op=ALU.add)
                    for hh in range(8):
                        K.v('dve', 'max', [cand], [tv], out=tv[:, hh, 0:8], in_=cand[:, hh, :])
                        candw = scw[:].rearrange("p (h a) n -> p h (a n)", h=8)
                        K.v('dve', 'match_replace', [cand, tv], [scw], out=candw[:, hh, :], in_to_replace=tv[:, hh, 0:8],
                            in_values=cand[:, hh, :], imm_value=-1e30)
                        K.v('dve', 'max', [scw], [tv], out=tv[:, hh, 8:16], in_=candw[:, hh, :])
                        for kh in range(2):
                            ks = slice(kh * 8, kh * 8 + 8)
                            K.v('dve', 'tensor_tensor', [cand, tv], [eq], out=eq[:],
                                in0=cand[:, hh, :].unsqueeze(1).to_broadcast([128, 8, 256]),
                                in1=tv[:, hh, ks].unsqueeze(2).to_broadcast([128, 8, 256]), op=ALU.is_equal)
                            K.v('dve', 'tensor_tensor', [eq, cidx], [eq], out=eq[:], in0=eq[:],
                                in1=cidx[:, hh, :].unsqueeze(1).to_broadcast([128, 8, 256]), op=ALU.mult)
                            K.v('dve', 'tensor_reduce', [eq], [idxf], out=idxf[:, hh * 16 + kh * 8:hh * 16 + kh * 8 + 8],
                                in_=eq[:], axis=AX.X, op=ALU.add)
                    if l > 0:
                        K.v('dve', 'tensor_scalar', [idxf], [idxf], out=idxf[:], in0=idxf[:], scalar1=float(l * 16384),
                            scalar2=None, op0=ALU.add)
                    K.v('dve', 'tensor_scalar', [idxf], [idxf], out=idxf[:], in0=idxf[:], scalar1=float(l * 16384),
                        scalar2=float(l * 16384 + 16383), op0=ALU.max, op1=ALU.min)
                    K.v('dve', 'tensor_copy', [idxf], [idx], out=idx[:], in_=idxf[:])

                def stage_R2(t):
                    gw = gw2[t % 2]
                    K.v('dve', 'tensor_tensor', [tv], [gw], out=gw[:], in0=tv[:],
                        in1=tv[:, :, 0:1].to_broadcast([128, 8, 16]), op=ALU.subtract)
                    K.act(gw[:], gw[:], AF.Exp, [gw], [gw])
                    K.v('dve', 'tensor_reduce', [gw], [zz], out=zz[:], in_=gw[:], axis=AX.X, op=ALU.add)
                    K.v('dve', 'reciprocal', [zz], [zz], out=zz[:], in_=zz[:])
                    K.v('dve', 'tensor_tensor', [gw, zz], [gw], out=gw[:], in0=gw[:],
                        in1=zz[:].unsqueeze(2).to_broadcast([128, 8, 16]), op=ALU.mult)

                def stage_U(t):
                    idx, gw, ga = idx2[t % 2], gw2[t % 2], ga2[t % 2]
                    for col in range(128):
                        si = next_slot()
                        u_ = ring[si]
                        gat(u_, pu, idx, col, si)
                        K.v('dve', 'scalar_tensor_tensor', [u_, h32], [u_, aa], out=u_[:], in0=u_[:], scalar=1.0, in1=h32[:],
                            op0=ALU.mult, op1=ALU.mult, accum_out=aa[:, col:col + 1])
                    K.v('dve', 'tensor_tensor', [aa], [t1], out=t1[:], in0=aa[:], in1=aa[:], op=ALU.mult)
                    K.v('dve', 'tensor_scalar', [t1], [t1], out=t1[:], in0=t1[:], scalar1=0.044715, scalar2=1.0,
                        op0=ALU.mult, op1=ALU.add)
                    K.v('dve', 'tensor_tensor', [t1, aa], [t1], out=t1[:], in0=t1[:], in1=aa[:], op=ALU.mult)
                    K.act(t2[:], t1[:], AF.Tanh, [t1], [t2], scale=0.7978845608028654)
                    K.v('dve', 'tensor_scalar', [t2], [t2], out=t2[:], in0=t2[:], scalar1=1.0, scalar2=0.5, op0=ALU.add,
                        op1=ALU.mult)
                    K.v('dve', 'tensor_tensor', [t2, aa], [t2], out=t2[:], in0=t2[:], in1=aa[:], op=ALU.mult)
                    K.v('dve', 'tensor_tensor', [t2, gw], [ga], out=ga[:], in0=t2[:], in1=gw[:].rearrange("p h k -> p (h k)"),
                        op=ALU.mult)

                def stage_V(t):
                    idx, ga = idx2[t % 2], ga2[t % 2]
                    for col in range(128):
                        si = next_slot()
                        v_ = ring[si]
                        d_ = dg[col % 4]
                        gat(v_, pv, idx, col, si)
                        K.act(d_[:], self.ident_f[:], AF.Copy, [self.ident_f, ga], [d_], scale=ga[:, col:col + 1])
                        for j in range(2):
                            K.mm(pV[j][:], d_[:], v_[:, j * 512:(j + 1) * 512], col == 0, col == 127, [d_, v_], [pV[j]])

                def stage_F(t):
                    x = x2[t % 2]
                    t0 = t * 128
                    for j in range(2):
                        sl = slice(j * 512, (j + 1) * 512)
                        K.v('dve', 'tensor_tensor', [pV[j], G2], [acc], out=acc[:, sl], in0=pV[j][:], in1=G2[:, sl], op=ALU.mult)
                    K.v('dve', 'tensor_tensor', [acc, x], [x], out=x[:], in0=x[:], in1=acc[:], op=ALU.add)
                    if not last:
                        K.dma(S['xres_' + s][t0:t0 + 128, :], x[:], [x], [S['xres_' + s]])
                    else:
                        K.act(acc[:], x[:], AF.Square, [x], [acc, ss2], accum_out=ss2[:, 1:2])
                        self.rstd(ss2, 1, D)
                        K.v('dve', 'scalar_tensor_tensor', [x, ss2, fng], [acc], out=acc[:], in0=x[:], scalar=ss2[:, 1:2],
                            in1=fng[:], op0=ALU.mult, op1=ALU.mult)
                        K.dma(O['y_' + s][t0:t0 + 128, :], acc[:], [acc], [O['y_' + s]])

                nt = self.T[s] // 128
                stage_R(0)
                stage_R2(0)
                stage_U(0)
                for t in range(nt):
                    if t + 1 < nt:
                        stage_R(t + 1)
                    stage_V(t)
                    if t + 1 < nt:
                        stage_R2(t + 1)
                    stage_F(t)
                    if t + 1 < nt:
                        stage_U(t + 1)
            K.barrier()


Prog.phase_C = _phase_C
Prog.phase_C1 = _phase_C1
Prog.phase_C2 = _phase_C2


_CACHE = {}


def _build():
    if 'nc' not in _CACHE:
        nc = bass.Bass("TRN2", target_bir_lowering=False)
        P = Prog(nc, depth=DEPTH)
        P.build()
        _CACHE['nc'] = nc
        _CACHE['consts'] = make_consts()
    return _CACHE['nc'], _CACHE['consts']


def kernel(**inputs):
    inp = {k: np.asarray(v) for k, v in inputs.items()}
    nc, consts = _build()
    in_maps = [prep_core_inputs(inp, c, consts) for c in range(NCORES)]
    res = run_bass_kernel_spmd(nc, in_maps, core_ids=list(range(NCORES)))
    R = res.results
    cat = lambda name: np.concatenate([np.asarray(R[c][name]) for c in range(NCORES)], axis=0)
    B = NCORES * NPSEQ
    y_p = cat('y_p').reshape(B, PSEQ, D)
    y_s = np.stack([np.asarray(R[c]['y_s']) for c in range(NCORES)], axis=0)
    na_k = cat('o_nak').reshape(B, DEPTH, PSEQ, 4, 64)
    na_v = cat('o_nav').reshape(B, DEPTH, PSEQ, 4, 64)
    mC = cat('o_C').reshape(B, DEPTH, 2, 4, 64, 64)
    mn = cat('o_n').reshape(B, DEPTH, 2, 4, 64)
    mm = cat('o_m').reshape(B, DEPTH, 2, 4)
    sw_k = cat('o_swk').reshape(B, DEPTH, PSEQ, 2, 64)
    sw_v = cat('o_swv').reshape(B, DEPTH, PSEQ, 2, 64)
    ckv = cat('o_ckv').reshape(B, DEPTH, PSEQ, 128)
    kr = cat('o_kr').reshape(B, DEPTH, PSEQ, 32)
    outs = (y_p, y_s, na_k, na_v, mC, mn, mm, sw_k, sw_v, ckv, kr)
    return tuple(np.ascontiguousarray(o, dtype=np.float32) for o in outs)
```

```python
import numpy as np
import ml_dtypes
from contextlib import ExitStack
import concourse.bass as bass
import concourse.mybir as mybir
from concourse.bass_utils import run_bass_kernel_spmd

F32 = mybir.dt.float32
F32R = mybir.dt.float32r
BF16 = mybir.dt.bfloat16
I32 = mybir.dt.int32
U32 = mybir.dt.uint32
AF = mybir.ActivationFunctionType
ALU = mybir.AluOpType
AX = mybir.AxisListType

D = 1024
DEPTH = 4
NCORES = 8
PSEQ = 256
NPSEQ = 4
TP = NPSEQ * PSEQ
TS = 4096
PAST = 512
INW = 6768
NG = 2672
EPS = 1e-6

O_NAQ, O_NAK, O_NAV = 0, 256, 512
O_MLQ, O_MLK, O_MLV, O_MLO, O_MLI, O_MLF = 768, 1024, 1280, 1536, 1792, 1800
O_SWQ, O_SWK, O_SWV = 1808, 2064, 2192
O_CQ, O_CKV, O_KR = 2320, 2512, 2640
O_GATE = 2672


class Dummy:
    def __getitem__(self, k):
        return self

    def __getattr__(self, n):
        return self

    def __call__(self, *a, **k):
        return self


class Ten:
    def __init__(self, t, multi=False):
        self.t = t
        self.multi = multi
        self.w = {}
        self.r = {}

    def __getitem__(self, k):
        return self.t[k]

    def ap(self):
        return self.t[:]

    def reset(self):
        self.w = {}
        self.r = {}


class Builder:
    ENG = ['pe', 'act', 'dve', 'pool', 'sp']

    def __init__(self, nc, ndma=40):
        self.nc = nc
        self.h = {'pe': nc.tensor, 'act': nc.scalar, 'dve': nc.vector, 'pool': nc.gpsimd, 'sp': nc.sync}
        self.ndma = ndma
        self.dram = []
        self.need = set()
        self.uid = 0
        self.es = ExitStack()
        self.sem = {e: self.es.enter_context(nc.semaphore("s_" + e)) for e in self.ENG}
        self.dsem = [self.es.enter_context(nc.semaphore("d_%d" % i)) for i in range(ndma + 24)]

    def begin(self, dry):
        self.dry = dry
        self.opi = 0
        self.seq = {e: 0 for e in self.ENG}
        self.sig = {e: 0 for e in self.ENG}
        self.seen = {e: {} for e in self.ENG}
        self.last = {}
        self.last_pb = 0
        self.dval = [0] * (self.ndma + 24)
        self.drr = 0
        self.nwait = 0
        for t in self.dram:
            t.reset()

    def run(self, fn):
        self.begin(True)
        fn(self)
        nops = self.opi
        self.begin(False)
        fn(self)
        assert self.opi == nops, (self.opi, nops)

    def dr(self, name, shape, dtype, kind="Internal"):
        t = Ten(self.nc.dram_tensor(name, list(shape), dtype, kind=kind), multi=True)
        self.dram.append(t)
        return t

    def sb(self, es, name, shape, dtype):
        if self.dry:
            return Ten(Dummy())
        self.uid += 1
        return Ten(es.enter_context(self.nc.sbuf_tensor("sb%d_%s" % (self.uid, name), list(shape), dtype)))

    def ps(self, es, name, shape, dtype):
        if self.dry:
            return Ten(Dummy())
        self.uid += 1
        return Ten(es.enter_context(self.nc.psum_tensor("ps%d_%s" % (self.uid, name), list(shape), dtype)))

    def _wait(self, eng, key, ev):
        if ev[0] == 'c':
            if self.seen[eng].get(key, 0) >= ev[2]:
                return
            self.seen[eng][key] = ev[2]
            if self.dry:
                self.need.add(ev[3])
            else:
                assert ev[4] is not None
                self.h[eng].wait_ge(self.sem[ev[1]], ev[4])
                self.nwait += 1
        else:
            if self.seen[eng].get(key, 0) >= ev[2]:
                return
            self.seen[eng][key] = ev[2]
            if not self.dry:
                self.h[eng].wait_ge(self.dsem[ev[1]], ev[2])
                self.nwait += 1

    def op(self, eng, reads, writes, fn, dma=False, slot=None):
        deps = []
        for b in reads:
            for k, ev in b.w.items():
                deps.append((k, ev, 'raw'))
        for b in writes:
            if not b.multi:
                for k, ev in b.w.items():
                    deps.append((k, ev, 'waw'))
            for k, ev in b.r.items():
                deps.append((k, ev, 'war'))
        for k, ev, kind in deps:
            if (not dma) and ev[0] == 'c' and ev[1] == eng and kind != 'raw':
                continue
            self._wait(eng, k, ev)
        opidx = self.opi
        self.opi += 1
        if dma:
            if slot is not None:
                idx = self.ndma + slot
            else:
                idx = self.drr
                self.drr = (self.drr + 1) % self.ndma
                if self.dval[idx] > 0:
                    self._wait(eng, ('d', idx), ('d', idx, self.dval[idx]))
            self.dval[idx] += 16
            ev = ('d', idx, self.dval[idx])
            key = ('d', idx)
            if not self.dry:
                fn().then_inc(self.dsem[idx], 16)
        else:
            self.seq[eng] += 1
            sigval = None
            if not self.dry:
                inst = fn()
                if opidx in self.need:
                    self.sig[eng] += 1
                    sigval = self.sig[eng]
                    inst.then_inc(self.sem[eng], 1)
            ev = ('c', eng, self.seq[eng], opidx, sigval)
            key = eng
            self.last[eng] = ev
        for b in reads:
            b.r[key] = ev
        for b in writes:
            if b.multi:
                b.w[key] = ev
            else:
                b.w = {key: ev}
                b.r = {}
        return ev

    def barrier(self):
        for f in self.ENG:
            for e in self.ENG:
                if e != f and e in self.last:
                    self._wait(f, e, self.last[e])
            for idx in range(self.ndma + 24):
                if self.dval[idx] > 0:
                    self._wait(f, ('d', idx), ('d', idx, self.dval[idx]))

    def finish(self):
        for idx in range(self.ndma + 24):
            if self.dval[idx] > 0:
                self._wait('sp', ('d', idx), ('d', idx, self.dval[idx]))

    def dma(self, out, in_, reads, writes, q='sp', **kw):
        return self.op(q, reads, writes, lambda: self.h[q].dma_start(out=out, in_=in_, **kw), dma=True)

    def mm(self, out, lhsT, rhs, start, stop, reads, writes, pbase=0):
        if getattr(self, 'last_pb', 0) != pbase and 'pe' in self.last:
            self._wait('pe', 'pe_ser', self.last['pe'])
        self.last_pb = pbase
        return self.op('pe', reads, writes,
                       lambda: self.nc.tensor.matmul(out, lhsT=lhsT, rhs=rhs, start=start, stop=stop))

    def tr(self, out, in_, ident, reads, writes):
        return self.op('pe', reads, writes, lambda: self.nc.tensor.transpose(out, in_, ident))

    def act(self, out, in_, func, reads, writes, **kw):
        return self.op('act', reads, writes, lambda: self.nc.scalar.activation(out=out, in_=in_, func=func, **kw))

    def v(self, eng, name, reads, writes, *a, **kw):
        return self.op(eng, reads, writes, lambda: getattr(self.h[eng], name)(*a, **kw))


def bcast_rows(ap, n):
    return ap.to_broadcast([n] + list(ap.shape[1:]))


def make_consts():
    c = {}
    return make_consts_B(_make_consts_A(c))


def _make_consts_A(c):
    c['ident_bf'] = np.eye(128, dtype=np.float32).astype(ml_dtypes.bfloat16)
    c['ident_f'] = np.eye(128, dtype=np.float32)
    p = np.arange(128)[:, None]
    f = np.arange(128)[None, :]
    c['tri_le'] = (p <= f).astype(np.float32)
    c['tri_ge'] = (p >= f).astype(np.float32)
    pos = np.arange(TS)
    rows, cols = pos // 64, pos % 64
    def table(half):
        fr = 10000.0 ** (-np.arange(half, dtype=np.float32) / half)
        ar = rows[:, None].astype(np.float32) * fr[None, :]
        ac = cols[:, None].astype(np.float32) * fr[None, :]
        cs = np.concatenate([np.cos(ar), np.cos(ac)], axis=1)
        sn = np.concatenate([np.sin(ar), np.sin(ac)], axis=1)
        return np.concatenate([cs, sn], axis=1).astype(np.float32)
    c['rope16'] = table(16)
    c['rope8'] = table(8)
    return c


CONST_SPECS = {
    'ident_bf': ([128, 128], BF16), 'ident_f': ([128, 128], F32),
    'tri_le': ([128, 128], F32), 'tri_ge': ([128, 128], F32),
    'rope16': ([TS, 64], F32), 'rope8': ([TS, 32], F32),
}

IN_SPECS = {
    'xp': ([TP, D], F32), 'xs': ([TS, D], F32),
    'cnak': ([DEPTH, PAST, 256], F32), 'cnav': ([DEPTH, PAST, 256], F32),
    'cswk': ([DEPTH, PAST, 128], F32), 'cswv': ([DEPTH, PAST, 128], F32),
    'cckv': ([DEPTH, PAST, 128], F32), 'ckr': ([DEPTH, PAST, 32], F32),
    'stC': ([DEPTH, 8, 64, 64], F32), 'stn': ([DEPTH, 8, 64], F32), 'stm': ([DEPTH, 8], F32),
    'cvec': ([128, 16], F32),
    'w_mod': ([DEPTH, D, 6 * D], F32), 'b_mod': ([DEPTH, 6 * D], F32),
    'norm1_g': ([DEPTH, D], F32), 'norm2_g': ([DEPTH, D], F32),
    'w_in': ([DEPTH, D, INW], F32), 'b_in': ([DEPTH, INW], F32),
    'na_rpb': ([DEPTH, 60, 31], F32), 'sw_sink': ([DEPTH, 4], F32),
    'mla_q_norm': ([DEPTH, 192], F32), 'w_uq': ([DEPTH, 192, 384], F32),
    'mla_kv_norm': ([DEPTH, 128], F32), 'w_uk': ([DEPTH, 128, 256], F32), 'w_uv': ([DEPTH, 128, 256], F32),
    'w_branch': ([DEPTH, 1024, D], F32), 'w_out': ([DEPTH, D, D], F32),
    'peer_wq': ([DEPTH, D, D], F32), 'peer_keys': ([DEPTH, 16, 128, 64], F32),
    'peer_u': ([DEPTH * 16384, D], F32), 'peer_v': ([DEPTH * 16384, D], F32),
    'final_norm_g': ([1, D], F32),
}

OUT_SPECS = {
    'y_p': ([TP, D], F32), 'y_s': ([TS, D], F32),
    'o_nak': ([NPSEQ, DEPTH, PSEQ, 256], F32), 'o_nav': ([NPSEQ, DEPTH, PSEQ, 256], F32),
    'o_C': ([NPSEQ, DEPTH, 8, 64, 64], F32), 'o_n': ([NPSEQ, DEPTH, 8, 64], F32), 'o_m': ([NPSEQ, DEPTH, 8], F32),
    'o_swk': ([NPSEQ, DEPTH, PSEQ, 128], F32), 'o_swv': ([NPSEQ, DEPTH, PSEQ, 128], F32),
    'o_ckv': ([NPSEQ, DEPTH, PSEQ, 128], F32), 'o_kr': ([NPSEQ, DEPTH, PSEQ, 32], F32),
}

def scratch_specs():
    sp = {}
    for s, T in (('p', TP), ('s', TS)):
        sp['xres_' + s] = ([T, D], F32)
        sp['xmid_' + s] = ([T, D], F32)
        sp['featT_' + s] = ([12, 128, T], BF16)
        sp['m64T_' + s] = ([8, 64, T], BF16)
        sp['m32T_' + s] = ([5, 32, T], BF16)
        sp['vaug_' + s] = ([T, 14 * 65], BF16)
        sp['mlx_' + s] = ([T, 528], F32)
        sp['oT_' + s] = ([8, 128, T], BF16)
    sp['modrow'] = ([DEPTH, 2, 6 * D], F32)
    sp['pad2'] = ([60, 128], F32)
    return sp


class Prog:
    def __init__(self, nc, depth=DEPTH, phases=('mod', 'A', 'B', 'C'), dbg=(), skip=()):
        self.skip = skip
        self.nc = nc
        self.depth = depth
        self.phases = phases
        self.K = Builder(nc)
        K = self.K
        self.I = {n: K.dr(n, s, d, kind="ExternalInput") for n, (s, d) in IN_SPECS.items()}
        self.C = {n: K.dr(n, s, d, kind="ExternalInput") for n, (s, d) in CONST_SPECS.items()}
        self.O = {n: K.dr(n, s, d, kind="ExternalOutput") for n, (s, d) in OUT_SPECS.items()}
        self.S = {}
        for n, (s, d) in scratch_specs().items():
            self.S[n] = K.dr(n, s, d, kind=("ExternalOutput" if n in dbg else "Internal"))
        self.T = {'p': TP, 's': TS}

    def build(self):
        self.K.run(self.main)

    def main(self, K):
        with ExitStack() as es:
            self.ident_bf = K.sb(es, "ident_bf", [128, 128], BF16)
            self.ident_f = K.sb(es, "ident_f", [128, 128], F32)
            self.ones_f = K.sb(es, "ones_f", [128, 128], F32)
            K.dma(self.ident_bf[:], self.C['ident_bf'][:, :], [self.C['ident_bf']], [self.ident_bf])
            K.dma(self.ident_f[:], self.C['ident_f'][:, :], [self.C['ident_f']], [self.ident_f])
            K.v('dve', 'memset', [], [self.ones_f], self.ones_f[:], 1.0)
            for l in range(self.depth):
                if 'mod' in self.phases:
                    self.phase_mod(l)
                    K.barrier()
                if 'A' in self.phases:
                    self.phase_A(l)
                    K.barrier()
                if 'B' in self.phases:
                    self.phase_B(l)
                    K.barrier()
                if 'C' in self.phases:
                    self.phase_C(l)
                    K.barrier()
            K.barrier()
            K.finish()

    def phase_mod(self, l):
        K, I = self.K, self.I
        with ExitStack() as es:
            cv = K.sb(es, "cv", [128, 16], F32)
            sm = K.sb(es, "sm", [128, 16], F32)
            bm = K.sb(es, "bm", [1, 6 * D], F32)
            mrow = K.sb(es, "mrow", [2, 6 * D], F32)
            wch = [K.sb(es, "wch%d" % i, [128, 8, 512], F32) for i in range(2)]
            pm = [K.ps(es, "pm%d" % i, [128, 512], F32) for i in range(2)]
            K.dma(cv[:], I['cvec'][:, :], [I['cvec']], [cv])
            K.dma(bm[:], I['b_mod'][l:l + 1, :], [I['b_mod']], [bm])
            K.act(sm[:], cv[:], AF.Silu, [cv], [sm])
            for j in range(12):
                w = wch[j % 2]
                p = pm[j % 2]
                K.dma(w[:], I['w_mod'][l, :, j * 512:(j + 1) * 512].rearrange("(k p) n -> p k n", p=128),
                      [I['w_mod']], [w])
                for k in range(8):
                    K.mm(p[0:2, :], sm[:, k:16:8], w[:, k, :], k == 0, False, [sm, w], [p])
                K.mm(p[0:2, :], self.ones_f[0:1, 0:2], bm[0:1, j * 512:(j + 1) * 512], False, True,
                     [self.ones_f, bm], [p])
                K.act(mrow[:, j * 512:(j + 1) * 512], p[0:2, :], AF.Copy, [p], [mrow])
            K.dma(self.S['modrow'][l, :, :], mrow[:], [mrow], [self.S['modrow']])

    def load_mod(self, es, l, s, which, name):
        K, I = self.K, self.I
        si = 0 if s == 'p' else 1
        out = {}
        for wi in which:
            t = K.sb(es, "%s_%s_%d" % (name, s, wi), [128, D], F32)
            src = self.S['modrow'][l, si:si + 1, wi * D:(wi + 1) * D]
            K.dma(t[:], bcast_rows(src, 128), [self.S['modrow']], [t])
            if wi in (1, 4):
                gsrc = I['norm1_g' if wi == 1 else 'norm2_g']
                g = K.sb(es, "%s_g_%s_%d" % (name, s, wi), [128, D], F32)
                K.dma(g[:], bcast_rows(gsrc[l:l + 1, :], 128), [gsrc], [g])
                K.v('dve', 'scalar_tensor_tensor', [t, g], [t], out=t[:], in0=t[:], scalar=1.0, in1=g[:],
                    op0=ALU.add, op1=ALU.mult)
            out[wi] = t
        return out

    def rstd(self, ss, c, n):
        K = self.K
        K.v('dve', 'tensor_scalar', [ss], [ss], out=ss[:, c:c + 1], in0=ss[:, c:c + 1], scalar1=1.0 / n, scalar2=EPS,
            op0=ALU.mult, op1=ALU.add)
        K.act(ss[:, c:c + 1], ss[:, c:c + 1], AF.Sqrt, [ss], [ss])
        K.v('dve', 'reciprocal', [ss], [ss], out=ss[:, c:c + 1], in_=ss[:, c:c + 1])

    def norm_mod(self, x, A, B, h32, hbf, ss, junk):
        K = self.K
        K.act(junk[:], x[:], AF.Square, [x], [junk, ss], accum_out=ss[:, 0:1])
        self.rstd(ss, 0, D)
        K.v('dve', 'scalar_tensor_tensor', [x, ss, A], [h32], out=h32[:], in0=x[:], scalar=ss[:, 0:1], in1=A[:],
            op0=ALU.mult, op1=ALU.mult)
        if hbf is not None:
            K.v('dve', 'tensor_tensor', [h32, B], [hbf], out=hbf[:], in0=h32[:], in1=B[:], op=ALU.add)
        else:
            K.v('dve', 'tensor_tensor', [h32, B], [h32], out=h32[:], in0=h32[:], in1=B[:], op=ALU.add)

    def load_w_bf(self, dst, src_ap, src_t, stage, rows, cols, kch):
        K = self.K
        K.dma(stage[0:rows, 0:kch, 0:cols], src_ap, [src_t], [stage])
        K.v('pool', 'tensor_copy', [stage], [dst], out=dst, in_=stage[0:rows, 0:kch, 0:cols])

    def phase_A(self, l):
        K, I, S, O, C = self.K, self.I, self.S, self.O, self.C
        with ExitStack() as es:
            winb = K.sb(es, "winb", [128, 8, NG], BF16)
            stage = K.sb(es, "stageA", [128, 8, 512], F32)
            binb = K.sb(es, "binb", [128, NG], F32)
            wuq = K.sb(es, "wuq", [128, 2, 384], BF16)
            wuk = K.sb(es, "wuk", [128, 256], BF16)
            wuv = K.sb(es, "wuv", [128, 256], BF16)
            qng = K.sb(es, "qng", [128, 192], F32)
            kvng = K.sb(es, "kvng", [128, 128], F32)
            x = K.sb(es, "xA", [128, D], F32)
            xA0 = x
            xB = K.sb(es, "xA2", [128, D], F32)
            h32 = K.sb(es, "h32A", [128, D], F32)
            hbf = K.sb(es, "hbfA", [128, D], BF16)
            hT = K.sb(es, "hTA", [128, 8, 128], BF16)
            ss = K.sb(es, "ssA", [128, 4], F32)
            prow = K.sb(es, "prow", [128, NG], F32)
            pb = K.sb(es, "pbA", [128, 12, 128], BF16)
            fst = K.sb(es, "fstA", [128, 12, 128], BF16)
            vaug = K.sb(es, "vaugA", [128, 14, 65], BF16)
            mlx = K.sb(es, "mlxA", [128, 528], F32)
            rt16 = K.sb(es, "rt16", [128, 64], F32)
            rt8 = K.sb(es, "rt8", [128, 32], F32)
            tmp = [K.sb(es, "ropet%d" % i, [128, 192], F32) for i in range(4)]
            qlat = K.sb(es, "qlat", [128, 192], BF16)
            qlT = K.sb(es, "qlT", [128, 2, 128], BF16)
            qm = K.sb(es, "qm", [128, 4, 96], F32)
            qnr = K.sb(es, "qnr", [128, 384], BF16)
            ckn = K.sb(es, "ckn", [128, 128], F32)
            cknb = K.sb(es, "cknb", [128, 128], BF16)
            ckT = K.sb(es, "ckT", [128, 128], BF16)
            krb = K.sb(es, "krb", [128, 32], BF16)
            m64 = K.sb(es, "m64", [64, 8, 128], BF16)
            m32 = K.sb(es, "m32", [32, 5, 128], BF16)
            pT = K.ps(es, "pT", [128, 8, 128], BF16)
            pP = [K.ps(es, "pP%d" % i, [128, 512], F32) for i in range(2)]
            pF = [K.ps(es, "pF%d" % i, [128, 8, 128], BF16) for i in range(2)]
            pM = K.ps(es, "pM", [128, 8, 128], BF16)
            pQ = K.ps(es, "pQ", [128, 512], F32)
            pK = K.ps(es, "pK", [128, 512], F32)

            for j in range(6):
                c0 = j * 512
                w = min(512, NG - c0)
                K.dma(stage[:, :, 0:w], I['w_in'][l, :, c0:c0 + w].rearrange("(k p) n -> p k n", p=128),
                      [I['w_in']], [stage])
                K.v('pool', 'tensor_copy', [stage], [winb], out=winb[:, :, c0:c0 + w], in_=stage[:, :, 0:w])
            K.dma(binb[:], bcast_rows(I['b_in'][l:l + 1, 0:NG], 128), [I['b_in']], [binb])
            K.dma(stage[:, 0, 0:384], I['w_uq'][l, 0:128, :], [I['w_uq']], [stage])
            K.dma(stage[0:64, 1, 0:384], I['w_uq'][l, 128:192, :], [I['w_uq']], [stage])
            K.v('pool', 'tensor_copy', [stage], [wuq], out=wuq[:, 0, :], in_=stage[:, 0, 0:384])
            K.v('pool', 'tensor_copy', [stage], [wuq], out=wuq[0:64, 1, :], in_=stage[0:64, 1, 0:384])
            K.dma(stage[:, 2, 0:256], I['w_uk'][l, :, :], [I['w_uk']], [stage])
            K.dma(stage[:, 3, 0:256], I['w_uv'][l, :, :], [I['w_uv']], [stage])
            K.v('pool', 'tensor_copy', [stage], [wuk], out=wuk[:], in_=stage[:, 2, 0:256])
            K.v('pool', 'tensor_copy', [stage], [wuv], out=wuv[:], in_=stage[:, 3, 0:256])
            K.dma(qng[:], bcast_rows(I['mla_q_norm'][l:l + 1, :], 128), [I['mla_q_norm']], [qng])
            K.dma(kvng[:], bcast_rows(I['mla_kv_norm'][l:l + 1, :], 128), [I['mla_kv_norm']], [kvng])
            K.v('dve', 'memset', [], [vaug], vaug[:], 1.0)
            mods = {s: self.load_mod(es, l, s, [0, 1], "mA") for s in ('p', 's')}

            for s in ('p', 's'):
                T = self.T[s]
                A1, B1 = mods[s][1], mods[s][0]
                xsrc = (I['xp'] if s == 'p' else I['xs']) if l == 0 else S['xres_' + s]
                xs2 = [xA0, xB]
                K.dma(xs2[0][:], xsrc[0:128, :], [xsrc], [xs2[0]])
                for t in range(T // 128):
                    t0 = t * 128
                    x = xs2[t % 2]
                    if t + 1 < T // 128:
                        K.dma(xs2[(t + 1) % 2][:], xsrc[t0 + 128:t0 + 256, :], [xsrc], [xs2[(t + 1) % 2]])
                    if s == 's':
                        K.dma(rt16[:], C['rope16'][t0:t0 + 128, :], [C['rope16']], [rt16])
                        K.dma(rt8[:], C['rope8'][t0:t0 + 128, :], [C['rope8']], [rt8])
                    self.norm_mod(x, A1, B1, h32, hbf, ss, h32)
                    for k in range(8):
                        K.tr(pT[:, k, :], hbf[:, k * 128:(k + 1) * 128], self.ident_bf[:], [hbf, self.ident_bf], [pT])
                    K.act(hT[:], pT[:], AF.Copy, [pT], [hT])
                    for j in range(6):
                        c0 = j * 512
                        w = min(512, NG - c0)
                        p = pP[j % 2]
                        for k in range(8):
                            K.mm(p[:, 0:w], hT[:, k, :], winb[:, k, c0:c0 + w], k == 0, k == 7, [hT, winb], [p])
                        K.v('dve', 'tensor_tensor', [p, binb], [prow], out=prow[:, c0:c0 + w], in0=p[:, 0:w],
                            in1=binb[:, c0:c0 + w], op=ALU.add)
                    if s == 'p':
                        sq, pos = t // 2, (t % 2) * 128
                        for nm, c0, w in (('o_nak', O_NAK, 256), ('o_nav', O_NAV, 256), ('o_swk', O_SWK, 128),
                                          ('o_swv', O_SWV, 128), ('o_kr', O_KR, 32)):
                            K.dma(O[nm][sq, l, pos:pos + 128, :], prow[:, c0:c0 + w], [prow], [O[nm]])
                    if s == 's':
                        X = prow[:, O_SWQ:O_SWQ + 384].rearrange("p (h a b c) -> p h a b c", h=6, a=2, b=2)
                        xa, xb = X[:, :, :, 0, :], X[:, :, :, 1, :]
                        cs = rt16[:, 0:32].rearrange("p (a c) -> p a c", a=2).unsqueeze(1).to_broadcast([128, 6, 2, 16])
                        sn = rt16[:, 32:64].rearrange("p (a c) -> p a c", a=2).unsqueeze(1).to_broadcast([128, 6, 2, 16])
                        tv = [tt[:, 0:192].rearrange("p (h a c) -> p h a c", h=6, a=2) for tt in tmp]
                        self.rope(prow, xa, xb, cs, sn, tv, tmp, [rt16])
                    for g, c0, nh in ((0, O_NAV, 4), (4, O_MLV, 4), (8, O_SWV, 2)):
                        K.v('pool', 'tensor_copy', [prow], [vaug], out=vaug[:, g:g + nh, 0:64],
                            in_=prow[:, c0:c0 + nh * 64].rearrange("p (h d) -> p h d", h=nh))
                    K.act(mlx[:, 0:256], prow[:, O_MLK:O_MLK + 256], AF.Copy, [prow], [mlx], scale=0.125)
                    K.act(mlx[:, 256:528], prow[:, O_MLO:O_MLO + 272], AF.Copy, [prow], [mlx])
                    K.dma(S['mlx_' + s][t0:t0 + 128, :], mlx[:], [mlx], [S['mlx_' + s]])
                    K.act(pb[:, 0:4, :], prow[:, 0:512].rearrange("p (g c) -> p g c", g=4), AF.Copy, [prow], [pb])
                    K.act(pb[:, 4:6, :], prow[:, O_MLQ:O_MLQ + 256].rearrange("p (g c) -> p g c", g=2), AF.Copy,
                          [prow], [pb])
                    K.act(pb[:, 6:8, :], mlx[:, 0:256].rearrange("p (g c) -> p g c", g=2), AF.Copy, [mlx], [pb])
                    K.act(pb[:, 8:11, :], prow[:, O_SWQ:O_SWQ + 384].rearrange("p (g c) -> p g c", g=3), AF.Copy,
                          [prow], [pb])
                    K.act(pb[:, 11, 0:64], prow[:, O_SWK + 64:O_SWK + 128], AF.Copy, [prow], [pb])
                    K.act(pb[:, 11, 64:128], prow[:, O_SWK:O_SWK + 64], AF.Copy, [prow], [pb])
                    for g in range(12):
                        pf = pF[0] if g < 8 else pF[1]
                        K.tr(pf[:, g % 8, :], pb[:, g, :], self.ident_bf[:], [pb, self.ident_bf], [pf])
                    K.v('dve', 'tensor_copy', [pF[0]], [fst], out=fst[:, 0:8, :], in_=pF[0][:])
                    K.v('dve', 'tensor_copy', [pF[1]], [fst], out=fst[:, 8:12, :], in_=pF[1][:, 0:4, :])
                    K.dma(S['featT_' + s][:, :, t0:t0 + 128].rearrange("g p t -> p g t"), fst[:], [fst],
                          [S['featT_' + s]])
                    K.act(tmp[0][:, 0:192], prow[:, O_CQ:O_CQ + 192], AF.Square, [prow], [tmp[0], ss],
                          accum_out=ss[:, 1:2])
                    self.rstd(ss, 1, 192)
                    K.v('dve', 'scalar_tensor_tensor', [prow, ss, qng], [qlat], out=qlat[:],
                        in0=prow[:, O_CQ:O_CQ + 192], scalar=ss[:, 1:2], in1=qng[:], op0=ALU.mult, op1=ALU.mult)
                    K.tr(pM[:, 0, :], qlat[:, 0:128], self.ident_bf[:], [qlat, self.ident_bf], [pM])
                    K.tr(pM[0:64, 1, :], qlat[:, 128:192], self.ident_bf[:], [qlat, self.ident_bf], [pM])
                    K.act(tmp[0][:, 0:128], prow[:, O_CKV:O_CKV + 128], AF.Square, [prow], [tmp[0], ss],
                          accum_out=ss[:, 2:3])
                    self.rstd(ss, 2, 128)
                    K.v('dve', 'scalar_tensor_tensor', [prow, ss, kvng], [ckn], out=ckn[:],
                        in0=prow[:, O_CKV:O_CKV + 128], scalar=ss[:, 2:3], in1=kvng[:], op0=ALU.mult, op1=ALU.mult)
                    if s == 'p':
                        K.dma(O['o_ckv'][sq, l, pos:pos + 128, :], ckn[:], [ckn], [O['o_ckv']])
                    K.act(cknb[:], ckn[:], AF.Copy, [ckn], [cknb])
                    K.tr(pM[:, 2, :], cknb[:], self.ident_bf[:], [cknb, self.ident_bf], [pM])
                    K.v('dve', 'tensor_copy', [pM], [qlT], out=qlT[:, 0, :], in_=pM[:, 0, :])
                    K.v('dve', 'tensor_copy', [pM], [qlT], out=qlT[0:64, 1, :], in_=pM[0:64, 1, :])
                    K.v('dve', 'tensor_copy', [pM], [ckT], out=ckT[:], in_=pM[:, 2, :])
                    K.mm(pQ[:, 0:384], qlT[:, 0, :], wuq[:, 0, :], True, False, [qlT, wuq], [pQ])
                    K.mm(pQ[:, 0:384], qlT[0:64, 1, :], wuq[0:64, 1, :], False, True, [qlT, wuq], [pQ])
                    K.act(qm[:], pQ[:, 0:384].rearrange("p (h c) -> p h c", h=4), AF.Copy, [pQ], [qm])
                    if s == 's':
                        X = qm[:, :, 64:96].rearrange("p h (a b c) -> p h a b c", a=2, b=2)
                        xa, xb = X[:, :, :, 0, :], X[:, :, :, 1, :]
                        cs = rt8[:, 0:16].rearrange("p (a c) -> p a c", a=2).unsqueeze(1).to_broadcast([128, 4, 2, 8])
                        sn = rt8[:, 16:32].rearrange("p (a c) -> p a c", a=2).unsqueeze(1).to_broadcast([128, 4, 2, 8])
                        tv = [tt[:, 0:64].rearrange("p (h a c) -> p h a c", h=4, a=2) for tt in tmp]
                        self.rope(qm, xa, xb, cs, sn, tv, tmp, [rt8])
                        X = prow[:, O_KR:O_KR + 32].rearrange("p (h a b c) -> p h a b c", h=1, a=2, b=2)
                        xa, xb = X[:, :, :, 0, :], X[:, :, :, 1, :]
                        cs1 = rt8[:, 0:16].rearrange("p (a c) -> p a c", a=2).unsqueeze(1)
                        sn1 = rt8[:, 16:32].rearrange("p (a c) -> p a c", a=2).unsqueeze(1)
                        tv = [tt[:, 0:16].rearrange("p (h a c) -> p h a c", h=1, a=2) for tt in tmp]
                        self.rope(prow, xa, xb, cs1, sn1, tv, tmp, [rt8])
                    K.act(qnr[:, 0:256].rearrange("p (h c) -> p h c", h=4), qm[:, :, 0:64], AF.Copy, [qm], [qnr])
                    K.act(qnr[:, 256:384].rearrange("p (h c) -> p h c", h=4), qm[:, :, 64:96], AF.Copy, [qm], [qnr])
                    K.act(krb[:], prow[:, O_KR:O_KR + 32], AF.Copy, [prow], [krb])
                    for hh in range(4):
                        K.tr(pF[0][0:64, hh, :], qnr[:, hh * 64:(hh + 1) * 64], self.ident_bf[:],
                             [qnr, self.ident_bf], [pF[0]])
                        K.tr(pF[1][0:32, hh, :], qnr[:, 256 + hh * 32:256 + (hh + 1) * 32], self.ident_bf[:],
                             [qnr, self.ident_bf], [pF[1]])
                    K.tr(pF[1][0:32, 4, :], krb[:], self.ident_bf[:], [krb, self.ident_bf], [pF[1]])
                    K.v('dve', 'tensor_copy', [pF[0]], [m64], out=m64[:, 0:4, :], in_=pF[0][0:64, 0:4, :])
                    K.v('dve', 'tensor_copy', [pF[1]], [m32], out=m32[:], in_=pF[1][0:32, 0:5, :])
                    for hh in range(4):
                        K.mm(pK[0:64, hh * 128:(hh + 1) * 128], wuk[:, hh * 64:(hh + 1) * 64], ckT[:], True, True,
                             [wuk, ckT], [pK])
                    K.act(m64[:, 4:8, :], pK[0:64, :].rearrange("p (h t) -> p h t", h=4), AF.Copy, [pK], [m64])
                    K.mm(pQ[:, 0:256], ckT[:], wuv[:], True, True, [ckT, wuv], [pQ])
                    K.v('dve', 'tensor_copy', [pQ], [vaug], out=vaug[:, 10:14, 0:64],
                        in_=pQ[:, 0:256].rearrange("p (h d) -> p h d", h=4))
                    K.dma(S['m64T_' + s][:, :, t0:t0 + 128].rearrange("g p t -> p g t"), m64[:], [m64],
                          [S['m64T_' + s]])
                    K.dma(S['m32T_' + s][:, :, t0:t0 + 128].rearrange("g p t -> p g t"), m32[:], [m32],
                          [S['m32T_' + s]])
                    K.dma(S['vaug_' + s][t0:t0 + 128, :], vaug[:].rearrange("p g c -> p (g c)"), [vaug],
                          [S['vaug_' + s]])

    def rope(self, X, xa, xb, cs, sn, tv, tmp, tabs):
        K = self.K
        K.v('dve', 'tensor_tensor', [X] + tabs, [tmp[0]], out=tv[0], in0=xa, in1=cs, op=ALU.mult)
        K.v('dve', 'tensor_tensor', [X] + tabs, [tmp[1]], out=tv[1], in0=xb, in1=sn, op=ALU.mult)
        K.v('dve', 'tensor_tensor', [X] + tabs, [tmp[2]], out=tv[2], in0=xa, in1=sn, op=ALU.mult)
        K.v('dve', 'tensor_tensor', [X] + tabs, [tmp[3]], out=tv[3], in0=xb, in1=cs, op=ALU.mult)
        K.v('dve', 'tensor_tensor', [tmp[0], tmp[1]], [X], out=xa, in0=tv[0], in1=tv[1], op=ALU.subtract)
        K.v('dve', 'tensor_tensor', [tmp[2], tmp[3]], [X], out=xb, in0=tv[2], in1=tv[3], op=ALU.add)


def prep_core_inputs(inp, core, consts):
    b = core
    f = np.ascontiguousarray
    m = {}
    m['xp'] = f(inp['x_prompt'][4 * b:4 * b + 4].reshape(TP, D))
    m['xs'] = f(inp['x_sample'][b])
    m['cnak'] = f(inp['cache_na_k'][b].reshape(DEPTH, PAST, 256))
    m['cnav'] = f(inp['cache_na_v'][b].reshape(DEPTH, PAST, 256))
    m['cswk'] = f(inp['cache_swa_k'][b].reshape(DEPTH, PAST, 128))
    m['cswv'] = f(inp['cache_swa_v'][b].reshape(DEPTH, PAST, 128))
    m['cckv'] = f(inp['cache_mla_ckv'][b])
    m['ckr'] = f(inp['cache_mla_krope'][b])
    m['stC'] = f(inp['state_mlstm_C'][b].reshape(DEPTH, 8, 64, 64))
    m['stn'] = f(inp['state_mlstm_n'][b].reshape(DEPTH, 8, 64))
    m['stm'] = f(inp['state_mlstm_m'][b].reshape(DEPTH, 8))
    cv = np.concatenate([inp['c_ctx'].reshape(8, 128).T, inp['c'][b].reshape(8, 128).T], axis=1)
    m['cvec'] = f(cv.astype(np.float32))
    for n in ('w_mod', 'b_mod', 'norm1_g', 'norm2_g', 'w_in', 'b_in', 'sw_sink', 'mla_q_norm', 'w_uq',
              'mla_kv_norm', 'w_uk', 'w_uv', 'w_out', 'peer_wq'):
        m[n] = inp[n]
    m['peer_u'] = inp['peer_u'].reshape(DEPTH * 16384, D)
    m['peer_v'] = inp['peer_v'].reshape(DEPTH * 16384, D)
    m['na_rpb'] = inp['na_rpb'].reshape(DEPTH, 60, 31)
    m['w_branch'] = inp['w_branch'].reshape(DEPTH, 1024, D)
    m['peer_keys'] = inp['peer_keys'].reshape(DEPTH, 16, 128, 64)
    m['final_norm_g'] = inp['final_norm_g'].reshape(1, D)
    m.update(consts)
    return m


def make_consts_B(c):
    kcp = np.arange(64)[:, None]
    qc = np.arange(64)[None, :]
    kc = 63 - kcp
    cs = np.clip(qc - 8, 0, 48)
    ok = ((kc >= cs) & (kc < cs + 16)).astype(np.float32)
    c['na_ok8'] = (ok * 8.0).astype(np.float32)
    c['na_neg'] = ((ok - 1.0) * 240000.0).astype(np.float32)
    jlo = np.zeros((64, 128), np.float32)
    jhi = np.zeros((64, 128), np.float32)
    for k in range(64):
        jlo[k, 63 - k] = 1.0
        jhi[k, 127 - k] = 1.0
    c['jlo'] = jlo.astype(ml_dtypes.bfloat16)
    c['jhi'] = jhi.astype(ml_dtypes.bfloat16)
    mge = (c['tri_ge'] - 1.0) * 240000.0
    mle = (c['tri_le'] - 1.0) * 240000.0
    c['mb_ge2'] = np.concatenate([mge, mge], axis=1).astype(ml_dtypes.bfloat16)
    c['mb_le2'] = np.concatenate([mle, mle], axis=1).astype(ml_dtypes.bfloat16)
    return c


CONST_SPECS.update({
    'na_ok8': ([64, 64], F32), 'na_neg': ([64, 64], F32), 'jlo': ([64, 128], BF16), 'jhi': ([64, 128], BF16),
    'mb_ge2': ([128, 256], BF16), 'mb_le2': ([128, 256], BF16),
})


def _load_tok(self, dst, dst_ap_fn, src, col0, col1, tok0, nchunks, step=8):
    K = self.K
    for c0 in range(0, nchunks, step):
        n = min(step, nchunks - c0)
        K.dma(dst_ap_fn(c0, n), src[tok0 + c0 * 128:tok0 + (c0 + n) * 128, col0:col1].rearrange("(c p) f -> p c f", p=128),
              [src], [dst])


def _pipeline(steps, s_fn, rest_fn):
    n = len(steps)
    if n == 0:
        return
    s_fn(steps[0], 0)
    for i in range(n):
        if i + 1 < n:
            s_fn(steps[i + 1], (i + 1) % 2)
        rest_fn(steps[i], i % 2)


def _phase_B(self, l):
    K = self.K
    for nm in ('mix_na', 'mix_sw', 'mix_mla', 'mix_ml'):
        if '_' + nm in self.skip:
            continue
        getattr(self, nm)(l)
        K.barrier()


def _finish_heads(self, acc, nh, rec, ob, extra_den=None, npart=128):
    K = self.K
    rv = rec[0:npart, 0:nh]
    if extra_den is not None:
        K.v('dve', 'tensor_tensor', [acc[0], extra_den], [rec], out=rv, in0=acc[1][:, :, 64], in1=extra_den[0:npart, 0:nh],
            op=ALU.add)
        K.v('dve', 'reciprocal', [rec], [rec], out=rv, in_=rv)
    else:
        K.v('dve', 'reciprocal', [acc[0]], [rec], out=rv, in_=acc[1][:, :, 64])
    K.v('dve', 'tensor_tensor', [acc[0], rec], [ob[0]], out=ob[1], in0=acc[1][:, :, 0:64],
        in1=rv.unsqueeze(2).to_broadcast([npart, nh, 64]), op=ALU.mult)


def _mix_na(self, l):
    K, I, S, C = self.K, self.I, self.S, self.C
    sc = 0.125
    with ExitStack() as es:
        qk = K.sb(es, "naqk", [128, 4, PSEQ], BF16)
        va = K.sb(es, "nava", [128, 2, 260], BF16)
        E = [K.sb(es, "naE%d" % i, [128, 256], BF16) for i in range(2)]
        rec = K.sb(es, "narec", [128, 4], F32)
        ob = K.sb(es, "naob", [128, 256], BF16)
        ot = K.sb(es, "naot", [128, 2, 128], BF16)
        pS = [K.ps(es, "napS%d" % i, [128, 512], F32) for i in range(2)]
        pA = [K.ps(es, "napA%d" % i, [128, 512], F32) for i in range(2)]
        pO = K.ps(es, "napO", [128, 8, 128], BF16)
        for sq in range(NPSEQ):
            b0 = sq * PSEQ
            K.dma(qk[:], S['featT_p'][0:4, :, b0:b0 + PSEQ].rearrange("g p t -> p g t"), [S['featT_p']], [qk])
            K.dma(va[:], S['vaug_p'][b0:b0 + PSEQ, 0:260].rearrange("(c p) f -> p c f", p=128), [S['vaug_p']], [va])
            step = 0
            for h in range(4):
                g, hb = h // 2, (h % 2) * 64
                for c in range(2):
                    p = pS[step % 2]
                    e = E[step % 2]
                    step += 1
                    K.mm(p[:, 0:256], qk[hb:hb + 64, 2 + g, c * 128:(c + 1) * 128], qk[hb:hb + 64, g, :], True, True,
                         [qk], [p], pbase=hb)
                    K.act(e[:], p[:, 0:256], AF.Exp, [p], [e], scale=sc)
                    for j in range(2):
                        K.mm(pA[j][:, h * 65:(h + 1) * 65], e[:, j * 128:(j + 1) * 128], va[:, c, h * 65:(h + 1) * 65],
                             c == 0, c == 1, [e, va], [pA[j]])
            for j in range(2):
                accv = pA[j][:, 0:260].rearrange("p (h c) -> p h c", h=4)
                self.finish_heads((pA[j], accv), 4, rec, (ob, ob[:].rearrange("p (h c) -> p h c", h=4)))
                for g in range(2):
                    K.tr(pO[:, g, :], ob[:, g * 128:(g + 1) * 128], self.ident_bf[:], [ob, self.ident_bf], [pO])
                K.v('dve', 'tensor_copy', [pO], [ot], out=ot[:], in_=pO[:, 0:2, :])
                t0 = b0 + j * 128
                K.dma(S['oT_p'][0:2, :, t0:t0 + 128].rearrange("g p t -> p g t"), ot[:], [ot], [S['oT_p']])
    K.barrier()
    if 'na_s' in self.skip:
        return
    with ExitStack() as es:
        qk = K.sb(es, "nsqk", [128, 4, TS], BF16)
        VA = K.sb(es, "nsVA", [128, 32, 260], BF16)
        VB = K.sb(es, "nsVB", [128, 31, 260], BF16)
        ctok = K.sb(es, "nsctok", [128, 4, 256], F32)
        ctb = K.sb(es, "nsctb", [128, 4, 256], BF16)
        ckT = K.sb(es, "nsckT", [128, 2, 512], BF16)
        cva = K.sb(es, "nscva", [128, 4, 260], BF16)
        rp = K.sb(es, "nsrp", [60, 31], F32)
        rrev = K.sb(es, "nsrrev", [60, 128], F32)
        Tpp = K.sb(es, "nsTpp", [64, 60, 64], F32)
        ok8 = K.sb(es, "nsok8", [64, 64], F32)
        neg = K.sb(es, "nsneg", [64, 64], F32)
        BLK = K.sb(es, "nsBLK", [64, 15, 4, 64], BF16)
        jlo = K.sb(es, "nsjlo", [64, 128], BF16)
        jhi = K.sb(es, "nsjhi", [64, 128], BF16)
        E = [K.sb(es, "nsE%d" % i, [128, 256], BF16) for i in range(2)]
        rec = K.sb(es, "nsrec", [128, 4], F32)
        ob = K.sb(es, "nsob", [64, 256], BF16)
        ot = K.sb(es, "nsot", [128, 2, 128], BF16)
        pS = [K.ps(es, "nspS%d" % i, [128, 512], F32) for i in range(2)]
        pA = [K.ps(es, "nspA%d" % i, [128, 512], F32) for i in range(2)]
        pO = K.ps(es, "nspO", [128, 8, 128], BF16)
        pT = K.ps(es, "nspT", [128, 8, 128], BF16)
        K.dma(qk[:], S['featT_s'][0:4, :, :].rearrange("g p t -> p g t"), [S['featT_s']], [qk])
        self.load_tok(VA, lambda c0, n: VA[:, c0:c0 + n, :], S['vaug_s'], 0, 260, 0, 32)
        self.load_tok(VB, lambda c0, n: VB[:, c0:c0 + n, :], S['vaug_s'], 0, 260, 64, 31)
        K.dma(ctok[:], I['cnak'][l].rearrange("(c p) f -> p c f", p=128), [I['cnak']], [ctok])
        K.act(ctb[:], ctok[:], AF.Copy, [ctok], [ctb])
        for c in range(4):
            for g in range(2):
                K.tr(pT[:, c * 2 + g, :], ctb[:, c, g * 128:(g + 1) * 128], self.ident_bf[:], [ctb, self.ident_bf], [pT])
        K.v('dve', 'tensor_copy', [pT], [ckT], out=ckT[:].rearrange("p g (c t) -> p c g t", c=4),
            in_=pT[:].rearrange("p (c g) t -> p c g t", c=4))
        K.v('dve', 'memset', [], [cva], cva[:], 1.0)
        K.dma(ctok[:], I['cnav'][l].rearrange("(c p) f -> p c f", p=128), [I['cnav']], [ctok])
        K.v('dve', 'tensor_copy', [ctok], [cva], out=cva[:].rearrange("p c (h e) -> p c h e", h=4)[:, :, :, 0:64],
            in_=ctok[:].rearrange("p c (h e) -> p c h e", h=4))
        if 'na_s1' in self.skip:
            return
        K.dma(rp[:], I['na_rpb'][l], [I['na_rpb']], [rp])
        K.v('dve', 'memset', [], [rrev], rrev[:], 0.0)
        K.v('dve', 'tensor_copy', [rp], [rrev], out=rrev[:, 48:79], in_=rp[:, ::-1])
        K.dma(S['pad2'][:, :], rrev[:], [rrev], [S['pad2']])
        src = bass.AP(S['pad2'][:, :].tensor, 0, [[1, 64], [128, 60], [1, 64]])
        K.dma(Tpp[:], src, [S['pad2']], [Tpp])
        K.dma(ok8[:], C['na_ok8'][:, :], [C['na_ok8']], [ok8])
        K.dma(neg[:], C['na_neg'][:, :], [C['na_neg']], [neg])
        K.dma(jlo[:], C['jlo'][:, :], [C['jlo']], [jlo])
        K.dma(jhi[:], C['jhi'][:, :], [C['jhi']], [jhi])
        K.v('dve', 'tensor_tensor', [Tpp, ok8], [Tpp], out=Tpp[:], in0=Tpp[:],
            in1=ok8[:].unsqueeze(1).to_broadcast([64, 60, 64]), op=ALU.mult)
        K.v('dve', 'tensor_tensor', [Tpp, neg], [BLK], out=BLK[:].rearrange("p r h q -> p h r q"),
            in0=Tpp[:].rearrange("p (h r) q -> p h r q", h=4),
            in1=neg[:].unsqueeze(1).unsqueeze(1).to_broadcast([64, 4, 15, 64]), op=ALU.add)
        if 'na_s2' in self.skip:
            return
        steps = []
        for r in range(64):
            rs = min(max(r - 4, 0), 56)
            o = rs - r
            chunks = []
            for jj in range(4):
                k0 = rs * 64 + jj * 128
                if k0 % 128 == 0:
                    vv = VA[:, k0 // 128, :]
                else:
                    vv = VB[:, (k0 - 64) // 128, :]
                chunks.append(('loc', k0, vv, (2 * jj + o + 7, 2 * jj + 1 + o + 7)))
            for c in range(4):
                chunks.append(('ctx', c, cva[:, c, :], None))
            for ci, ch in enumerate(chunks):
                steps.append((r, ci, len(chunks)) + ch)

        def s_fn(st, b):
            r, ci, nch, kind, k0, vv, drs = st
            p = pS[b]
            for h in (0, 2, 1, 3):
                g, hb = h // 2, (h % 2) * 64
                if kind == 'loc':
                    kk = qk[hb:hb + 64, 2 + g, k0:k0 + 128]
                    rd = [qk]
                else:
                    kk = ckT[hb:hb + 64, g, k0 * 128:(k0 + 1) * 128]
                    rd = [qk, ckT]
                K.mm(p[:, h * 64:(h + 1) * 64], kk, qk[hb:hb + 64, g, r * 64:(r + 1) * 64], h == 0,
                     (kind == 'ctx'), rd, [p], pbase=hb)
            if kind == 'loc':
                K.mm(p[:, 0:256], jlo[:], BLK[:, drs[0], :, :].rearrange("p h q -> p (h q)"), False, False,
                     [jlo, BLK], [p])
                K.mm(p[:, 0:256], jhi[:], BLK[:, drs[1], :, :].rearrange("p h q -> p (h q)"), False, True,
                     [jhi, BLK], [p])

        def rest_fn(st, b):
            r, ci, nch, kind, k0, vv, drs = st
            p, e = pS[b], E[b]
            acc = pA[r % 2]
            K.act(e[:], p[:, 0:256], AF.Exp, [p], [e], scale=sc)
            rdv = [e, VA, VB, cva]
            for h in range(4):
                K.mm(acc[0:64, h * 65:(h + 1) * 65], e[:, h * 64:(h + 1) * 64], vv[:, h * 65:(h + 1) * 65],
                     ci == 0 and h == 0, ci == nch - 1, rdv, [acc])
            if ci == nch - 1:
                accv = acc[0:64, 0:260].rearrange("p (h c) -> p h c", h=4)
                self.finish_heads((acc, accv), 4, rec, (ob, ob[:].rearrange("p (h c) -> p h c", h=4)), npart=64)
                for g in range(2):
                    K.tr(pO[:, g, (r % 2) * 64:(r % 2) * 64 + 64], ob[:, g * 128:(g + 1) * 128], self.ident_bf[0:64, 0:64],
                         [ob, self.ident_bf], [pO])
                if r % 2 == 1:
                    K.v('dve', 'tensor_copy', [pO], [ot], out=ot[:], in_=pO[:, 0:2, :])
                    t0 = (r // 2) * 128
                    K.dma(S['oT_s'][0:2, :, t0:t0 + 128].rearrange("g p t -> p g t"), ot[:], [ot], [S['oT_s']])

        _pipeline(steps, s_fn, rest_fn)


Prog.phase_B = _phase_B
Prog.load_tok = _load_tok
Prog.finish_heads = _finish_heads
Prog.mix_na = _mix_na


def _stub(self, l):
    pass


for _n in ('mix_sw', 'mix_mla', 'mix_ml'):
    if not hasattr(Prog, _n):
        setattr(Prog, _n, _stub)


def _mix_sw(self, l):
    K, I, S, C = self.K, self.I, self.S, self.C
    sc = 0.125
    with ExitStack() as es:
        sk = K.sb(es, "swsk", [128, 4], F32)
        K.dma(sk[:], bcast_rows(I['sw_sink'][l:l + 1, :], 128), [I['sw_sink']], [sk])
        K.act(sk[:], sk[:], AF.Exp, [sk], [sk])
        rec = K.sb(es, "swrec", [128, 4], F32)
        ob = K.sb(es, "swob", [128, 256], BF16)
        ot = K.sb(es, "swot", [128, 2, 128], BF16)
        pS = [K.ps(es, "swpS%d" % i, [128, 512], F32) for i in range(2)]
        pA = [K.ps(es, "swpA%d" % i, [128, 512], F32) for i in range(2)]
        pO = K.ps(es, "swpO", [128, 8, 128], BF16)
        pT = K.ps(es, "swpT", [128, 8, 128], BF16)

        def emit_out(acc, s, t0):
            accv = acc[:, 0:260].rearrange("p (h c) -> p h c", h=4)
            self.finish_heads((acc, accv), 4, rec, (ob, ob[:].rearrange("p (h c) -> p h c", h=4)), extra_den=sk)
            for g in range(2):
                K.tr(pO[:, g, :], ob[:, g * 128:(g + 1) * 128], self.ident_bf[:], [ob, self.ident_bf], [pO])
            K.v('dve', 'tensor_copy', [pO], [ot], out=ot[:], in_=pO[:, 0:2, :])
            K.dma(S['oT_' + s][4:6, :, t0:t0 + 128].rearrange("g p t -> p g t"), ot[:], [ot], [S['oT_' + s]])

        with ExitStack() as es2:
            qk = K.sb(es2, "swqk", [128, 4, PSEQ], BF16)
            va = K.sb(es2, "swva", [128, 2, 130], BF16)
            E = [K.sb(es2, "swE%d" % i, [128, 512], BF16) for i in range(2)]
            step = 0
            for sq in range(NPSEQ):
                b0 = sq * PSEQ
                K.dma(qk[:], S['featT_p'][8:12, :, b0:b0 + PSEQ].rearrange("g p t -> p g t"), [S['featT_p']], [qk])
                K.dma(va[:], S['vaug_p'][b0:b0 + PSEQ, 520:650].rearrange("(c p) f -> p c f", p=128), [S['vaug_p']], [va])
                for g in range(2):
                    for c in range(2):
                        p = pS[step % 2]
                        e = E[step % 2]
                        step += 1
                        for r in range(2):
                            kg = 2 if g == r else 3
                            K.mm(p[:, r * 256:(r + 1) * 256], qk[r * 64:r * 64 + 64, kg, c * 128:(c + 1) * 128],
                                 qk[r * 64:r * 64 + 64, g, :], r == 0, True, [qk], [p], pbase=r * 64)
                        K.act(e[:], p[:], AF.Exp, [p], [e], scale=sc)
                        for r in range(2):
                            for j in range(2):
                                hh = g * 2 + r
                                K.mm(pA[j][:, hh * 65:(hh + 1) * 65], e[:, r * 256 + j * 128:r * 256 + (j + 1) * 128],
                                     va[:, c, g * 65:(g + 1) * 65], (g == 0 and c == 0 and r == 0), c == 1, [e, va], [pA[j]])
                for j in range(2):
                    emit_out(pA[j], 'p', b0 + j * 128)
        K.barrier()
        with ExitStack() as es2:
            qk = K.sb(es2, "swsqk", [128, 4, TS], BF16)
            VA = K.sb(es2, "swsVA", [128, 32, 130], BF16)
            ctok = K.sb(es2, "swctok", [128, 4, 128], F32)
            ctb = K.sb(es2, "swctb", [128, 4, 2, 128], BF16)
            ck = K.sb(es2, "swck", [128, 2, 512], BF16)
            cva = K.sb(es2, "swcva", [128, 4, 130], BF16)
            mge = K.sb(es2, "swmge", [128, 256], BF16)
            mle = K.sb(es2, "swmle", [128, 256], BF16)
            E = [K.sb(es2, "swsE%d" % i, [128, 256], BF16) for i in range(2)]
            K.dma(qk[:], S['featT_s'][8:12, :, :].rearrange("g p t -> p g t"), [S['featT_s']], [qk])
            self.load_tok(VA, lambda c0, n: VA[:, c0:c0 + n, :], S['vaug_s'], 520, 650, 0, 32)
            K.dma(mge[:], C['mb_ge2'][:, :], [C['mb_ge2']], [mge])
            K.dma(mle[:], C['mb_le2'][:, :], [C['mb_le2']], [mle])
            K.dma(ctok[:], I['cswk'][l].rearrange("(c p) f -> p c f", p=128), [I['cswk']], [ctok])
            K.act(ctb[:, :, 0, :], ctok[:], AF.Copy, [ctok], [ctb])
            K.act(ctb[:, :, 1, 0:64], ctok[:, :, 64:128], AF.Copy, [ctok], [ctb])
            K.act(ctb[:, :, 1, 64:128], ctok[:, :, 0:64], AF.Copy, [ctok], [ctb])
            for c in range(4):
                for v in range(2):
                    K.tr(pT[:, c * 2 + v, :], ctb[:, c, v, :], self.ident_bf[:], [ctb, self.ident_bf], [pT])
            K.v('dve', 'tensor_copy', [pT], [ck], out=ck[:].rearrange("p v (c t) -> p c v t", c=4),
                in_=pT[:].rearrange("p (c v) t -> p c v t", c=4))
            K.v('dve', 'memset', [], [cva], cva[:], 1.0)
            K.dma(ctok[:], I['cswv'][l].rearrange("(c p) f -> p c f", p=128), [I['cswv']], [ctok])
            K.v('dve', 'tensor_copy', [ctok], [cva], out=cva[:].rearrange("p c (h e) -> p c h e", h=2)[:, :, :, 0:64],
                in_=ctok[:].rearrange("p c (h e) -> p c h e", h=2))
            steps = []
            NB = TS // 128
            for n in range(NB):
                for g in range(2):
                    chunks = []
                    if n > 0:
                        chunks.append(('loc', n - 1, mge))
                    chunks.append(('loc', n, None))
                    if n < NB - 1:
                        chunks.append(('loc', n + 1, mle))
                    for c in range(4):
                        chunks.append(('ctx', c, None))
                    for ci, ch in enumerate(chunks):
                        steps.append((n, g, ci, len(chunks)) + ch)

            def s_fn(st, b):
                n, g, ci, nch, kind, c, mask = st
                p = pS[b]
                for r in range(2):
                    v = 0 if g == r else 1
                    if kind == 'loc':
                        kk = qk[r * 64:r * 64 + 64, 2 + v, c * 128:(c + 1) * 128]
                    else:
                        kk = ck[r * 64:r * 64 + 64, v, c * 128:(c + 1) * 128]
                    K.mm(p[:, r * 128:(r + 1) * 128], kk, qk[r * 64:r * 64 + 64, g, n * 128:(n + 1) * 128],
                         r == 0, mask is None, [qk, ck], [p], pbase=r * 64)
                if mask is not None:
                    K.mm(p[:, 0:256], self.ident_bf[:], mask[:], False, True, [self.ident_bf, mask], [p])

            def rest_fn(st, b):
                n, g, ci, nch, kind, c, mask = st
                p, e = pS[b], E[b]
                acc = pA[n % 2]
                K.act(e[:], p[:, 0:256], AF.Exp, [p], [e], scale=sc)
                vv = VA[:, c, :] if kind == 'loc' else cva[:, c, :]
                for r in range(2):
                    hh = g * 2 + r
                    K.mm(acc[:, hh * 65:(hh + 1) * 65], e[:, r * 128:(r + 1) * 128], vv[:, g * 65:(g + 1) * 65],
                         (g == 0 and ci == 0 and r == 0), ci == nch - 1, [e, VA, cva], [acc])
                if g == 1 and ci == nch - 1:
                    emit_out(acc, 's', n * 128)

            _pipeline(steps, s_fn, rest_fn)


def _mix_mla(self, l):
    K, I, S, C = self.K, self.I, self.S, self.C
    sc = 96.0 ** -0.5
    with ExitStack() as es:
        rec = K.sb(es, "mlrec", [128, 4], F32)
        ob = K.sb(es, "mlob", [128, 256], BF16)
        ot = K.sb(es, "mlot", [128, 2, 128], BF16)
        E = [K.sb(es, "mlaE%d" % i, [128, 512], BF16) for i in range(2)]
        pS = [K.ps(es, "mlpS%d" % i, [128, 512], F32) for i in range(2)]
        pA = [K.ps(es, "mlpA%d" % i, [128, 512], F32) for i in range(4)]
        pO = K.ps(es, "mlpO", [128, 8, 128], BF16)
        knT = K.sb(es, "mlknT", [64, 4, TS + PAST], BF16)
        krT = K.sb(es, "mlkrT", [32, TS + PAST], BF16)
        VA = K.sb(es, "mlVA", [128, 36, 260], BF16)
        qn = K.sb(es, "mlqn", [64, 4, 512], BF16)
        qr = K.sb(es, "mlqr", [32, 4, 512], BF16)

        def run(s, q0, nq, k0, nkc, extra_chunks):
            K.dma(qn[:, :, 0:nq], S['m64T_' + s][0:4, :, q0:q0 + nq].rearrange("g p t -> p g t"), [S['m64T_' + s]], [qn])
            K.dma(qr[:, :, 0:nq], S['m32T_' + s][0:4, :, q0:q0 + nq].rearrange("g p t -> p g t"), [S['m32T_' + s]], [qr])
            chunks = list(range(nkc)) + list(extra_chunks)
            steps = [(h, ci, c) for h in range(4) for ci, c in enumerate(chunks)]

            def s_fn(st, b):
                h, ci, c = st
                p = pS[b]
                K.mm(p[:, 0:nq], knT[:, h, c * 128:(c + 1) * 128], qn[:, h, 0:nq], True, False, [knT, qn], [p])
                K.mm(p[:, 0:nq], krT[:, c * 128:(c + 1) * 128], qr[:, h, 0:nq], False, True, [krT, qr], [p])

            def rest_fn(st, b):
                h, ci, c = st
                p, e = pS[b], E[b]
                K.act(e[:, 0:nq], p[:, 0:nq], AF.Exp, [p], [e], scale=sc)
                for j in range(nq // 128):
                    K.mm(pA[j][:, h * 65:(h + 1) * 65], e[:, j * 128:(j + 1) * 128], VA[:, c, h * 65:(h + 1) * 65],
                         ci == 0, ci == len(chunks) - 1, [e, VA], [pA[j]])

            _pipeline(steps, s_fn, rest_fn)
            for j in range(nq // 128):
                accv = pA[j][:, 0:260].rearrange("p (h c) -> p h c", h=4)
                self.finish_heads((pA[j], accv), 4, rec, (ob, ob[:].rearrange("p (h c) -> p h c", h=4)))
                for g in range(2):
                    K.tr(pO[:, g, :], ob[:, g * 128:(g + 1) * 128], self.ident_bf[:], [ob, self.ident_bf], [pO])
                K.v('dve', 'tensor_copy', [pO], [ot], out=ot[:], in_=pO[:, 0:2, :])
                t0 = q0 + j * 128
                K.dma(S['oT_' + s][6:8, :, t0:t0 + 128].rearrange("g p t -> p g t"), ot[:], [ot], [S['oT_' + s]])

        for sq in range(NPSEQ):
            b0 = sq * PSEQ
            K.dma(knT[:, :, 0:PSEQ], S['m64T_p'][4:8, :, b0:b0 + PSEQ].rearrange("g p t -> p g t"), [S['m64T_p']], [knT])
            K.dma(krT[:, 0:PSEQ], S['m32T_p'][4, :, b0:b0 + PSEQ], [S['m32T_p']], [krT])
            K.dma(VA[:, 0:2, :], S['vaug_p'][b0:b0 + PSEQ, 650:910].rearrange("(c p) f -> p c f", p=128),
                  [S['vaug_p']], [VA])
            run('p', b0, PSEQ, 0, 2, [])
        K.barrier()
        with ExitStack() as es2:
            stage = K.sb(es2, "mlstage", [128, 2, 256], F32)
            wuk = K.sb(es2, "mlwuk", [128, 256], BF16)
            wuv = K.sb(es2, "mlwuv", [128, 256], BF16)
            ctok = K.sb(es2, "mlctok", [128, 4, 160], F32)
            ctb = K.sb(es2, "mlctb", [128, 4, 160], BF16)
            ccT = K.sb(es2, "mlccT", [128, 512], BF16)
            pT = pO
            K.dma(knT[:, :, 0:TS], S['m64T_s'][4:8, :, :].rearrange("g p t -> p g t"), [S['m64T_s']], [knT])
            K.dma(krT[:, 0:TS], S['m32T_s'][4, :, :], [S['m32T_s']], [krT])
            self.load_tok(VA, lambda c0, n: VA[:, c0:c0 + n, :], S['vaug_s'], 650, 910, 0, 32)
            K.dma(stage[:, 0, :], I['w_uk'][l, :, :], [I['w_uk']], [stage])
            K.dma(stage[:, 1, :], I['w_uv'][l, :, :], [I['w_uv']], [stage])
            K.v('dve', 'tensor_copy', [stage], [wuk], out=wuk[:], in_=stage[:, 0, :])
            K.v('dve', 'tensor_copy', [stage], [wuv], out=wuv[:], in_=stage[:, 1, :])
            K.dma(ctok[:, :, 0:128], I['cckv'][l].rearrange("(c p) f -> p c f", p=128), [I['cckv']], [ctok])
            K.dma(ctok[:, :, 128:160], I['ckr'][l].rearrange("(c p) f -> p c f", p=128), [I['ckr']], [ctok])
            K.act(ctb[:], ctok[:], AF.Copy, [ctok], [ctb])
            for c in range(4):
                K.tr(pT[:, c, :], ctb[:, c, 0:128], self.ident_bf[:], [ctb, self.ident_bf], [pT])
                K.tr(pT[0:32, 4 + c, :], ctb[:, c, 128:160], self.ident_bf[:], [ctb, self.ident_bf], [pT])
            K.v('dve', 'tensor_copy', [pT], [ccT], out=ccT[:].rearrange("p (c t) -> p c t", c=4), in_=pT[:, 0:4, :])
            K.v('dve', 'tensor_copy', [pT], [krT], out=krT[:, TS:TS + PAST].rearrange("p (c t) -> p c t", c=4),
                in_=pT[0:32, 4:8, :])
            for h in range(4):
                p = pS[h % 2]
                K.mm(p[0:64, :], wuk[:, h * 64:(h + 1) * 64], ccT[:], True, True, [wuk, ccT], [p])
                K.act(knT[:, h, TS:TS + PAST], p[0:64, :], AF.Copy, [p], [knT])
            K.v('dve', 'memset', [], [VA], VA[:, 32:36, :], 1.0)
            for c in range(4):
                p = pS[c % 2]
                K.mm(p[:, 0:256], ccT[:, c * 128:(c + 1) * 128], wuv[:], True, True, [ccT, wuv], [p])
                K.v('dve', 'tensor_copy', [p], [VA], out=VA[:, 32 + c, :].rearrange("p (h e) -> p h e", h=4)[:, :, 0:64],
                    in_=p[:, 0:256].rearrange("p (h e) -> p h e", h=4))
            for qg in range(TS // 512):
                run('s', qg * 512, 512, 0, 32, [32, 33, 34, 35])


Prog.mix_sw = _mix_sw
Prog.mix_mla = _mix_mla


def _ml_part2(self, d, c, cs, dsl, tri, wcol, WT, PT, pS, ktok, kw, pA, VA, qk, Cbf, pC, Cdec, Cst, ecol, den, htmp, hsum):
    K = self.K
    K.v('dve', 'tensor_tensor', [tri[d], wcol], [WT[d]], out=WT[d][:],
        in0=tri[d][:].unsqueeze(1).to_broadcast([128, 4, 128]),
        in1=wcol[:, c, dsl].unsqueeze(2).to_broadcast([128, 4, 128]), op=ALU.mult)
    K.v('dve', 'tensor_tensor', [pS[d], WT[d]], [PT[d]], out=PT[d][:],
        in0=pS[d][:].rearrange("p (h t) -> p h t", h=4), in1=WT[d][:], op=ALU.mult)
    K.v('dve', 'tensor_tensor', [ktok, wcol], [kw[d]], out=kw[d][:],
        in0=ktok[:, c, :].rearrange("p (h e) -> p h e", h=4),
        in1=wcol[:, c, dsl].unsqueeze(2).to_broadcast([128, 4, 64]), op=ALU.mult)
    for h in range(4):
        K.mm(pA[d][:, h * 65:(h + 1) * 65], PT[d][:, h, :], VA[:, c, h * 65:(h + 1) * 65], h == 0, False,
             [PT[d], VA], [pA[d]])
    for h in (0, 2, 1, 3):
        g, hb = h // 2, (h % 2) * 64
        K.mm(pA[d][:, h * 65:(h + 1) * 65], qk[hb:hb + 64, g, cs], Cbf[d][hb:hb + 64, g, :], False, True,
             [qk, Cbf[d]], [pA[d]], pbase=hb)
    for g in range(2):
        for ab in range(2):
            K.mm(pC[d][:, (g * 2 + ab) * 65:(g * 2 + ab + 1) * 65],
                 kw[d][:, 2 * g:2 * g + 2, :].rearrange("p h e -> p (h e)"),
                 VA[:, c, (2 * g + ab) * 65:(2 * g + ab + 1) * 65], (g == 0 and ab == 0), True,
                 [kw[d], VA], [pC[d]])
    pcv = pC[d][:, 0:260].rearrange("p (g a e) -> p g a e", g=2, a=2)
    for hf in range(2):
        ps_ = slice(hf * 64, hf * 64 + 64)
        K.v('dve', 'tensor_tensor', [Cdec[d], pC[d]], [Cst[d]], out=Cst[d][ps_, :, :],
            in0=Cdec[d][ps_, :, :], in1=pcv[ps_, :, hf, :], op=ALU.add)
    accv = pA[d][:, 0:260].rearrange("p (h e) -> p h e", h=4)
    K.v('dve', 'tensor_copy', [pA[d]], [den[d]], out=den[d][:], in_=accv[:, :, 64])
    K.v('dve', 'scalar_tensor_tensor', [den[d]], [den[d]], out=den[d][:], in0=den[d][:], scalar=-1.0,
        in1=den[d][:], op0=ALU.mult, op1=ALU.max)
    K.v('dve', 'tensor_tensor', [den[d], ecol], [den[d]], out=den[d][:], in0=den[d][:],
        in1=ecol[:, c, dsl], op=ALU.max)
    K.v('dve', 'reciprocal', [den[d]], [den[d]], out=den[d][:], in_=den[d][:])
    K.v('dve', 'tensor_tensor', [pA[d], den[d]], [htmp[d]], out=htmp[d][:], in0=accv[:, :, 0:64],
        in1=den[d][:].unsqueeze(2).to_broadcast([128, 4, 64]), op=ALU.mult)
    K.v('pool', 'tensor_tensor', [hsum, htmp[d]], [hsum], out=hsum[:, c, :], in0=hsum[:, c, :],
        in1=htmp[d][:].rearrange("p h e -> p (h e)"), op=ALU.add)


def _mix_ml(self, l):
    K, I, S, C, O = self.K, self.I, self.S, self.C, self.O
    for s in ('p', 's'):
        seqs = [(sq, sq * PSEQ, PSEQ // 128) for sq in range(NPSEQ)] if s == 'p' else [(0, 0, TS // 128)]
        N = seqs[0][2]
        with ExitStack() as es:
            tle = K.sb(es, "mtle", [128, 128], F32)
            tge = K.sb(es, "mtge", [128, 128], F32)
            K.dma(tle[:], C['tri_le'][:, :], [C['tri_le']], [tle])
            K.dma(tge[:], C['tri_ge'][:, :], [C['tri_ge']], [tge])
            tri = (tle, tge)
            qk = K.sb(es, "mqk", [128, 4, N * 128], BF16)
            VA = K.sb(es, "mVA", [128, N, 260], BF16)
            stg = K.sb(es, "mstg", [128, 4, 256], F32)
            ktok = K.sb(es, "mktok", [128, N, 256], BF16)
            og = K.sb(es, "mog", [128, N, 256], BF16)
            hsum = K.sb(es, "mhsum", [128, N, 256], F32)
            G = K.sb(es, "mG", [128, N, 16], F32)
            sp = K.sb(es, "msp", [128, N, 8], F32)
            nb = K.sb(es, "mnb", [128, N, 8], F32)
            aa = K.sb(es, "maa", [128, N, 8], F32)
            tot = K.sb(es, "mtot", [128, N, 8], F32)
            amx = K.sb(es, "mamx", [128, N, 8], F32)
            Mc = K.sb(es, "mMc", [128, N, 8], F32)
            mpv = K.sb(es, "mmpv", [128, N, 8], F32)
            wcol = K.sb(es, "mwcol", [128, N, 8], F32)
            dcy = K.sb(es, "mdcy", [128, N, 8], F32)
            ecol = K.sb(es, "mecol", [128, N, 8], F32)
            mcur = K.sb(es, "mmcur", [128, 8], F32)
            amr = K.sb(es, "mamr", [4, 2, N], F32)
            Dm = K.sb(es, "mDm", [4, 2, N, 4], F32)
            Cst = [K.sb(es, "mCst%d" % d, [128, 2, 65], F32) for d in range(2)]
            Cdec = [K.sb(es, "mCdec%d" % d, [128, 2, 65], F32) for d in range(2)]
            Cbf = [K.sb(es, "mCbf%d" % d, [128, 2, 65], BF16) for d in range(2)]
            WT = [K.sb(es, "mWT%d" % d, [128, 4, 128], F32) for d in range(2)]
            PT = [K.sb(es, "mPT%d" % d, [128, 4, 128], BF16) for d in range(2)]
            kw = [K.sb(es, "mkw%d" % d, [128, 4, 64], BF16) for d in range(2)]
            den = [K.sb(es, "mden%d" % d, [128, 4], F32) for d in range(2)]
            htmp = [K.sb(es, "mht%d" % d, [128, 4, 64], F32) for d in range(2)]
            ss = K.sb(es, "mss", [128, 4], F32)
            sq2 = K.sb(es, "msq2", [128, 256], F32)
            ob = K.sb(es, "mob", [128, 256], BF16)
            ot = K.sb(es, "mot", [128, 2, 128], BF16)
            pS = [K.ps(es, "mpS%d" % d, [128, 512], F32) for d in range(2)]
            pA = [K.ps(es, "mpA%d" % d, [128, 512], F32) for d in range(2)]
            pC = [K.ps(es, "mpC%d" % d, [128, 512], F32) for d in range(2)]
            pX = K.ps(es, "mpX", [128, 512], F32)
            pO = K.ps(es, "mpO", [128, 8, 128], BF16)
            for (sq, b0, _) in seqs:
                K.dma(qk[:], S['featT_' + s][4:8, :, b0:b0 + N * 128].rearrange("g p t -> p g t"), [S['featT_' + s]], [qk])
                self.load_tok(VA, lambda c0, n: VA[:, c0:c0 + n, :], S['vaug_' + s], 260, 520, b0, N)
                self.load_tok(G, lambda c0, n: G[:, c0:c0 + n, :], S['mlx_' + s], 512, 528, b0, N)
                for c0 in range(0, N, 4):
                    n = min(4, N - c0)
                    self.load_tok(stg, lambda cc, nn: stg[:, 0:n, :], S['mlx_' + s], 0, 256, b0 + c0 * 128, n)
                    K.act(ktok[:, c0:c0 + n, :], stg[:, 0:n, :], AF.Copy, [stg], [ktok])
                    self.load_tok(stg, lambda cc, nn: stg[:, 0:n, :], S['mlx_' + s], 256, 512, b0 + c0 * 128, n)
                    K.act(og[:, c0:c0 + n, :], stg[:, 0:n, :], AF.Sigmoid, [stg], [og])
                K.v('dve', 'memset', [], [hsum], hsum[:], 0.0)
                K.act(sp[:], G[:, :, 8:16], AF.Exp, [G], [sp], scale=-1.0)
                K.act(sp[:], sp[:], AF.Ln, [sp], [sp], bias=1.0)
                for d in range(2):
                    K.mm(pX[:, d * N * 4:(d + 1) * N * 4].rearrange("p (c j) -> p c j", j=4), tri[d][:],
                         sp[:, :, d * 4:(d + 1) * 4], d == 0, d == 1, [tri[d], sp], [pX])
                K.v('dve', 'tensor_copy', [pX], [nb], out=nb[:].rearrange("p c (d j) -> p d c j", d=2),
                    in_=pX[:, 0:N * 8].rearrange("p (d c j) -> p d c j", d=2, c=N))
                K.v('dve', 'tensor_tensor', [G, nb], [aa], out=aa[:], in0=G[:, :, 0:8], in1=nb[:], op=ALU.add)
                K.mm(pX[:, 0:N * 8], self.ones_f[:], sp[:].rearrange("p c j -> p (c j)"), True, True, [self.ones_f, sp], [pX])
                K.v('dve', 'tensor_copy', [pX], [tot], out=tot[:], in_=pX[:, 0:N * 8].rearrange("p (c j) -> p c j", c=N))
                for d in range(2):
                    for c0 in range(0, N, 4):
                        n = min(4, N - c0)
                        for ci in range(n):
                            c = c0 + ci
                            K.mm(pX[0:4, ci * 128:(ci + 1) * 128], G[:, c, d * 4:(d + 1) * 4], self.ident_f[:], ci == 0,
                                 False, [G, self.ident_f], [pX])
                            K.mm(pX[0:4, ci * 128:(ci + 1) * 128], sp[:, c, d * 4:(d + 1) * 4], tri[d][:], False, True,
                                 [sp, tri[d]], [pX])
                        K.v('dve', 'tensor_reduce', [pX], [amr], out=amr[:, d, c0:c0 + n],
                            in_=pX[0:4, 0:n * 128].rearrange("p (c t) -> p c t", c=n), axis=AX.X, op=ALU.max)
                K.v('dve', 'tensor_tensor', [amr, self.ident_f], [Dm], out=Dm[:],
                    in0=amr[:].unsqueeze(3).to_broadcast([4, 2, N, 4]),
                    in1=self.ident_f[0:4, 0:4].unsqueeze(1).unsqueeze(1).to_broadcast([4, 2, N, 4]), op=ALU.mult)
                K.mm(pX[:, 0:N * 8], self.ones_f[0:4, :], Dm[:].rearrange("p d c j -> p (d c j)"), True, True,
                     [self.ones_f, Dm], [pX])
                K.v('dve', 'tensor_copy', [pX], [amx], out=amx[:].rearrange("p c (d j) -> p d c j", d=2),
                    in_=pX[:, 0:N * 8].rearrange("p (d c j) -> p d c j", d=2, c=N))
                if s == 's':
                    K.dma(mcur[:], bcast_rows(I['stm'][l:l + 1, :], 128), [I['stm']], [mcur])
                else:
                    K.v('dve', 'memset', [], [mcur], mcur[:], 0.0)
                for j in range(N):
                    for d in range(2):
                        c = j if d == 0 else N - 1 - j
                        sl = slice(d * 4, d * 4 + 4)
                        K.v('dve', 'tensor_copy', [mcur], [mpv], out=mpv[:, c, sl], in_=mcur[:, sl])
                        K.v('dve', 'tensor_tensor', [mcur, amx], [Mc], out=Mc[:, c, sl], in0=mcur[:, sl], in1=amx[:, c, sl],
                            op=ALU.max)
                        K.v('dve', 'tensor_tensor', [Mc, tot], [mcur], out=mcur[:, sl], in0=Mc[:, c, sl], in1=tot[:, c, sl],
                            op=ALU.subtract)
                if s == 'p':
                    K.dma(O['o_m'][sq, l:l + 1, :], mcur[0:1, :], [mcur], [O['o_m']])
                K.v('dve', 'tensor_tensor', [aa, Mc], [wcol], out=wcol[:], in0=aa[:], in1=Mc[:], op=ALU.subtract)
                K.act(wcol[:], wcol[:], AF.Exp, [wcol], [wcol])
                K.v('dve', 'tensor_tensor', [mpv, Mc], [dcy], out=dcy[:], in0=mpv[:], in1=Mc[:], op=ALU.subtract)
                K.act(dcy[:], dcy[:], AF.Exp, [dcy], [dcy])
                K.v('dve', 'tensor_tensor', [nb, Mc], [ecol], out=ecol[:], in0=nb[:], in1=Mc[:], op=ALU.subtract)
                K.act(ecol[:], ecol[:], AF.Exp, [ecol], [ecol])
                for d in range(2):
                    if s == 's':
                        for h in range(4):
                            g, hb = h // 2, (h % 2) * 64
                            K.dma(Cst[d][hb:hb + 64, g, 0:64], I['stC'][l, d * 4 + h, :, :], [I['stC']], [Cst[d]])
                            K.dma(Cst[d][hb:hb + 64, g, 64:65], I['stn'][l, d * 4 + h:d * 4 + h + 1, :].rearrange("o n -> n o"),
                                  [I['stn']], [Cst[d]])
                    else:
                        K.v('dve', 'memset', [], [Cst[d]], Cst[d][:], 0.0)
                for j in range(N):
                  for part in range(2):
                    for d in range(2):
                        c = j if d == 0 else N - 1 - j
                        cs = slice(c * 128, (c + 1) * 128)
                        dsl = slice(d * 4, d * 4 + 4)
                        if part == 1:
                            self._ml_part2(d, c, cs, dsl, tri, wcol, WT, PT, pS, ktok, kw, pA, VA, qk, Cbf, pC, Cdec, Cst, ecol,
                                           den, htmp, hsum)
                            continue
                        for g in range(2):
                            for hf in range(2):
                                ps_ = slice(hf * 64, hf * 64 + 64)
                                col = d * 4 + 2 * g + hf
                                K.act(Cdec[d][ps_, g, :], Cst[d][ps_, g, :], AF.Copy, [Cst[d], dcy], [Cdec[d]],
                                      scale=dcy[ps_, c, col:col + 1])
                        K.act(Cbf[d][:], Cdec[d][:], AF.Copy, [Cdec[d]], [Cbf[d]])
                        for h in (0, 2, 1, 3):
                            g, hb = h // 2, (h % 2) * 64
                            K.mm(pS[d][:, h * 128:(h + 1) * 128], qk[hb:hb + 64, 2 + g, cs], qk[hb:hb + 64, g, cs], h == 0, True,
                                 [qk], [pS[d]], pbase=hb)
                        continue
                if s == 'p':
                    for d in range(2):
                        for h in range(4):
                            g, hb = h // 2, (h % 2) * 64
                            K.dma(O['o_C'][sq, l, d * 4 + h, :, :], Cst[d][hb:hb + 64, g, 0:64], [Cst[d]], [O['o_C']])
                            K.dma(O['o_n'][sq, l, d * 4 + h:d * 4 + h + 1, :].rearrange("o n -> n o"),
                                  Cst[d][hb:hb + 64, g, 64:65], [Cst[d]], [O['o_n']])
                for c in range(N):
                    K.v('dve', 'tensor_tensor', [hsum], [sq2], out=sq2[:], in0=hsum[:, c, :], in1=hsum[:, c, :], op=ALU.mult)
                    K.v('dve', 'tensor_reduce', [sq2], [ss], out=ss[:], in_=sq2[:].rearrange("p (h e) -> p h e", h=4),
                        axis=AX.X, op=ALU.add)
                    K.v('dve', 'tensor_scalar', [ss], [ss], out=ss[:], in0=ss[:], scalar1=1.0 / 64, scalar2=EPS,
                        op0=ALU.mult, op1=ALU.add)
                    K.act(ss[:], ss[:], AF.Sqrt, [ss], [ss])
                    K.v('dve', 'reciprocal', [ss], [ss], out=ss[:], in_=ss[:])
                    K.v('dve', 'tensor_tensor', [hsum, ss], [sq2], out=sq2[:].rearrange("p (h e) -> p h e", h=4),
                        in0=hsum[:, c, :].rearrange("p (h e) -> p h e", h=4),
                        in1=ss[:].unsqueeze(2).to_broadcast([128, 4, 64]), op=ALU.mult)
                    K.v('dve', 'tensor_tensor', [sq2, og], [ob], out=ob[:], in0=sq2[:], in1=og[:, c, :], op=ALU.mult)
                    for g in range(2):
                        K.tr(pO[:, g, :], ob[:, g * 128:(g + 1) * 128], self.ident_bf[:], [ob, self.ident_bf], [pO])
                    K.v('dve', 'tensor_copy', [pO], [ot], out=ot[:], in_=pO[:, 0:2, :])
                    t0 = b0 + c * 128
                    K.dma(S['oT_' + s][2:4, :, t0:t0 + 128].rearrange("g p t -> p g t"), ot[:], [ot], [S['oT_' + s]])
        K.barrier()


Prog.mix_ml = _mix_ml
Prog._ml_part2 = _ml_part2


def _phase_C(self, l):
    self.phase_C1(l)
    self.K.barrier()
    self.phase_C2(l)


def _phase_C1(self, l):
    K, I, S = self.K, self.I, self.S
    with ExitStack() as es:
        wing = K.sb(es, "c1wing", [128, 8, 4096], BF16)
        wbr = K.sb(es, "c1wbr", [128, 8, 1024], BF16)
        wout = K.sb(es, "c1wout", [128, 8, 1024], BF16)
        stage = K.sb(es, "c1stage", [128, 8, 512], F32)
        bg = K.sb(es, "c1bg", [128, 4096], F32)
        x = K.sb(es, "c1x", [128, D], F32)
        xC0 = x
        xB = K.sb(es, "c1xB", [128, D], F32)
        oTB = K.sb(es, "c1oTB", [128, 8, 128], BF16)
        h32 = K.sb(es, "c1h32", [128, D], F32)
        hbf = K.sb(es, "c1hbf", [128, D], BF16)
        hT = K.sb(es, "c1hT", [128, 8, 128], BF16)
        ss = K.sb(es, "c1ss", [128, 4], F32)
        oTt = K.sb(es, "c1oTt", [128, 8, 128], BF16)
        oTC0 = oTt
        sg = K.sb(es, "c1sg", [128, 1024], F32)
        acc = K.sb(es, "c1acc", [128, 1024], F32)
        accb = K.sb(es, "c1accb", [128, 1024], BF16)
        accT = K.sb(es, "c1accT", [128, 8, 128], BF16)
        pT = K.ps(es, "c1pT", [128, 8, 128], BF16)
        pG = [K.ps(es, "c1pG%d" % i, [128, 512], F32) for i in range(4)]
        pY = [K.ps(es, "c1pY%d" % i, [128, 512], F32) for i in range(2)]
        pM = pG[0:2]
        sgs = [sg, K.sb(es, "c1sg2", [128, 1024], F32)]
        for j in range(8):
            K.dma(stage[:], I['w_in'][l, :, O_GATE + j * 512:O_GATE + (j + 1) * 512].rearrange("(k p) n -> p k n", p=128),
                  [I['w_in']], [stage])
            K.v('pool', 'tensor_copy', [stage], [wing], out=wing[:, :, j * 512:(j + 1) * 512], in_=stage[:])
        for j in range(2):
            K.dma(stage[:], I['w_branch'][l, :, j * 512:(j + 1) * 512].rearrange("(k p) n -> p k n", p=128),
                  [I['w_branch']], [stage])
            K.v('pool', 'tensor_copy', [stage], [wbr], out=wbr[:, :, j * 512:(j + 1) * 512], in_=stage[:])
            K.dma(stage[:], I['w_out'][l, :, j * 512:(j + 1) * 512].rearrange("(k p) n -> p k n", p=128),
                  [I['w_out']], [stage])
            K.v('pool', 'tensor_copy', [stage], [wout], out=wout[:, :, j * 512:(j + 1) * 512], in_=stage[:])
        K.dma(bg[:], bcast_rows(I['b_in'][l:l + 1, O_GATE:INW], 128), [I['b_in']], [bg])
        for s in ('p', 's'):
            with ExitStack() as es2:
                mods = self.load_mod(es2, l, s, [0, 1, 2], "mC1")
                A1, B1, G1 = mods[1], mods[0], mods[2]
                xsrc = (I['xp'] if s == 'p' else I['xs']) if l == 0 else S['xres_' + s]
                xs2, os2 = [xC0, xB], [oTC0, oTB]
                nt1 = self.T[s] // 128

                def load_c1(t):
                    K.dma(xs2[t % 2][:], xsrc[t * 128:t * 128 + 128, :], [xsrc], [xs2[t % 2]])
                    K.dma(os2[t % 2][:], S['oT_' + s][:, :, t * 128:t * 128 + 128].rearrange("g p t -> p g t"),
                          [S['oT_' + s]], [os2[t % 2]])

                load_c1(0)
                for t in range(nt1):
                    t0 = t * 128
                    x, oTt = xs2[t % 2], os2[t % 2]
                    if t + 1 < nt1:
                        load_c1(t + 1)
                    self.norm_mod(x, A1, B1, h32, hbf, ss, h32)
                    for k in range(8):
                        K.tr(pT[:, k, :], hbf[:, k * 128:(k + 1) * 128], self.ident_bf[:], [hbf, self.ident_bf], [pT])
                    K.act(hT[:], pT[:], AF.Copy, [pT], [hT])
                    def gate_mm(n):
                        for j in range(2):
                            c0 = n * 1024 + j * 512
                            pg = pG[(n % 2) * 2 + j]
                            for k in range(8):
                                K.mm(pg[:], hT[:, k, :], wing[:, k, c0:c0 + 512], k == 0, k == 7, [hT, wing], [pg])

                    def branch_mm(n):
                        for j in range(2):
                            for kc in range(2):
                                K.mm(pY[j][:], oTt[:, 2 * n + kc, :], wbr[:, 2 * n + kc, j * 512:(j + 1) * 512], kc == 0,
                                     kc == 1, [oTt, wbr], [pY[j]])

                    def chain(n):
                        sg_ = sgs[n % 2]
                        for j in range(2):
                            c0 = n * 1024 + j * 512
                            pg = pG[(n % 2) * 2 + j]
                            K.v('dve', 'tensor_tensor', [pg, bg], [sg_], out=sg_[:, j * 512:(j + 1) * 512], in0=pg[:],
                                in1=bg[:, c0:c0 + 512], op=ALU.add)
                        K.act(sg_[:], sg_[:], AF.Sigmoid, [sg_], [sg_])
                        for j in range(2):
                            sl = slice(j * 512, (j + 1) * 512)
                            if n == 0:
                                K.v('dve', 'tensor_tensor', [pY[j], sg_], [acc], out=acc[:, sl], in0=pY[j][:], in1=sg_[:, sl],
                                    op=ALU.mult)
                            else:
                                K.v('dve', 'tensor_tensor', [pY[j], sg_], [sg_], out=sg_[:, sl], in0=pY[j][:], in1=sg_[:, sl],
                                    op=ALU.mult)
                                K.v('pool', 'tensor_tensor', [acc, sg_], [acc], out=acc[:, sl], in0=acc[:, sl], in1=sg_[:, sl],
                                    op=ALU.add)

                    gate_mm(0)
                    branch_mm(0)
                    for n in range(4):
                        if n + 1 < 4:
                            gate_mm(n + 1)
                        chain(n)
                        if n + 1 < 4:
                            branch_mm(n + 1)
                    K.act(accb[:], acc[:], AF.Copy, [acc], [accb])
                    for k in range(8):
                        K.tr(pT[:, k, :], accb[:, k * 128:(k + 1) * 128], self.ident_bf[:], [accb, self.ident_bf], [pT])
                    K.act(accT[:], pT[:], AF.Copy, [pT], [accT])
                    for j in range(2):
                        for k in range(8):
                            K.mm(pM[j][:], accT[:, k, :], wout[:, k, j * 512:(j + 1) * 512], k == 0, k == 7, [accT, wout],
                                 [pM[j]])
                        sl = slice(j * 512, (j + 1) * 512)
                        K.v('dve', 'tensor_tensor', [pM[j], G1], [h32], out=h32[:, sl], in0=pM[j][:], in1=G1[:, sl],
                            op=ALU.mult)
                    K.v('dve', 'tensor_tensor', [x, h32], [h32], out=h32[:], in0=x[:], in1=h32[:], op=ALU.add)
                    K.dma(S['xmid_' + s][t0:t0 + 128, :], h32[:], [h32], [S['xmid_' + s]])
            K.barrier()


def _phase_C2(self, l):
    K, I, S, O = self.K, self.I, self.S, self.O
    last = (l == self.depth - 1)
    with ExitStack() as es:
        wq = K.sb(es, "c2wq", [128, 8, 1024], BF16)
        kb = K.sb(es, "c2kb", [128, 16, 64], BF16)
        keysT = K.sb(es, "c2keysT", [128, 8, 128], BF16)
        fng = K.sb(es, "c2fng", [128, D], F32)
        x2 = [K.sb(es, "c2x%d" % i, [128, D], F32) for i in range(2)]
        h32 = K.sb(es, "c2h32", [128, D], F32)
        hbf = K.sb(es, "c2hbf", [128, D], BF16)
        hT = K.sb(es, "c2hT", [128, 8, 128], BF16)
        ss = K.sb(es, "c2ss", [128, 4], F32)
        ss2 = K.sb(es, "c2ss2", [128, 4], F32)
        qT = K.sb(es, "c2qT", [128, 8, 128], BF16)
        sc = K.sb(es, "c2sc", [128, 16, 128], F32)
        scw = K.sb(es, "c2scw", [128, 16, 128], F32)
        topv = K.sb(es, "c2topv", [128, 16, 16], F32)
        topi = K.sb(es, "c2topi", [128, 16, 16], U32)
        topf = K.sb(es, "c2topf", [128, 16, 16], F32)
        cand = K.sb(es, "c2cand", [128, 8, 256], F32)
        cidx = K.sb(es, "c2cidx", [128, 8, 256], F32)
        tv = K.sb(es, "c2tv", [128, 8, 16], F32)
        eq = K.sb(es, "c2eq", [128, 8, 256], F32)
        idxf = K.sb(es, "c2idxf", [128, 128], F32)
        idx2 = [K.sb(es, "c2idx%d" % i, [128, 128], I32) for i in range(2)]
        gw2 = [K.sb(es, "c2gw%d" % i, [128, 8, 16], F32) for i in range(2)]
        ga2 = [K.sb(es, "c2ga%d" % i, [128, 128], F32) for i in range(2)]
        zz = K.sb(es, "c2zz", [128, 8], F32)
        aa = K.sb(es, "c2aa", [128, 128], F32)
        t1 = K.sb(es, "c2t1", [128, 128], F32)
        t2 = K.sb(es, "c2t2", [128, 128], F32)
        acc = K.sb(es, "c2acc", [128, D], F32)
        junk = acc
        dg = [K.sb(es, "c2dg%d" % i, [128, 128], F32) for i in range(4)]
        pT = K.ps(es, "c2pT", [128, 8, 128], BF16)
        pQ = [K.ps(es, "c2pQ%d" % i, [128, 512], F32) for i in range(2)]
        pS = [K.ps(es, "c2pS%d" % i, [128, 512], F32) for i in range(2)]
        pV = [K.ps(es, "c2pV%d" % i, [128, 512], F32) for i in range(2)]
        with ExitStack() as esw:
            stage = K.sb(esw, "c2stage", [128, 8, 512], F32)
            for j in range(2):
                K.dma(stage[:], I['peer_wq'][l, :, j * 512:(j + 1) * 512].rearrange("(k p) n -> p k n", p=128),
                      [I['peer_wq']], [stage])
                K.v('pool', 'tensor_copy', [stage], [wq], out=wq[:, :, j * 512:(j + 1) * 512], in_=stage[:])
            K.dma(stage[:, 0:2, :].rearrange("p a (b d) -> p (a b) d", d=64), I['peer_keys'][l].rearrange("g n d -> n g d"),
                  [I['peer_keys']], [stage])
            K.v('pool', 'tensor_copy', [stage], [kb], out=kb[:], in_=stage[:, 0:2, :].rearrange("p a (b d) -> p (a b) d", d=64))
            K.barrier()
        NR = 20
        ring = [K.sb(es, "c2ring%d" % i, [128, D], F32) for i in range(NR)]
        rc = [0]

        def next_slot():
            i = rc[0] % NR
            rc[0] += 1
            return i

        for hh in range(8):
            K.tr(pT[:, hh, :], kb[:, 2 * hh:2 * hh + 2, :].rearrange("p a d -> p (a d)"), self.ident_bf[:],
                 [kb, self.ident_bf], [pT])
        K.act(keysT[:], pT[:], AF.Copy, [pT], [keysT])
        K.dma(fng[:], bcast_rows(I['final_norm_g'][0:1, :], 128), [I['final_norm_g']], [fng])
        pu, pv = I['peer_u'], I['peer_v']

        def gat(dst, tab, idx, col, slot):
            K.op('pool', [idx, tab], [dst], (lambda: self.nc.gpsimd.indirect_dma_start(
                out=dst[:], out_offset=None, in_=tab[:, :],
                in_offset=bass.IndirectOffsetOnAxis(ap=idx[:, col:col + 1], axis=0))), dma=True, slot=slot)

        for s in ('p', 's'):
            with ExitStack() as es2:
                mods = self.load_mod(es2, l, s, [3, 4, 5], "mC2")
                A2, B2, G2 = mods[4], mods[3], mods[5]

                def stage_R(t):
                    x, idx, gw = x2[t % 2], idx2[t % 2], gw2[t % 2]
                    t0 = t * 128
                    K.dma(x[:], S['xmid_' + s][t0:t0 + 128, :], [S['xmid_' + s]], [x])
                    self.norm_mod(x, A2, B2, h32, None, ss, junk)
                    K.act(hbf[:], h32[:], AF.Copy, [h32], [hbf])
                    for k in range(8):
                        K.tr(pT[:, k, :], hbf[:, k * 128:(k + 1) * 128], self.ident_bf[:], [hbf, self.ident_bf], [pT])
                    K.act(hT[:], pT[:], AF.Copy, [pT], [hT])
                    for j in range(8):
                        pq = pQ[j // 4]
                        for k in range(8):
                            K.mm(pq[:, (j % 4) * 128:(j % 4 + 1) * 128], wq[:, k, j * 128:(j + 1) * 128], hT[:, k, :],
                                 (k == 0 and j % 4 == 0), k == 7, [wq, hT], [pq])
                    for i in range(2):
                        K.act(qT[:, i * 4:(i + 1) * 4, :], pQ[i][:].rearrange("p (j t) -> p j t", j=4), AF.Copy, [pQ[i]], [qT])
                    for half in range(2):
                        for xx in range(2):
                            for h4 in range(4):
                                hh = half * 4 + h4
                                b = h4 // 2
                                cix = (h4 % 2) * 2 + xx
                                K.mm(pS[b][:, cix * 128:(cix + 1) * 128], qT[xx * 64:xx * 64 + 64, hh, :],
                                     keysT[xx * 64:xx * 64 + 64, hh, :], (xx == 0 and h4 % 2 == 0), True, [qT, keysT], [pS[b]],
                                     pbase=xx * 64)
                        for b in range(2):
                            K.act(sc[:, half * 8 + b * 4:half * 8 + b * 4 + 4, :], pS[b][:].rearrange("p (j t) -> p j t", j=4),
                                  AF.Copy, [pS[b]], [sc])
                    for hx in range(16):
                        K.v('dve', 'max', [sc], [topv], out=topv[:, hx, 0:8], in_=sc[:, hx, :])
                        K.v('dve', 'max_index', [sc, topv], [topi], out=topi[:, hx, 0:8], in_max=topv[:, hx, 0:8],
                            in_values=sc[:, hx, :])
                        K.v('dve', 'match_replace', [sc, topv], [scw], out=scw[:, hx, :], in_to_replace=topv[:, hx, 0:8],
                            in_values=sc[:, hx, :], imm_value=-1e30)
                        K.v('dve', 'max', [scw], [topv], out=topv[:, hx, 8:16], in_=scw[:, hx, :])
                        K.v('dve', 'max_index', [scw, topv], [topi], out=topi[:, hx, 8:16], in_max=topv[:, hx, 8:16],
                            in_values=scw[:, hx, :])
                    K.v('dve', 'tensor_copy', [topi], [topf], out=topf[:], in_=topi[:])
                    tvv = topv[:].rearrange("p (h x) k -> p h x k", x=2)
                    tff = topf[:].rearrange("p (h x) k -> p h x k", x=2)
                    K.v('dve', 'tensor_tensor', [topv], [cand], out=cand[:].rearrange("p h (a b) -> p h a b", a=16),
                        in0=tvv[:, :, 0, :].unsqueeze(3).to_broadcast([128, 8, 16, 16]),
                        in1=tvv[:, :, 1, :].unsqueeze(2).to_broadcast([128, 8, 16, 16]), op=ALU.add)
                    K.v('dve', 'tensor_scalar', [topf], [topf], out=tff[:, :, 0, :], in0=tff[:, :, 0, :], scalar1=128.0,
                        scalar2=None, op0=ALU.mult)
                    K.v('dve', 'tensor_tensor', [topf], [cidx], out=cidx[:].rearrange("p h (a b) -> p h a b", a=16),
                        in0=tff[:, :, 0, :].unsqueeze(3).to_broadcast([128, 8, 16, 16]),
                        in1=tff[:, :, 1, :].unsqueeze(2).to_broadcast([128, 8, 16, 16]), op=ALU.add)
                    for hh in range(8):
                        K.v('dve', 'max', [cand], [tv], out=tv[:, hh, 0:8], in_=cand[:, hh, :])
                        candw = scw[:].rearrange("p (h a) n -> p h (a n)", h=8)
                        K.v('dve', 'match_replace', [cand, tv], [scw], out=candw[:, hh, :], in_to_replace=tv[:, hh, 0:8],
                            in_values=cand[:, hh, :], imm_value=-1e30)
                        K.v('dve', 'max', [scw], [tv], out=tv[:, hh, 8:16], in_=candw[:, hh, :])
                        for kh in range(2):
                            ks = slice(kh * 8, kh * 8 + 8)
                            K.v('dve', 'tensor_tensor', [cand, tv], [eq], out=eq[:],
                                in0=cand[:, hh, :].unsqueeze(1).to_broadcast([128, 8, 256]),
                                in1=tv[:, hh, ks].unsqueeze(2).to_broadcast([128, 8, 256]), op=ALU.is_equal)
                            K.v('dve', 'tensor_tensor', [eq, cidx], [eq], out=eq[:], in0=eq[:],
                                in1=cidx[:, hh, :].unsqueeze(1).to_broadcast([128, 8, 256]), op=ALU.mult)
                            K.v('dve', 'tensor_reduce', [eq], [idxf], out=idxf[:, hh * 16 + kh * 8:hh * 16 + kh * 8 + 8],
                                in_=eq[:], axis=AX.X, op=ALU.add)
                    if l > 0:
                        K.v('dve', 'tensor_scalar', [idxf], [idxf], out=idxf[:], in0=idxf[:], scalar1=float(l * 16384),
                            scalar2=None, op0=ALU.add)
                    K.v('dve', 'tensor_scalar', [idxf], [idxf], out=idxf[:], in0=idxf[:], scalar1=float(l * 16384),
                        scalar2=float(l * 16384 + 16383), op0=ALU.max, op1=ALU.min)
                    K.v('dve', 'tensor_copy', [idxf], [idx], out=idx[:], in_=idxf[:])

                def stage_R2(t):
                    gw = gw2[t % 2]
                    K.v('dve', 'tensor_tensor', [tv], [gw], out=gw[:], in0=tv[:],
                        in1=tv[:, :, 0:1].to_broadcast([128, 8, 16]), op=ALU.subtract)
                    K.act(gw[:], gw[:], AF.Exp, [gw], [gw])
                    K.v('dve', 'tensor_reduce', [gw], [zz], out=zz[:], in_=gw[:], axis=AX.X, op=ALU.add)
                    K.v('dve', 'reciprocal', [zz], [zz], out=zz[:], in_=zz[:])
                    K.v('dve', 'tensor_tensor', [gw, zz], [gw], out=gw[:], in0=gw[:],
                        in1=zz[:].unsqueeze(2).to_broadcast([128, 8, 16]), op=ALU.mult)

                def stage_U(t):
                    idx, gw, ga = idx2[t % 2], gw2[t % 2], ga2[t % 2]
                    for col in range(128):
                        si = next_slot()
                        u_ = ring[si]
                        gat(u_, pu, idx, col, si)
                        K.v('dve', 'scalar_tensor_tensor', [u_, h32], [u_, aa], out=u_[:], in0=u_[:], scalar=1.0, in1=h32[:],
                            op0=ALU.mult, op1=ALU.mult, accum_out=aa[:, col:col + 1])
                    K.v('dve', 'tensor_tensor', [aa], [t1], out=t1[:], in0=aa[:], in1=aa[:], op=ALU.mult)
                    K.v('dve', 'tensor_scalar', [t1], [t1], out=t1[:], in0=t1[:], scalar1=0.044715, scalar2=1.0,
                        op0=ALU.mult, op1=ALU.add)
                    K.v('dve', 'tensor_tensor', [t1, aa], [t1], out=t1[:], in0=t1[:], in1=aa[:], op=ALU.mult)
                    K.act(t2[:], t1[:], AF.Tanh, [t1], [t2], scale=0.7978845608028654)
                    K.v('dve', 'tensor_scalar', [t2], [t2], out=t2[:], in0=t2[:], scalar1=1.0, scalar2=0.5, op0=ALU.add,
                        op1=ALU.mult)
                    K.v('dve', 'tensor_tensor', [t2, aa], [t2], out=t2[:], in0=t2[:], in1=aa[:], op=ALU.mult)
                    K.v('dve', 'tensor_tensor', [t2, gw], [ga], out=ga[:], in0=t2[:], in1=gw[:].rearrange("p h k -> p (h k)"),
                        op=ALU.mult)

                def stage_V(t):
                    idx, ga = idx2[t % 2], ga2[t % 2]
                    for col in range(128):
                        si = next_slot()
                        v_ = ring[si]
                        d_ = dg[col % 4]
                        gat(v_, pv, idx, col, si)
                        K.act(d_[:], self.ident_f[:], AF.Copy, [self.ident_f, ga], [d_], scale=ga[:, col:col + 1])
                        for j in range(2):
                            K.mm(pV[j][:], d_[:], v_[:, j * 512:(j + 1) * 512], col == 0, col == 127, [d_, v_], [pV[j]])

                def stage_F(t):
                    x = x2[t % 2]
                    t0 = t * 128
                    for j in range(2):
                        sl = slice(j * 512, (j + 1) * 512)
                        K.v('dve', 'tensor_tensor', [pV[j], G2], [acc], out=acc[:, sl], in0=pV[j][:], in1=G2[:, sl], op=ALU.mult)
                    K.v('dve', 'tensor_tensor', [acc, x], [x], out=x[:], in0=x[:], in1=acc[:], op=ALU.add)
                    if not last:
                        K.dma(S['xres_' + s][t0:t0 + 128, :], x[:], [x], [S['xres_' + s]])
                    else:
                        K.act(acc[:], x[:], AF.Square, [x], [acc, ss2], accum_out=ss2[:, 1:2])
                        self.rstd(ss2, 1, D)
                        K.v('dve', 'scalar_tensor_tensor', [x, ss2, fng], [acc], out=acc[:], in0=x[:], scalar=ss2[:, 1:2],
                            in1=fng[:], op0=ALU.mult, op1=ALU.mult)
                        K.dma(O['y_' + s][t0:t0 + 128, :], acc[:], [acc], [O['y_' + s]])

                nt = self.T[s] // 128
                stage_R(0)
                stage_R2(0)
                stage_U(0)
                for t in range(nt):
                    if t + 1 < nt:
                        stage_R(t + 1)
                    stage_V(t)
                    if t + 1 < nt:
                        stage_R2(t + 1)
                    stage_F(t)
                    if t + 1 < nt:
                        stage_U(t + 1)
            K.barrier()


Prog.phase_C = _phase_C
Prog.phase_C1 = _phase_C1
Prog.phase_C2 = _phase_C2


_CACHE = {}


def _build():
    if 'nc' not in _CACHE:
        nc = bass.Bass("TRN2", target_bir_lowering=False)
        P = Prog(nc, depth=DEPTH)
        P.build()
        _CACHE['nc'] = nc
        _CACHE['consts'] = make_consts()
    return _CACHE['nc'], _CACHE['consts']


def kernel(**inputs):
    inp = {k: np.asarray(v) for k, v in inputs.items()}
    nc, consts = _build()
    in_maps = [prep_core_inputs(inp, c, consts) for c in range(NCORES)]
    res = run_bass_kernel_spmd(nc, in_maps, core_ids=list(range(NCORES)))
    R = res.results
    cat = lambda name: np.concatenate([np.asarray(R[c][name]) for c in range(NCORES)], axis=0)
    B = NCORES * NPSEQ
    y_p = cat('y_p').reshape(B, PSEQ, D)
    y_s = np.stack([np.asarray(R[c]['y_s']) for c in range(NCORES)], axis=0)
    na_k = cat('o_nak').reshape(B, DEPTH, PSEQ, 4, 64)
    na_v = cat('o_nav').reshape(B, DEPTH, PSEQ, 4, 64)
    mC = cat('o_C').reshape(B, DEPTH, 2, 4, 64, 64)
    mn = cat('o_n').reshape(B, DEPTH, 2, 4, 64)
    mm = cat('o_m').reshape(B, DEPTH, 2, 4)
    sw_k = cat('o_swk').reshape(B, DEPTH, PSEQ, 2, 64)
    sw_v = cat('o_swv').reshape(B, DEPTH, PSEQ, 2, 64)
    ckv = cat('o_ckv').reshape(B, DEPTH, PSEQ, 128)
    kr = cat('o_kr').reshape(B, DEPTH, PSEQ, 32)
    outs = (y_p, y_s, na_k, na_v, mC, mn, mm, sw_k, sw_v, ckv, kr)
    return tuple(np.ascontiguousarray(o, dtype=np.float32) for o in outs)
```
